# Optimizing a Trainium2 kernel written in Bass

```python
import math
import jax, jax.numpy as jnp
from jax import lax
import numpy as np

D_MODEL = 1024
BATCH = 32
SEQ = 256
DEPTH = 1
DEC_BATCH = 2
DEC_SEQ = 4096
PAST_LEN = 512

GRID_W = 64
N_HEADS = 8
QK_NOPE = 64
ROPE_DIM = 32
QK_DIM = QK_NOPE + ROPE_DIM
V_DIM = 64
Q_LORA = 384
KV_LORA = 256
D_ATT = N_HEADS * V_DIM
D_HY = D_MODEL - D_ATT
D_MIX = D_ATT + D_HY
D_IN = Q_LORA + KV_LORA + ROPE_DIM + 3 * D_HY
HY_ORDER = 2
FILT_BANDS = 16
FILT_EMB = 1 + 2 * FILT_BANDS
FILT_HIDDEN = 64
DECAY_TARGET = 1e-2
FAST_DECAY_PCT = 0.3
SLOW_DECAY_PCT = 1.5
DECAY_SHIFT = 0.05
D_FF = 2816
ROPE_THETA = 10000.0
Q_BLOCK = 128
EPS = 1e-6

kernel_name = 'hybrid_mla_hyena_prefix_dit_step'


def _rms(x, g):
    xf = x.astype(jnp.float32)
    y = xf * lax.rsqrt(jnp.mean(xf * xf, axis=-1, keepdims=True) + EPS)
    return (y * g.astype(jnp.float32)).astype(x.dtype)


def _dwconv3(x, w, b):
    xp = jnp.pad(x, ((0, 0), (1, 1), (0, 0)))
    return xp[:, :-2] * w[0] + xp[:, 1:-1] * w[1] + xp[:, 2:] * w[2] + b


def _modulation(cvec, p):
    m = jax.nn.silu(cvec) @ p['w_ada'] + p['b_ada']
    return m.reshape(cvec.shape[0], 6, 1, D_MODEL)


def _axial_rope_tables(n):
    rows = n // GRID_W
    row = jnp.repeat(jnp.arange(rows), GRID_W).astype(jnp.float32)
    col = jnp.tile(jnp.arange(GRID_W), rows).astype(jnp.float32)
    quarter = ROPE_DIM // 4
    freqs = ROPE_THETA ** (-jnp.arange(quarter, dtype=jnp.float32) / quarter)
    ang_r = row[:, None] * freqs
    ang_c = col[:, None] * freqs
    ang = jnp.concatenate([ang_r, ang_r, ang_c, ang_c], axis=-1)
    return jnp.cos(ang)[:, None, :], jnp.sin(ang)[:, None, :]


def _rope(x, cos, sin):
    a, b, cc, d = jnp.split(x, 4, axis=-1)
    rot = jnp.concatenate([-b, a, -d, cc], axis=-1)
    return (x.astype(jnp.float32) * cos + rot.astype(jnp.float32) * sin).astype(x.dtype)


def _rope_heads(t, cos, sin):
    return jnp.concatenate([t[..., :QK_NOPE], _rope(t[..., QK_NOPE:], cos, sin)], axis=-1)


def _attention(q, k, v):
    b, lq, h, dk = q.shape
    scale = dk ** -0.5
    kf = k.astype(jnp.float32)
    vf = v.astype(jnp.float32)
    qb = jnp.moveaxis(q.reshape(b, lq // Q_BLOCK, Q_BLOCK, h, dk), 1, 0)

    def block(qblk):
        s = jnp.einsum('bqhd,bkhd->bhqk', qblk.astype(jnp.float32), kf) * scale
        return jnp.einsum('bhqk,bkhd->bqhd', jax.nn.softmax(s, axis=-1), vf)

    o = jnp.moveaxis(lax.map(block, qb), 0, 1).reshape(b, lq, h * v.shape[-1])
    return o.astype(v.dtype)


def _project(h, p):
    b, n, _ = h.shape
    proj = h @ p['w_in']
    q_c, kv_c, k_pe, hy = jnp.split(proj, [Q_LORA, Q_LORA + KV_LORA, Q_LORA + KV_LORA + ROPE_DIM], axis=-1)
    q = (_rms(q_c, p['g_qa']) @ p['w_uq']).reshape(b, n, N_HEADS, QK_DIM)
    q = _rms(q, p['g_q'])
    kv_c = _rms(kv_c, p['g_kva'])
    return q, kv_c, k_pe, hy


def _mla_keys_values(kv_c, k_pe, p):
    b, n, _ = kv_c.shape
    kv = (kv_c @ p['w_ukv']).reshape(b, n, N_HEADS, QK_NOPE + V_DIM)
    k_nope, v = jnp.split(kv, [QK_NOPE], axis=-1)
    k = jnp.concatenate([k_nope, jnp.broadcast_to(k_pe[:, :, None, :], (b, n, N_HEADS, ROPE_DIM))], axis=-1)
    return _rms(k, p['g_k']), v


def _hyena_filters(n, p):
    t = jnp.linspace(0.0, 1.0, n, dtype=jnp.float32)[:, None]
    w = 2.0 * math.pi * jnp.arange(n, dtype=jnp.float32)[:, None] / n
    bands = jnp.linspace(1e-4, FILT_BANDS - 1, FILT_BANDS, dtype=jnp.float32)
    z = jnp.concatenate([t, jnp.cos(bands * w), -jnp.sin(bands * w)], axis=-1)
    h = jnp.sin(p['filt_freq1'] * (z @ p['filt_w1'] + p['filt_b1']))
    h = jnp.sin(p['filt_freq2'] * (h @ p['filt_w2'] + p['filt_b2']))
    h = (h @ p['filt_w3']).astype(jnp.float32).reshape(n, HY_ORDER, 2, D_HY)
    max_decay = math.log(DECAY_TARGET) / FAST_DECAY_PCT
    min_decay = math.log(DECAY_TARGET) / SLOW_DECAY_PCT
    deltas = jnp.abs(jnp.linspace(min_decay, max_decay, D_HY, dtype=jnp.float32))
    window = jnp.exp(-t * deltas) + DECAY_SHIFT
    return h * window[:, None, None, :]


def _long_conv_bidir(u, h_fwd, h_bwd, skip):
    n = u.shape[1]
    kern = jnp.concatenate([h_fwd, jnp.zeros((1, D_HY), jnp.float32), jnp.flip(h_bwd[1:], axis=0)], axis=0)
    uf = u.astype(jnp.float32)
    y = jnp.fft.irfft(jnp.fft.rfft(uf, n=2 * n, axis=1) * jnp.fft.rfft(kern, axis=0)[None], n=2 * n, axis=1)[:, :n]
    return (y + uf * skip.astype(jnp.float32)).astype(u.dtype)


def _hyena(u, p):
    n = u.shape[1]
    u = _dwconv3(u, p['hy_conv_w'], p['hy_conv_b'])
    v, x1, x2 = jnp.split(u, 3, axis=-1)
    h = _hyena_filters(n, p)
    z = x1 * _long_conv_bidir(v, h[:, 0, 0], h[:, 0, 1], p['filt_bias'][0])
    return x2 * _long_conv_bidir(z, h[:, 1, 0], h[:, 1, 1], p['filt_bias'][1])


def _merge(attn, hy, p):
    mix = jnp.concatenate([_rms(attn, p['g_out_att']), _rms(hy, p['g_out_hy'])], axis=-1)
    return mix @ p['w_out']


def _ffn(h, p):
    u = _dwconv3(h @ p['w_up'], p['ffn_conv_w'], p['ffn_conv_b'])
    a, g = jnp.split(u, 2, axis=-1)
    return (jax.nn.silu(a) * g) @ p['w_down']


def _context_layer(x, c_ctx, p):
    m = _modulation(c_ctx[None], p)
    h = _rms(x, p['g_mix']) * (1 + m[:, 1]) + m[:, 0]
    q, kv_c, k_pe, hy = _project(h, p)
    k, v = _mla_keys_values(kv_c, k_pe, p)
    x = x + m[:, 2] * _merge(_attention(q, k, v), _hyena(hy, p), p)
    h = _rms(x, p['g_ffn']) * (1 + m[:, 4]) + m[:, 3]
    x = x + m[:, 5] * _ffn(h, p)
    return x, kv_c, k_pe


def _latent_layer(x, c, ckv_ctx, kpe_ctx, p):
    n = x.shape[1]
    m = _modulation(c, p)
    h = _rms(x, p['g_mix']) * (1 + m[:, 1]) + m[:, 0]
    q, kv_c, k_pe, hy = _project(h, p)
    cos, sin = _axial_rope_tables(n)
    q = _rope_heads(q, cos, sin)
    k_lat, v_lat = _mla_keys_values(kv_c, k_pe, p)
    k_lat = _rope_heads(k_lat, cos, sin)
    k_ctx, v_ctx = _mla_keys_values(ckv_ctx, kpe_ctx, p)
    attn = _attention(q, jnp.concatenate([k_lat, k_ctx], axis=1), jnp.concatenate([v_lat, v_ctx], axis=1))
    x = x + m[:, 2] * _merge(attn, _hyena(hy, p), p)
    h = _rms(x, p['g_ffn']) * (1 + m[:, 4]) + m[:, 3]
    return x + m[:, 5] * _ffn(h, p)


def setup_inputs(seed: int = 0) -> dict:
    key = jax.random.key(seed)
    ks = iter(jax.random.split(key, 40))

    def nrm(shape, scale):
        return jax.random.normal(next(ks), shape, jnp.float32) * scale

    def gain(shape):
        return 1.0 + nrm(shape, 0.02)

    L = DEPTH
    return {
        'x_prompt': nrm((BATCH, SEQ, D_MODEL), 1.0),
        'x_sample': nrm((DEC_BATCH, DEC_SEQ, D_MODEL), 1.0),
        'cache_ckv': nrm((DEC_BATCH, DEPTH, PAST_LEN, KV_LORA), 1.0),
        'cache_kpe': nrm((DEC_BATCH, DEPTH, PAST_LEN, ROPE_DIM), 1.0),
        'c': nrm((DEC_BATCH, D_MODEL), 1.0),
        'c_ctx': nrm((D_MODEL,), 1.0),
        'w_ada': nrm((L, D_MODEL, 6 * D_MODEL), 0.5 * D_MODEL ** -0.5),
        'b_ada': nrm((L, 6 * D_MODEL), 0.01),
        'g_mix': gain((L, D_MODEL)),
        'w_in': nrm((L, D_MODEL, D_IN), D_MODEL ** -0.5),
        'g_qa': gain((L, Q_LORA)),
        'w_uq': nrm((L, Q_LORA, N_HEADS * QK_DIM), Q_LORA ** -0.5),
        'g_kva': gain((L, KV_LORA)),
        'w_ukv': nrm((L, KV_LORA, N_HEADS * (QK_NOPE + V_DIM)), KV_LORA ** -0.5),
        'g_q': gain((L, QK_DIM)),
        'g_k': gain((L, QK_DIM)),
        'hy_conv_w': nrm((L, 3, 3 * D_HY), 0.5),
        'hy_conv_b': nrm((L, 3 * D_HY), 0.01),
        'filt_w1': nrm((L, FILT_EMB, FILT_HIDDEN), FILT_EMB ** -0.5),
        'filt_b1': nrm((L, FILT_HIDDEN), 0.1),
        'filt_freq1': gain((L, FILT_HIDDEN)),
        'filt_w2': nrm((L, FILT_HIDDEN, FILT_HIDDEN), FILT_HIDDEN ** -0.5),
        'filt_b2': nrm((L, FILT_HIDDEN), 0.1),
        'filt_freq2': gain((L, FILT_HIDDEN)),
        'filt_w3': nrm((L, FILT_HIDDEN, HY_ORDER * 2 * D_HY), 0.05 * FILT_HIDDEN ** -0.5),
        'filt_bias': nrm((L, HY_ORDER, D_HY), 1.0),
        'g_out_att': gain((L, D_ATT)),
        'g_out_hy': gain((L, D_HY)),
        'w_out': nrm((L, D_MIX, D_MODEL), D_MIX ** -0.5),
        'g_ffn': gain((L, D_MODEL)),
        'w_up': nrm((L, D_MODEL, 2 * D_FF), D_MODEL ** -0.5),
        'ffn_conv_w': nrm((L, 3, 2 * D_FF), 0.5),
        'ffn_conv_b': nrm((L, 2 * D_FF), 0.01),
        'w_down': nrm((L, D_FF, D_MODEL), D_FF ** -0.5),
    }


def reference(x_prompt, x_sample, cache_ckv, cache_kpe, c, c_ctx, w_ada, b_ada, g_mix, w_in, g_qa, w_uq,
              g_kva, w_ukv, g_q, g_k, hy_conv_w, hy_conv_b, filt_w1, filt_b1, filt_freq1, filt_w2, filt_b2,
              filt_freq2, filt_w3, filt_bias, g_out_att, g_out_hy, w_out, g_ffn, w_up, ffn_conv_w, ffn_conv_b,
              w_down):
    y_prompt = x_prompt
    y_sample = x_sample
    ckv_list = []
    kpe_list = []
    for l in range(DEPTH):
        p = {
            'w_ada': w_ada[l], 'b_ada': b_ada[l], 'g_mix': g_mix[l], 'w_in': w_in[l],
            'g_qa': g_qa[l], 'w_uq': w_uq[l], 'g_kva': g_kva[l], 'w_ukv': w_ukv[l],
            'g_q': g_q[l], 'g_k': g_k[l], 'hy_conv_w': hy_conv_w[l], 'hy_conv_b': hy_conv_b[l],
            'filt_w1': filt_w1[l], 'filt_b1': filt_b1[l], 'filt_freq1': filt_freq1[l],
            'filt_w2': filt_w2[l], 'filt_b2': filt_b2[l], 'filt_freq2': filt_freq2[l],
            'filt_w3': filt_w3[l], 'filt_bias': filt_bias[l], 'g_out_att': g_out_att[l],
            'g_out_hy': g_out_hy[l], 'w_out': w_out[l], 'g_ffn': g_ffn[l], 'w_up': w_up[l],
            'ffn_conv_w': ffn_conv_w[l], 'ffn_conv_b': ffn_conv_b[l], 'w_down': w_down[l],
        }
        y_prompt, ckv_l, kpe_l = _context_layer(y_prompt, c_ctx, p)
        ckv_list.append(ckv_l)
        kpe_list.append(kpe_l)
        y_sample = _latent_layer(y_sample, c, cache_ckv[:, l], cache_kpe[:, l], p)
    new_ckv = jnp.stack(ckv_list, axis=1)
    new_kpe = jnp.stack(kpe_list, axis=1)
    return (y_prompt, y_sample, new_ckv, new_kpe)
```

```python
import math
from contextlib import ExitStack

import numpy as np
import ml_dtypes

import concourse.bass as bass
import concourse.mybir as mybir
from concourse.bass_utils import run_bass_kernel_spmd

F32 = mybir.dt.float32
BF16 = mybir.dt.bfloat16
AF = mybir.ActivationFunctionType
ALU = mybir.AluOpType
AX = mybir.AxisListType
NPBF = ml_dtypes.bfloat16

D = 1024
KC = 8
EPS = 1e-6
N_HEADS = 8
QK_NOPE = 64
ROPE = 32
QK = 96
VD = 64
QL = 384
KVL = 256
DHY = 512
DIN = 2208
DFF = 2816
NFF = 44
TP = 1024
TL = 4096
PAST = 512


class Dep:
    __slots__ = ("name", "w", "r", "excl")

    def __init__(self, name="", excl=False):
        self.name = name
        self.w = None
        self.r = {}
        self.excl = excl


class Prog:
    ENG = ("pe", "act", "dve", "pool", "sp")

    def __init__(self, nc, ndma_sems=8):
        self.nc = nc
        self.q = {e: [] for e in self.ENG}
        self.cnt = {e: 0 for e in self.ENG}
        self.known = {e: {} for e in self.ENG}
        self.sems = {}
        self.dma_ring = {}
        self.ndma_sems = ndma_sems

    def setup(self, stack):
        nc = self.nc
        for e in self.ENG:
            self.sems[e] = stack.enter_context(nc.semaphore("s_" + e))
        for e in ("sp", "act", "pool"):
            ring = [stack.enter_context(nc.semaphore("d_%s%d" % (e, i))) for i in range(self.ndma_sems)]
            self.dma_ring[e] = {"sems": ring, "vals": [0] * len(ring), "next": 0}

    def _need(self, eng, ev):
        sem, val, src = ev
        if src == eng and eng == "pe":
            return
        k = self.known[eng]
        if k.get(id(sem), 0) >= val:
            return
        k[id(sem)] = val
        self.q[eng].append(("wait", sem, val))

    def _deps(self, eng, reads, writes):
        for d in reads:
            if d.w is not None:
                self._need(eng, d.w)
        for d in writes:
            if d.w is not None:
                self._need(eng, d.w)
            for ev in d.r.values():
                self._need(eng, ev)

    def _commit(self, ev, reads, writes):
        for d in reads:
            old = d.r.get(id(ev[0]))
            if old is None or old[1] < ev[1]:
                d.r[id(ev[0])] = ev
        for d in writes:
            d.w = ev
            d.r = {}

    def op(self, eng, fn, reads=(), writes=()):
        for d in reads:
            if d.excl:
                for ev in d.r.values():
                    if ev[2] != eng:
                        self._need(eng, ev)
        self._deps(eng, reads, writes)
        self.cnt[eng] += 1
        ev = (self.sems[eng], self.cnt[eng], eng)
        self.q[eng].append(("op", fn, self.sems[eng], 1))
        self._commit(ev, reads, writes)
        return ev

    def dma(self, eng, fn, reads=(), writes=()):
        ring = self.dma_ring[eng]
        i = ring["next"]
        ring["next"] = (i + 1) % len(ring["sems"])
        sem = ring["sems"][i]
        if ring["vals"][i] > 0:
            self._need(eng, (sem, ring["vals"][i], "dma"))
        self._deps(eng, reads, writes)
        ring["vals"][i] += 16
        ev = (sem, ring["vals"][i], "dma")
        self.q[eng].append(("op", fn, sem, 16))
        self._commit(ev, reads, writes)
        return ev

    def barrier(self):
        for e in self.ENG:
            for e2 in self.ENG:
                if e2 != e and self.cnt[e2] > 0:
                    self._need(e, (self.sems[e2], self.cnt[e2], e2))
            for q in ("sp", "act", "pool"):
                ring = self.dma_ring[q]
                for sem, v in zip(ring["sems"], ring["vals"]):
                    if v > 0:
                        self._need(e, (sem, v, "dma"))

    def emit(self):
        nc = self.nc
        for e in ("sp", "act", "pool"):
            ring = self.dma_ring[e]
            for sem, v in zip(ring["sems"], ring["vals"]):
                if v > 0:
                    self._need("sp", (sem, v, "dma"))
        for e in ("pe", "act", "dve", "pool"):
            if self.cnt[e] > 0:
                self._need("sp", (self.sems[e], self.cnt[e], e))
        handles = {"pe": "tensor", "act": "scalar", "dve": "vector", "pool": "gpsimd", "sp": "sync"}
        with nc.Block() as block:
            for e in self.ENG:
                items = self.q[e]

                def body(engh, items=items):
                    for it in items:
                        if it[0] == "wait":
                            engh.wait_ge(it[1], it[2])
                        else:
                            it[1](engh).then_inc(it[2], it[3])
                getattr(block, handles[e])(body)


class Buf:
    def __init__(self, ap, name=""):
        self.ap = ap
        self.dep = Dep(name)

    def __getitem__(self, key):
        return self.ap[key]


class Arena:
    def __init__(self, nc, stack, name, nbytes):
        self.t = stack.enter_context(nc.sbuf_tensor(name, [128, nbytes // 4], F32))
        self.size = nbytes
        self.off = 0
        self.peak = 0
        self.hist = []

    def alloc(self, free_shape, dtype, name=""):
        n = 1
        for s in free_shape:
            n *= s
        esz = 4 if dtype == F32 else 2
        nb = (n * esz + 31) // 32 * 32
        assert self.off + nb <= self.size, "arena overflow %s %d+%d>%d" % (name, self.off, nb, self.size)
        a = self.t[:, self.off // 4:(self.off + nb) // 4]
        if dtype != F32:
            a = a.bitcast(dtype)
        a = a[:, 0:n]
        if len(free_shape) == 2:
            a = a.rearrange("p (a b) -> p a b", b=free_shape[1])
        elif len(free_shape) == 3:
            a = a.rearrange("p (a b c) -> p a b c", b=free_shape[1], c=free_shape[2])
        elif len(free_shape) == 4:
            a = a.rearrange("p (a b c d) -> p a b c d", b=free_shape[1], c=free_shape[2], d=free_shape[3])
        nbuf = Buf(a, name)
        lo, hi = self.off, self.off + nb
        for (o, e_, b) in self.hist:
            if o < hi and lo < e_:
                evs = list(b.dep.r.values())
                if b.dep.w is not None:
                    evs.append(b.dep.w)
                for ev in evs:
                    old = nbuf.dep.r.get(id(ev[0]))
                    if old is None or old[1] < ev[1]:
                        nbuf.dep.r[id(ev[0])] = ev
        self.hist.append((lo, hi, nbuf))
        self.off += nb
        self.peak = max(self.peak, self.off)
        return nbuf

    def mark(self):
        return self.off

    def release(self, m):
        self.off = m


def _rope_tables(ntok_positions):
    pos = np.asarray(ntok_positions)
    row = (pos // 64).astype(np.float32)
    col = (pos % 64).astype(np.float32)
    quarter = ROPE // 4
    freqs = (10000.0 ** (-np.arange(quarter, dtype=np.float32) / quarter)).astype(np.float32)
    ang_r = row[:, None] * freqs
    ang_c = col[:, None] * freqs
    ang = np.concatenate([ang_r, ang_r, ang_c, ang_c], axis=-1).astype(np.float32)
    return np.cos(ang).T.astype(np.float32), np.sin(ang).T.astype(np.float32)


def _rot32():
    R = np.zeros((32, 32), np.float32)
    for i in range(8):
        R[i, 8 + i] = -1.0
        R[8 + i, i] = 1.0
        R[16 + i, 24 + i] = -1.0
        R[24 + i, 16 + i] = 1.0
    return R


def _fa_table(pos, cg, na):
    ns = len(pos)
    at = 128 // cg
    nh = na * cg // 128
    T = np.zeros((ns * cg, nh, 3, 128), np.float64)
    a_t = np.arange(at)
    for s_, p in enumerate(pos):
        for h in range(nh):
            phi = 2 * np.pi * p * (h * at + a_t) / float(na)
            for c in range(cg):
                T[c * ns + s_, h, 0, a_t * cg + c] = np.cos(phi)
                T[c * ns + s_, h, 1, a_t * cg + c] = -np.sin(phi)
                T[c * ns + s_, h, 2, a_t * cg + c] = np.sin(phi)
    return T.astype(NPBF)


def _ga_table(posout, cg, na):
    npos = len(posout)
    at = 128 // cg
    nh = na * cg // 128
    G = np.zeros((128, nh, 2, 2 * npos * cg), np.float64)
    for h in range(nh):
        for a_t in range(at):
            alpha = h * at + a_t
            for n, p in enumerate(posout):
                psi = 2 * np.pi * p * alpha / float(na)
                cr, ci = np.cos(psi), np.sin(psi)
                for c in range(cg):
                    row = a_t * cg + c
                    cre = n * cg + c
                    cim = npos * cg + n * cg + c
                    G[row, h, 0, cre] = cr
                    G[row, h, 0, cim] = ci
                    G[row, h, 1, cre] = -ci
                    G[row, h, 1, cim] = cr
    return G.astype(NPBF)


def _filter_feats(n):
    t = np.linspace(0.0, 1.0, n, dtype=np.float32)[:, None]
    w = (2.0 * math.pi * np.arange(n, dtype=np.float32)[:, None] / n).astype(np.float32)
    bands = np.linspace(1e-4, 15.0, 16, dtype=np.float32)
    z = np.concatenate([t, np.cos(bands * w), -np.sin(bands * w)], axis=-1).astype(np.float32)
    max_decay = math.log(1e-2) / 0.3
    min_decay = math.log(1e-2) / 1.5
    deltas = np.abs(np.linspace(min_decay, max_decay, 512, dtype=np.float32))
    window = (np.exp(-t * deltas) + 0.05).astype(np.float32)
    return np.ascontiguousarray(z.T), window.astype(NPBF)


def _conv_tables(j):
    t = {}
    b = np.arange(128, dtype=np.float64)[:, None]
    be = np.arange(128, dtype=np.float64)[None, :]
    th = 2 * np.pi * b * (be + 0.5) / 256.0
    t["Fb"] = np.concatenate([np.cos(th), -np.sin(th), -np.sin(th), -np.cos(th)], axis=1).astype(NPBF)
    beK = np.arange(128, dtype=np.float64)[:, None]
    bp = np.arange(128, dtype=np.float64)[None, :]
    om_lo = 2 * np.pi * bp * (beK + 0.5) / 256.0
    om_hi = 2 * np.pi * (bp + 128) * (beK + 0.5) / 256.0
    for nm_, s in (("CB_p", 2.0 ** -11), ("CB_l", 2.0 ** -13)):
        t[nm_] = np.stack([s * np.cos(om_lo), -s * np.sin(om_lo), s * np.cos(om_hi), -s * np.sin(om_hi)], axis=1).astype(NPBF)
    r1 = np.zeros((128, 128), np.float32)
    for i in range(1, 128):
        r1[i, 128 - i] = 1.0
    e0 = np.zeros((128, 128), np.float32)
    e0[0, 0] = 1.0
    t["R1T"] = r1.astype(NPBF)
    t["E0T"] = e0.astype(NPBF)
    pos_p = [4 * (s_ // 2) + (s_ % 2) for s_ in range(8)]
    t["FAd_p"] = _fa_table(pos_p, 16, 16)
    t["FAk_p"] = _fa_table([0, 1, 15, 14, 0, 0, 0, 0], 16, 16)[None]
    po = pos_p + [(p - 1) % 16 for p in pos_p]
    t["GA1_p"] = _ga_table(po, 16, 16)
    t["GA2_p"] = _ga_table(po, 16, 16)
    pos_l = [(s_ + 8 * j) % 32 for s_ in range(32)]
    t["FAd_l"] = _fa_table(pos_l, 4, 64)
    t["FAk_l"] = np.stack([_fa_table(list(range(32)), 4, 64), _fa_table([64 - i for i in range(1, 33)], 4, 64)], axis=0)
    t["GA1_l"] = _ga_table(pos_l + [(p - 1) % 64 for p in pos_l], 4, 64)
    out2 = list(range(8)) + [31, 8]
    p2 = [pos_l[s_] for s_ in out2]
    t["GA2_l"] = _ga_table(p2 + [(p - 1) % 64 for p in p2], 4, 64)
    t["zT_p"], t["win_p"] = _filter_feats(256)
    t["zT_l"], t["win_l"] = _filter_feats(4096)
    seam = (4096 - 1024 * j) % 4096
    co = np.zeros(4, np.float32)
    for si in range(4):
        flag = 1.0 if si * 1024 == seam else 0.0
        co[si] = (1.0 - flag) if si == 0 else -flag
    t["seamc"] = np.tile(co[None, :], (128, 1)).astype(np.float32)
    hm = np.array([0.0 if j == 0 else 1.0, 0.0 if j == 3 else 1.0], np.float32)
    t["halomask"] = np.tile(hm[None, :], (128, 1)).astype(np.float32)
    return t


def host_tables(core):
    j = core % 4
    t = {}
    t["ident_f"] = np.eye(128, dtype=np.float32)
    t["ident_b"] = np.eye(128, dtype=np.float32).astype(NPBF)
    R = _rot32()
    sh = np.zeros((32, 96), np.float32)
    shr = np.zeros((32, 96), np.float32)
    r96 = np.zeros((96, 96), np.float32)
    for k in range(32):
        sh[k, 64 + k] = 1.0
        for m in range(32):
            shr[k, 64 + m] = R[m, k]
            r96[64 + k, 64 + m] = R[m, k]
    t["shiftT"] = sh.astype(NPBF)
    t["shiftrotT"] = shr.astype(NPBF)
    t["rot96T"] = r96.astype(NPBF)
    st_tok = (np.arange(TL) + 1024 * j) % TL
    ck, sk = _rope_tables(st_tok)
    t["cosk"] = np.concatenate([ck, np.ones((32, PAST), np.float32)], axis=1)
    t["sink"] = np.concatenate([sk, np.zeros((32, PAST), np.float32)], axis=1)
    qtok = np.concatenate([1024 * j + np.arange(1024), [(1024 * j - 1) % TL, (1024 * j + 1024) % TL]])
    cq, sq = _rope_tables(qtok)
    t["cosq"] = np.concatenate([np.ones((64, 1026), np.float32), cq], axis=0)
    t["sinq"] = np.concatenate([np.zeros((64, 1026), np.float32), sq], axis=0)
    t.update(_conv_tables(j))
    return t


VEC_LAYOUT = [("b_ada", 48), ("g_mix", 8), ("g_ffn", 8), ("g_qa", 3), ("hy_conv_w", 36), ("hy_conv_b", 12),
              ("ffn_conv_w", 132), ("ffn_conv_b", 44)]


def build_program(debug=(), GROUPS=(0, 1)):
    nc = bass.Bass("TRN2", target_bir_lowering=False)
    P = Prog(nc)

    def din(name, shape, dt=F32):
        return nc.dram_tensor(name, list(shape), dt, kind="ExternalInput").ap()

    def dout(name, shape, dt=F32):
        return nc.dram_tensor(name, list(shape), dt, kind="ExternalOutput").ap()

    I = {}
    I["xp"] = din("xp", [TP, D])
    I["xs"] = din("xs", [TL, D])
    I["ckv_c"] = din("ckv_c", [PAST, KVL])
    I["kpe_c"] = din("kpe_c", [PAST, ROPE])
    I["cvec"] = din("cvec", [D, 2])
    for nm, shp in [("w_ada", [D, 6 * D]), ("b_ada", [6 * D]), ("g_mix", [D]), ("w_in", [D, DIN]), ("g_qa", [QL]),
                    ("w_uq", [QL, N_HEADS * QK]), ("g_kva", [KVL]), ("w_ukv", [KVL, 1024]), ("g_q", [QK]),
                    ("g_k", [QK]), ("hy_conv_w", [3 * 1536]), ("hy_conv_b", [1536]), ("filt_w1", [33, 64]),
                    ("filt_b1", [64]), ("filt_freq1", [64]), ("filt_w2", [64, 64]), ("filt_b2", [64]),
                    ("filt_freq2", [64]), ("filt_w3", [64, 2048]), ("filt_bias", [2, 512]), ("g_out_att", [512]),
                    ("g_out_hy", [512]), ("w_out", [D, D]), ("g_ffn", [D]), ("w_up", [D, 2 * DFF]),
                    ("ffn_conv_w", [3 * 2 * DFF]), ("ffn_conv_b", [2 * DFF]), ("w_down", [DFF, D])]:
        I[nm] = din(nm, shp)
    I["shiftT"] = din("shiftT", [32, 96], BF16)
    I["shiftrotT"] = din("shiftrotT", [32, 96], BF16)
    I["rot96T"] = din("rot96T", [96, 96], BF16)
    I["cosk"] = din("cosk", [32, TL + PAST])
    I["sink"] = din("sink", [32, TL + PAST])
    I["cosq"] = din("cosq", [96, 1026])
    I["sinq"] = din("sinq", [96, 1026])
    I["Fb"] = din("Fb", [128, 512], BF16)
    I["CB_p"] = din("CB_p", [128, 4, 128], BF16)
    I["CB_l"] = din("CB_l", [128, 4, 128], BF16)
    I["R1T"] = din("R1T", [128, 128], BF16)
    I["E0T"] = din("E0T", [128, 128], BF16)
    I["FAd_p"] = din("FAd_p", [128, 2, 3, 128], BF16)
    I["FAd_l"] = din("FAd_l", [128, 2, 3, 128], BF16)
    I["FAk_p"] = din("FAk_p", [1, 128, 2, 3, 128], BF16)
    I["FAk_l"] = din("FAk_l", [2, 128, 2, 3, 128], BF16)
    I["GA1_p"] = din("GA1_p", [128, 2, 2, 512], BF16)
    I["GA2_p"] = din("GA2_p", [128, 2, 2, 512], BF16)
    I["GA1_l"] = din("GA1_l", [128, 2, 2, 512], BF16)
    I["GA2_l"] = din("GA2_l", [128, 2, 2, 160], BF16)
    I["zT_p"] = din("zT_p", [33, 256])
    I["zT_l"] = din("zT_l", [33, 4096])
    I["win_p"] = din("win_p", [256, 512], BF16)
    I["win_l"] = din("win_l", [4096, 512], BF16)
    I["seamc"] = din("seamc", [128, 4])
    I["halomask"] = din("halomask", [128, 2])
    I["ident_f"] = din("ident_f", [128, 128])
    I["ident_b"] = din("ident_b", [128, 128], BF16)

    O = {}
    O["y_prompt"] = dout("y_prompt", [TP, D])
    O["y_sample"] = dout("y_sample", [1024, D])
    O["new_ckv"] = dout("new_ckv", [TP, KVL])
    O["new_kpe"] = dout("new_kpe", [TP, ROPE])
    DBG = {}

    def dbg_out(name, shape):
        if name in debug:
            DBG[name] = dout("dbg_" + name, shape)
            return DBG[name]
        return None

    with ExitStack() as st:
        P.setup(st)
        AR = Arena(nc, st, "arena", 207 * 1024)
        ps = []
        for i in range(8):
            t_ = st.enter_context(nc.psum_tensor("psb%d" % i, [128, 512], F32))
            ps.append(Buf(t_[:], "ps%d" % i))
            ps[-1].dep.excl = True

        def psb(i):
            return ps[i].ap.bitcast(BF16)

        ident_f = AR.alloc([128], F32, "ident_f")
        ident_b = AR.alloc([128], BF16, "ident_b")
        P.dma("sp", lambda e: e.dma_start(out=ident_f.ap, in_=I["ident_f"]), writes=[ident_f.dep])
        P.dma("sp", lambda e: e.dma_start(out=ident_b.ap, in_=I["ident_b"]), writes=[ident_b.dep])
        epsb = AR.alloc([1], F32, "epsb")
        P.op("pool", lambda e: e.memset(epsb.ap, EPS), writes=[epsb.dep])
        ones_b = AR.alloc([128], BF16, "ones_b")
        P.op("pool", lambda e: e.memset(ones_b.ap, 1.0), writes=[ones_b.dep])
        ones_f = AR.alloc([128], F32, "ones_f")
        P.op("pool", lambda e: e.memset(ones_f.ap, 1.0), writes=[ones_f.dep])

        nrows = sum(n for _, n in VEC_LAYOUT)
        nblk = (nrows + 127) // 128
        vec = AR.alloc([nblk * 128], F32, "vec")
        voff = {}
        m0 = AR.mark()
        rows = AR.alloc([nblk, 128], F32, "vecrows")
        P.op("pool", lambda e: e.memset(rows.ap, 0.0), writes=[rows.dep])
        r = 0
        for nm, n in VEC_LAYOUT:
            voff[nm] = r
            src = I[nm].rearrange("(r p) -> r p", p=128)
            done = 0
            while done < n:
                blk, p0 = (r + done) // 128, (r + done) % 128
                cnt = min(n - done, 128 - p0)
                P.dma("sp", lambda e, blk=blk, p0=p0, cnt=cnt, src=src, done=done:
                      e.dma_start(out=rows.ap[p0:p0 + cnt, blk, :], in_=src[done:done + cnt, :]),
                      writes=[rows.dep])
                done += cnt
            r += n
        for blk in range(nblk):
            P.op("pe", lambda e, blk=blk: e.transpose(ps[7].ap[:, blk * 128:(blk + 1) * 128], rows.ap[:, blk, :], ident_f.ap),
                 reads=[rows.dep, ident_f.dep], writes=[ps[7].dep])
        P.op("dve", lambda e: e.tensor_copy(out=vec.ap, in_=ps[7].ap[:, 0:nblk * 128]), reads=[ps[7].dep], writes=[vec.dep])
        AR.release(m0)

        def vcol(nm, i):
            return vec.ap[:, voff[nm] + i: voff[nm] + i + 1]

        mod = AR.alloc([2, 48], F32, "mod")
        m0 = AR.mark()
        cv = AR.alloc([KC, 2], F32, "cv")
        P.dma("sp", lambda e: e.dma_start(out=cv.ap, in_=I["cvec"].rearrange("(k p) v -> p k v", p=128)), writes=[cv.dep])
        sc = AR.alloc([KC, 2], F32, "sc")
        P.op("act", lambda e: e.activation(out=sc.ap, in_=cv.ap, func=AF.Silu), reads=[cv.dep], writes=[sc.dep])
        wa = [AR.alloc([6 * D], F32, "wa%d" % i) for i in range(2)]
        wab = [AR.alloc([6 * D], BF16, "wab%d" % i) for i in range(2)]
        scb = AR.alloc([KC, 2], BF16, "scb")
        P.op("dve", lambda e: e.tensor_copy(out=scb.ap, in_=sc.ap), reads=[sc.dep], writes=[scb.dep])
        for k in range(KC):
            wb = wa[k % 2]
            wbb = wab[k % 2]
            P.dma("sp", lambda e, wb=wb, k=k: e.dma_start(out=wb.ap, in_=I["w_ada"][k * 128:(k + 1) * 128, :]), writes=[wb.dep])
            P.op("dve", lambda e, wb=wb, wbb=wbb: e.tensor_copy(out=wbb.ap[:, 0:2048], in_=wb.ap[:, 0:2048]), reads=[wb.dep], writes=[wbb.dep])
            P.op("act", lambda e, wb=wb, wbb=wbb: e.activation(out=wbb.ap[:, 2048:4096], in_=wb.ap[:, 2048:4096], func=AF.Copy),
                 reads=[wb.dep], writes=[wbb.dep])
            P.op("pool", lambda e, wb=wb, wbb=wbb: e.tensor_copy(out=wbb.ap[:, 4096:6144], in_=wb.ap[:, 4096:6144]), reads=[wb.dep], writes=[wbb.dep])
            for mc in range(48):
                P.op("pe", lambda e, wbb=wbb, mc=mc, k=k: e.matmul(
                    ps[6].ap[:, 2 * mc:2 * mc + 2], lhsT=wbb.ap[:, mc * 128:(mc + 1) * 128], rhs=scb.ap[:, k, :],
                    start=(k == 0 and mc == 0), stop=(k == KC - 1), skip_group_check=True), reads=[wbb.dep, scb.dep], writes=[ps[6].dep])
        P.op("dve", lambda e: e.tensor_tensor(
            out=mod.ap.rearrange("p v m -> p m v"), in0=ps[6].ap[:, 0:96].rearrange("p (m v) -> p m v", v=2),
            in1=vec.ap[:, voff["b_ada"]:voff["b_ada"] + 48].unsqueeze(2).to_broadcast([128, 48, 2]), op=ALU.add),
            reads=[ps[6].dep, vec.dep], writes=[mod.dep])
        AR.release(m0)
        A1 = AR.alloc([KC, 2], F32, "A1")
        A2 = AR.alloc([KC, 2], F32, "A2")
        for v in range(2):
            P.op("dve", lambda e, v=v: e.scalar_tensor_tensor(
                out=A1.ap[:, :, v], in0=mod.ap[:, v, 8:16], scalar=1.0, in1=vec.ap[:, voff["g_mix"]:voff["g_mix"] + 8],
                op0=ALU.add, op1=ALU.mult), reads=[mod.dep, vec.dep], writes=[A1.dep])
            P.op("dve", lambda e, v=v: e.scalar_tensor_tensor(
                out=A2.ap[:, :, v], in0=mod.ap[:, v, 32:40], scalar=1.0, in1=vec.ap[:, voff["g_ffn"]:voff["g_ffn"] + 8],
                op0=ALU.add, op1=ALU.mult), reads=[mod.dep, vec.dep], writes=[A2.dep])

        d2_ = dbg_out("vec", [128, nblk * 128])
        if d2_ is not None:
            P.dma("sp", lambda e: e.dma_start(out=d2_, in_=vec.ap), reads=[vec.dep])
        d_ = dbg_out("mod", [128, 96])
        if d_ is not None:
            P.dma("sp", lambda e: e.dma_start(out=d_, in_=mod.ap.rearrange("p v m -> p (v m)")), reads=[mod.dep])


        modD = nc.dram_tensor("modD", [96, 128], F32).ap()
        modD_dep = Dep("modD")
        m0 = AR.mark()
        modTt = AR.alloc([128], F32, "modTt")
        P.op("pe", lambda e: e.transpose(ps[7].ap[0:96, 0:128], mod.ap.rearrange("p v m -> p (v m)"), ident_f.ap),
             reads=[mod.dep, ident_f.dep], writes=[ps[7].dep])
        P.op("dve", lambda e: e.tensor_copy(out=modTt.ap[0:96, :], in_=ps[7].ap[0:96, 0:128]), reads=[ps[7].dep], writes=[modTt.dep])
        P.dma("sp", lambda e: e.dma_start(out=modD, in_=modTt.ap[0:96, :]), reads=[modTt.dep], writes=[modD_dep])
        AR.release(m0)
        modD_flat = modD.rearrange("r p -> (r p)")
        sel = AR.alloc([4], F32, "sel")
        P.op("pool", lambda e: e.memset(sel.ap, 0.0), writes=[sel.dep])
        P.op("pool", lambda e: e.tensor_copy(out=sel.ap[:, 0:1], in_=ident_f.ap[:, 127:128]), reads=[ident_f.dep], writes=[sel.dep])
        P.op("pool", lambda e: e.tensor_copy(out=sel.ap[:, 3:4], in_=ident_f.ap[:, 0:1]), reads=[ident_f.dep], writes=[sel.dep])
        halomask = AR.alloc([2], F32, "halomask")
        P.dma("sp", lambda e: e.dma_start(out=halomask.ap, in_=I["halomask"]), writes=[halomask.dep])
        x1scr = [nc.dram_tensor("x1scr0", [TP, D], F32).ap(), nc.dram_tensor("x1scr1", [1026, D], F32).ap()]
        x1scr_dep = [Dep("x1scr0"), Dep("x1scr1")]
        gkva_bc = AR.alloc([KVL], F32, "gkva_bc")
        P.dma("sp", lambda e: e.dma_start(out=gkva_bc.ap, in_=I["g_kva"].partition_broadcast(128)), writes=[gkva_bc.dep])

        def MM(out, lhsT, rhs, start=True, stop=True, r=(), w=(), skip=False):
            P.op("pe", lambda e: e.matmul(out, lhsT=lhsT, rhs=rhs, start=start, stop=stop, skip_group_check=skip),
                 reads=r, writes=w)

        def TR(out, in_, ident, r=(), w=()):
            P.op("pe", lambda e: e.transpose(out, in_, ident), reads=r, writes=w)

        def ACTF(out, in_, func, r=(), w=(), eng="act", **kw):
            P.op(eng, lambda e: e.activation(out=out, in_=in_, func=func, **kw), reads=r, writes=w)

        def TT(eng, out, in0, in1, op, r=(), w=()):
            P.op(eng, lambda e: e.tensor_tensor(out=out, in0=in0, in1=in1, op=op), reads=r, writes=w)

        def TS(eng, out, in0, s1, s2, op0, op1=None, r=(), w=()):
            if op1 is None:
                P.op(eng, lambda e: e.tensor_scalar(out=out, in0=in0, scalar1=s1, scalar2=None, op0=op0), reads=r, writes=w)
            else:
                P.op(eng, lambda e: e.tensor_scalar(out=out, in0=in0, scalar1=s1, scalar2=s2, op0=op0, op1=op1), reads=r, writes=w)

        def STT(out, in0, scalar, in1, op0, op1, r=(), w=()):
            P.op("dve", lambda e: e.scalar_tensor_tensor(out=out, in0=in0, scalar=scalar, in1=in1, op0=op0, op1=op1),
                 reads=r, writes=w)

        def CP(eng, out, in_, r=(), w=()):
            if eng == "act":
                P.op("act", lambda e: e.activation(out=out, in_=in_, func=AF.Copy), reads=r, writes=w)
            else:
                P.op(eng, lambda e: e.tensor_copy(out=out, in_=in_), reads=r, writes=w)

        def DMA(out, in_, r=(), w=(), q="sp", slow=False):
            if slow:
                P.dma(q, lambda e: e.dma_start(out=out, in_=in_, allow_slow_non_contiguous=True), reads=r, writes=w)
            else:
                P.dma(q, lambda e: e.dma_start(out=out, in_=in_), reads=r, writes=w)

        def RECIP(out, in_, r=(), w=()):
            P.op("dve", lambda e: e.reciprocal(out=out, in_=in_), reads=r, writes=w)

        def MEMSET(eng, out, val, w=()):
            P.op(eng, lambda e: e.memset(out, val), writes=w)

        gq_col = AR.alloc([1], F32, "gq_col")
        gk_col = AR.alloc([1], F32, "gk_col")
        gq2 = AR.alloc([1], F32, "gq2")
        gkpe = AR.alloc([1], F32, "gkpe")
        DMA(gq_col.ap[0:96, :], I["g_q"].rearrange("(p o) -> p o", o=1), w=[gq_col.dep], slow=True)
        DMA(gk_col.ap[0:96, :], I["g_k"].rearrange("(p o) -> p o", o=1), w=[gk_col.dep], slow=True)
        DMA(gkpe.ap[0:32, :], I["g_k"][64:96].rearrange("(p o) -> p o", o=1), w=[gkpe.dep], slow=True)
        CP("dve", gq2.ap[0:96, :], gq_col.ap[0:96, :], r=[gq_col.dep], w=[gq2.dep])
        TT("dve", gq2.ap[0:64, :], gq_col.ap[0:64, :], gk_col.ap[0:64, :], ALU.mult, r=[gq_col.dep, gk_col.dep], w=[gq2.dep])
        shiftT = AR.alloc([96], BF16, "shiftT")
        shiftrotT = AR.alloc([96], BF16, "shiftrotT")
        rot96T = AR.alloc([96], BF16, "rot96T")
        DMA(shiftT.ap[0:32, :], I["shiftT"], w=[shiftT.dep])
        DMA(shiftrotT.ap[0:32, :], I["shiftrotT"], w=[shiftrotT.dep])
        DMA(rot96T.ap[0:96, :], I["rot96T"], w=[rot96T.dep])
        gatt_bc = AR.alloc([512], F32, "gatt_bc")
        ghy_bc = AR.alloc([512], F32, "ghy_bc")
        DMA(gatt_bc.ap, I["g_out_att"].partition_broadcast(128), w=[gatt_bc.dep])
        DMA(ghy_bc.ap, I["g_out_hy"].partition_broadcast(128), w=[ghy_bc.dep])

        Uscr = [nc.dram_tensor("Uscr0", [1536, TP], BF16).ap(), nc.dram_tensor("Uscr1", [1536, TL], BF16).ap()]
        Uscr_dep = [Dep("Uscr0"), Dep("Uscr1")]

        def run_group(v):
            ntok = TP if v == 0 else TL
            nq = TP if v == 0 else 1026
            nqt = 8 if v == 0 else 9
            xin = I["xp"] if v == 0 else I["xs"]
            nkeys = ntok + (PAST if v == 1 else 0)
            nkt = nkeys // 128
            gm = AR.mark()
            O_all = AR.alloc([nqt, 512], F32, "O_all")
            g2m = AR.mark()
            bias_fm = AR.alloc([15], F32, "bias_fm")
            bias_row = AR.alloc([288], BF16, "bias_row")
            kvcT = AR.alloc([2, nkeys], BF16, "kvcT")
            kpeT = AR.alloc([nkeys], BF16, "kpeT")
            q_cT = AR.alloc([3, nq], F32, "q_cT")
            sspe = AR.alloc([nkt], F32, "sspe")
            hThalo = AR.alloc([KC, 2], BF16, "hThalo")
            p1m = AR.mark()
            Win = AR.alloc([KC, DIN], BF16, "Win")
            win_d = I["w_in"].rearrange("(k p) n -> p k n", p=128)
            for k in range(KC):
                DMA(Win.ap[:, k, :], win_d[:, k, :], w=[Win.dep], q="pool")

            xts = [AR.alloc([D], F32, "xt%d" % i) for i in range(3)]
            xns = [AR.alloc([D], BF16, "xn%d" % i) for i in range(2)]
            junk = AR.alloc([D], F32, "junk")
            stat = AR.alloc([8], F32, "stat")
            hTg = [AR.alloc([KC, 512], BF16, "hTg%d" % i) for i in range(2)]
            ckvo = [AR.alloc([KVL + ROPE], F32, "ckvo%d" % i) for i in range(2)]
            ckvb = [AR.alloc([KVL + ROPE], BF16, "ckvb%d" % i) for i in range(2)]
            ustg = [AR.alloc([512], BF16, "ustg%d" % i) for i in range(3)]
            ngroups = ntok // 512
            ucount = 0
            hTdeps = [[Dep("hT%d_%d" % (i, k)) for k in range(KC)] for i in range(2)]
            for i in range(2):
                for k in range(KC):
                    hTdeps[i][k].r = dict(hTg[i].dep.r)
            statB = AR.alloc([8], F32, "statB")

            def tile_pre(g, tt):
                ti = g * 4 + tt
                xt = xts[ti % 3]
                xn = xns[ti % 2]
                DMA(xt.ap, xin[ti * 128:(ti + 1) * 128, :], w=[xt.dep])
                ACTF(junk.ap, xt.ap, AF.Square, r=[xt.dep], w=[junk.dep, stat.dep], accum_out=stat.ap[:, 0:1])
                ACTF(stat.ap[:, 1:2], stat.ap[:, 0:1], AF.Sqrt, r=[stat.dep, epsb.dep], w=[stat.dep], scale=1.0 / D,
                     bias=epsb.ap[:, 0:1])
                RECIP(stat.ap[:, 2:3], stat.ap[:, 1:2], r=[stat.dep], w=[stat.dep])
                TS("dve", xn.ap, xt.ap, stat.ap[:, 2:3], None, ALU.mult, r=[xt.dep, stat.dep], w=[xn.dep])

            def tile_mid(g, tt):
                hT = hTg[g % 2]
                hd = hTdeps[g % 2]
                ti = g * 4 + tt
                xn = xns[ti % 2]
                pb = ti % 2
                for k in range(KC):
                    TR(psb(pb)[:, k * 128:(k + 1) * 128], xn.ap[:, k * 128:(k + 1) * 128], ident_b.ap,
                       r=[xn.dep, ident_b.dep], w=[ps[pb].dep])
                for k in range(KC):
                    if True:
                        TS("dve", hT.ap[:, k, tt * 128:(tt + 1) * 128], psb(pb)[:, k * 128:(k + 1) * 128], A1.ap[:, k, v:v + 1],
                           mod.ap[:, v, k:k + 1], ALU.mult, ALU.add, r=[ps[pb].dep, A1.dep, mod.dep], w=[hd[k]])
                    else:
                        ACTF(hT.ap[:, k, tt * 128:(tt + 1) * 128], psb(pb)[:, k * 128:(k + 1) * 128], AF.Identity,
                             r=[ps[pb].dep, A1.dep, mod.dep], w=[hd[k]], scale=A1.ap[:, k, v:v + 1], bias=mod.ap[:, v, k:k + 1])
                kb = 2 + (ti % 2)
                for k in range(KC):
                    MM(ps[kb].ap[:, 0:288], hT.ap[:, k, tt * 128:(tt + 1) * 128], Win.ap[:, k, 384:672],
                       start=(k == 0), stop=(k == KC - 1), r=[hd[k], Win.dep], w=[ps[kb].dep])

            def tile_post_a(g, tt):
                ti = g * 4 + tt
                kb = 2 + (ti % 2)
                co = ckvo[ti % 2]
                cb = ckvb[ti % 2]
                ACTF(junk.ap[:, 0:256], ps[kb].ap[:, 0:256], AF.Square, r=[ps[kb].dep], w=[junk.dep, statB.dep],
                     accum_out=statB.ap[:, 3:4])
                ACTF(statB.ap[:, 4:5], statB.ap[:, 3:4], AF.Sqrt, r=[statB.dep, epsb.dep], w=[statB.dep], scale=1.0 / KVL,
                     bias=epsb.ap[:, 0:1])
                RECIP(statB.ap[:, 5:6], statB.ap[:, 4:5], r=[statB.dep], w=[statB.dep])
                STT(co.ap[:, 0:256], ps[kb].ap[:, 0:256], statB.ap[:, 5:6], gkva_bc.ap, ALU.mult, ALU.mult,
                    r=[ps[kb].dep, statB.dep, gkva_bc.dep], w=[co.dep])
                CP("act", co.ap[:, 256:288], ps[kb].ap[:, 256:288], r=[ps[kb].dep], w=[co.dep])
                ACTF(junk.ap[:, 0:32], co.ap[:, 256:288], AF.Square, r=[co.dep], w=[junk.dep, sspe.dep],
                     accum_out=sspe.ap[:, ti:ti + 1])
                if v == 0:
                    DMA(O["new_ckv"][ti * 128:(ti + 1) * 128, :], co.ap[:, 0:256], r=[co.dep], q="pool")
                    DMA(O["new_kpe"][ti * 128:(ti + 1) * 128, :], co.ap[:, 256:288], r=[co.dep], q="pool")
                CP("dve", cb.ap, co.ap, r=[co.dep], w=[cb.dep])

            def tile_post_b(g, tt):
                ti = g * 4 + tt
                cb = ckvb[ti % 2]
                tb = 6 + (ti % 2)
                TR(psb(tb)[:, 0:128], cb.ap[:, 0:128], ident_b.ap, r=[cb.dep, ident_b.dep], w=[ps[tb].dep])
                TR(psb(tb)[:, 128:256], cb.ap[:, 128:256], ident_b.ap, r=[cb.dep, ident_b.dep], w=[ps[tb].dep])
                TR(psb(tb)[0:32, 256:384], cb.ap[:, 256:288], ident_b.ap, r=[cb.dep, ident_b.dep], w=[ps[tb].dep])
                CP("act", kvcT.ap[:, :, ti * 128:(ti + 1) * 128], psb(tb)[:, 0:256].rearrange("p (c t) -> p c t", t=128),
                   r=[ps[tb].dep], w=[kvcT.dep])
                CP("act", kpeT.ap[0:32, ti * 128:(ti + 1) * 128], psb(tb)[0:32, 256:384], r=[ps[tb].dep], w=[kpeT.dep])

            ucount_box = [0]

            def proj_ops(g, part):
                hT = hTg[g % 2]
                chunks = list(range(3, 15))
                if v == 1 and g in (3, 4, 5, 6):
                    chunks = list(range(3, 11))
                if v == 0 or g < 2:
                    chunks = [0, 1, 2] + chunks
                for ci in chunks[part::4]:
                    ucount = ucount_box[0]
                    c0 = ci * 128 if ci < 3 else 672 + (ci - 3) * 128
                    fb = 4 + (ucount % 2)
                    for k in range(KC):
                        MM(ps[fb].ap, Win.ap[:, k, c0:c0 + 128], hT.ap[:, k, :], start=(k == 0), stop=(k == KC - 1),
                           r=[Win.dep, hTdeps[g % 2][k]], w=[ps[fb].dep])
                    if ci < 3:
                        CP("act", q_cT.ap[:, ci, g * 512:(g + 1) * 512], ps[fb].ap, r=[ps[fb].dep], w=[q_cT.dep])
                    else:
                        us = ustg[ucount % 3]
                        CP("act" if ucount % 2 else "dve", us.ap, ps[fb].ap, r=[ps[fb].dep], w=[us.dep])
                        DMA(Uscr[v][(ci - 3) * 128:(ci - 2) * 128, g * 512:(g + 1) * 512], us.ap, r=[us.dep], w=[Uscr_dep[v]], q="pool")
                    ucount_box[0] += 1
                if part == 3:
                    if v == 1 and g == 2:
                        CP("pool", hThalo.ap[:, :, 1:2], hT.ap[:, :, 0:1], r=hTdeps[g % 2], w=[hThalo.dep])
                    if v == 1 and g == 7:
                        CP("pool", hThalo.ap[:, :, 0:1], hT.ap[:, :, 511:512], r=hTdeps[g % 2], w=[hThalo.dep])

            ntl = ngroups * 4
            tile_pre(0, 0)
            for i in range(ntl + 4):
                g, tt = i // 4, i % 4
                if i + 1 < ntl:
                    tile_pre((i + 1) // 4, (i + 1) % 4)
                if i < ntl:
                    tile_mid(g, tt)
                if i < ntl:
                    tile_post_a(g, tt)
                if g >= 1:
                    proj_ops(g - 1, tt)
                if i < ntl:
                    tile_post_b(g, tt)
            MEMSET("pool", hTg[0].ap[:, 0, 0:1], 0.0, w=hTdeps[0] + [hTg[0].dep])
            MEMSET("pool", hTg[1].ap[:, 0, 0:1], 0.0, w=hTdeps[1] + [hTg[1].dep])
            if v == 1:
                for ci in range(3):
                    for k in range(KC):
                        MM(ps[4].ap[:, 0:2], Win.ap[:, k, ci * 128:(ci + 1) * 128], hThalo.ap[:, k, :], start=(k == 0),
                           stop=(k == KC - 1), r=[Win.dep, hThalo.dep], w=[ps[4].dep])
                    CP("act", q_cT.ap[:, ci, 1024:1026], ps[4].ap[:, 0:2], r=[ps[4].dep], w=[q_cT.dep])
                cst = AR.alloc([4, KVL + ROPE], F32, "cst")
                csb = AR.alloc([4, KVL + ROPE], BF16, "csb")
                DMA(cst.ap[:, :, 0:256], I["ckv_c"].rearrange("(t p) f -> p t f", p=128), w=[cst.dep])
                DMA(cst.ap[:, :, 256:288], I["kpe_c"].rearrange("(t p) f -> p t f", p=128), w=[cst.dep])
                CP("pool", csb.ap, cst.ap, r=[cst.dep], w=[csb.dep])
                for t4 in range(4):
                    ti = 32 + t4
                    ACTF(junk.ap[:, 0:32], cst.ap[:, t4, 256:288], AF.Square, r=[cst.dep], w=[junk.dep, sspe.dep],
                         accum_out=sspe.ap[:, ti:ti + 1])
                    tb = 6 + (t4 % 2)
                    TR(psb(tb)[:, 0:128], csb.ap[:, t4, 0:128], ident_b.ap, r=[csb.dep, ident_b.dep], w=[ps[tb].dep])
                    TR(psb(tb)[:, 128:256], csb.ap[:, t4, 128:256], ident_b.ap, r=[csb.dep, ident_b.dep], w=[ps[tb].dep])
                    TR(psb(tb)[0:32, 256:384], csb.ap[:, t4, 256:288], ident_b.ap, r=[csb.dep, ident_b.dep], w=[ps[tb].dep])
                    CP("act", kvcT.ap[:, :, ti * 128:(ti + 1) * 128], psb(tb)[:, 0:256].rearrange("p (c t) -> p c t", t=128),
                       r=[ps[tb].dep], w=[kvcT.dep])
                    CP("act", kpeT.ap[0:32, ti * 128:(ti + 1) * 128], psb(tb)[0:32, 256:384], r=[ps[tb].dep], w=[kpeT.dep])
            AR.release(p1m)

            am = AR.mark()
            Wuq = AR.alloc([3, N_HEADS * QK], BF16, "Wuq")
            Wukv = AR.alloc([2, 1024], BF16, "Wukv")
            DMA(Wuq.ap, I["w_uq"].rearrange("(k p) n -> p k n", p=128), w=[Wuq.dep], q="pool")
            DMA(Wukv.ap, I["w_ukv"].rearrange("(k p) n -> p k n", p=128), w=[Wukv.dep], q="pool")
            Vext = AR.alloc([nkt, N_HEADS, 65], BF16, "Vext")
            MEMSET("pool", Vext.ap[:, :, :, 64:65], 1.0, w=[Vext.dep])
            kscale = AR.alloc([nkt, N_HEADS], F32, "kscale")
            KT = [AR.alloc([nkeys], BF16, "KT%d" % i) for i in range(2)]
            QT = AR.alloc([N_HEADS, nq], BF16, "QT")
            MEMSET("pool", QT.ap[96:128, :, :], 0.0, w=[QT.dep])
            for i_ in range(2):
                MEMSET("pool", KT[i_].ap[96:128, :], 0.0, w=[KT[i_].dep])
            tmpA = [AR.alloc([512], F32, "tmpA%d" % i) for i in range(2)]
            tmpB = [AR.alloc([512], F32, "tmpB%d" % i) for i in range(2)]
            tmpC = [AR.alloc([512], BF16, "tmpC%d" % i) for i in range(2)]
            junk2 = AR.alloc([512], F32, "junk2")
            st2 = AR.alloc([32], F32, "st2")
            gkb = [AR.alloc([512], BF16, "gkb%d" % i) for i in range(2)]
            if v == 1:
                ctab = [AR.alloc([512], F32, "ctab%d" % i) for i in range(2)]
                stab = [AR.alloc([512], F32, "stab%d" % i) for i in range(2)]
            def rope_block(kb):
                ks = slice(kb * 512, (kb + 1) * 512)
                gk = gkb[kb % 2]
                TS("pool", gk.ap[0:32, :], kpeT.ap[0:32, ks], gkpe.ap[0:32, 0:1], 0.0, ALU.mult, ALU.add,
                   r=[kpeT.dep, gkpe.dep], w=[gk.dep])
                MM(ps[5].ap[0:96, :], shiftT.ap[0:32, :], gk.ap[0:32, :], r=[shiftT.dep, gk.dep], w=[ps[5].dep])
                if v == 0:
                    CP("act", KT[0].ap[64:96, ks], ps[5].ap[64:96, :], r=[ps[5].dep], w=[KT[0].dep])
                    CP("dve", KT[1].ap[64:96, ks], ps[5].ap[64:96, :], r=[ps[5].dep], w=[KT[1].dep])
                else:
                    ct, stb = ctab[kb % 2], stab[kb % 2]
                    DMA(ct.ap[64:96, :], I["cosk"][:, ks], w=[ct.dep])
                    DMA(stb.ap[64:96, :], I["sink"][:, ks], w=[stb.dep])
                    MM(ps[6].ap[0:96, :], shiftrotT.ap[0:32, :], gk.ap[0:32, :], r=[shiftrotT.dep, gk.dep], w=[ps[6].dep])
                    ta, tb_ = tmpA[kb % 2], tmpB[kb % 2]
                    TT("dve", ta.ap[64:96, :], ps[5].ap[64:96, :], ct.ap[64:96, :], ALU.mult, r=[ps[5].dep, ct.dep], w=[ta.dep])
                    TT("dve", tb_.ap[64:96, :], ps[6].ap[64:96, :], stb.ap[64:96, :], ALU.mult, r=[ps[6].dep, stb.dep], w=[tb_.dep])
                    TT("dve", KT[0].ap[64:96, ks], ta.ap[64:96, :], tb_.ap[64:96, :], ALU.add, r=[ta.dep, tb_.dep], w=[KT[0].dep])
                    TT("dve", KT[1].ap[64:96, ks], ta.ap[64:96, :], tb_.ap[64:96, :], ALU.add, r=[ta.dep, tb_.dep], w=[KT[1].dep])

            def kv_tile(kt):
                for half in range(2):
                    bk = (kt % 2) * 2 + half
                    for c in range(2):
                        MM(ps[bk].ap, kvcT.ap[:, c, kt * 128:(kt + 1) * 128], Wukv.ap[:, c, half * 512:(half + 1) * 512],
                           start=(c == 0), stop=(c == 1), r=[kvcT.dep, Wukv.dep], w=[ps[bk].dep])
                    pv_ = ps[bk].ap.rearrange("p (h x) -> p h x", x=128)
                    CP("act", Vext.ap[:, kt, half * 4:(half + 1) * 4, 0:64], pv_[:, :, 64:128], r=[ps[bk].dep], w=[Vext.dep])
                    ACTF(junk2.ap[:, 0:256].rearrange("p (h x) -> p h x", x=64), pv_[:, :, 0:64], AF.Square,
                         r=[ps[bk].dep], w=[junk2.dep])
                    P.op("dve", lambda e, half=half: e.reduce_sum(
                        out=st2.ap[:, half * 4:(half + 1) * 4], in_=junk2.ap[:, 0:256].rearrange("p (h x) -> p h x", x=64),
                        axis=AX.X), reads=[junk2.dep], writes=[st2.dep])
                TS("dve", st2.ap[:, 8:16], st2.ap[:, 0:8], sspe.ap[:, kt:kt + 1], 1.0 / QK, ALU.add, ALU.mult,
                   r=[st2.dep, sspe.dep], w=[st2.dep])
                ACTF(st2.ap[:, 16:24], st2.ap[:, 8:16], AF.Sqrt, r=[st2.dep, epsb.dep], w=[st2.dep], bias=epsb.ap[:, 0:1])
                RECIP(st2.ap[:, 24:32], st2.ap[:, 16:24], r=[st2.dep], w=[st2.dep])
                TS("dve", kscale.ap[:, kt, :], st2.ap[:, 24:32], float(QK) ** -0.5, None, ALU.mult, r=[st2.dep], w=[kscale.dep])


            qnT = AR.alloc([3, nq], BF16, "qnT")
            sqb = AR.alloc([3, 512], BF16, "sqb")
            if v == 1:
                cosq = AR.alloc([nq], F32, "cosq")
                sinq = AR.alloc([nq], F32, "sinq")
                DMA(cosq.ap[0:96, :], I["cosq"], w=[cosq.dep])
                DMA(sinq.ap[0:96, :], I["sinq"], w=[sinq.dep])
            qgroups = [(0, 512), (512, 512)] + ([(1024, 2)] if v == 1 else [])

            def q_gen():
                for gi, (q0, qn_) in enumerate(qgroups):
                    qs = slice(q0, q0 + qn_)
                    ACTF(sqb.ap[:, :, 0:qn_], q_cT.ap[:, :, qs], AF.Square, r=[q_cT.dep], w=[sqb.dep])
                    for c in range(3):
                        MM(ps[7].ap[:, 0:qn_], ones_b.ap, sqb.ap[:, c, 0:qn_], start=(c == 0), stop=(c == 2),
                           r=[ones_b.dep, sqb.dep], w=[ps[7].dep])
                    ta = tmpA[gi % 2]
                    ACTF(ta.ap[:, 0:qn_], ps[7].ap[:, 0:qn_], AF.Ln, r=[ps[7].dep, epsb.dep], w=[ta.dep], scale=1.0 / QL,
                         bias=epsb.ap[:, 0:1])
                    ACTF(ta.ap[:, 0:qn_], ta.ap[:, 0:qn_], AF.Exp, r=[ta.dep], w=[ta.dep], scale=-0.5)
                    for c in range(3):
                        STT(qnT.ap[:, c, qs], q_cT.ap[:, c, qs], vcol("g_qa", c), ta.ap[:, 0:qn_], ALU.mult, ALU.mult,
                            r=[q_cT.dep, vec.dep, ta.dep], w=[qnT.dep])
                    yield
                    for h in range(N_HEADS):
                        qb = 5 + (h % 2)
                        for c in range(3):
                            MM(ps[qb].ap[0:96, 0:qn_], Wuq.ap[:, c, h * QK:(h + 1) * QK], qnT.ap[:, c, qs], start=(c == 0),
                               stop=(c == 2), r=[Wuq.dep, qnT.dep], w=[ps[qb].dep])
                        tb_ = tmpB[h % 2]
                        tc_ = tmpC[h % 2]
                        CP("act", tb_.ap[0:96, 0:qn_], ps[qb].ap[0:96, 0:qn_], r=[ps[qb].dep], w=[tb_.dep])
                        ACTF(tc_.ap[0:96, 0:qn_], ps[qb].ap[0:96, 0:qn_], AF.Square, r=[ps[qb].dep], w=[tc_.dep])
                        MM(ps[7].ap[0:96, 0:qn_], ones_b.ap[0:96, 0:96], tc_.ap[0:96, 0:qn_], r=[ones_b.dep, tc_.dep], w=[ps[7].dep])
                        t2 = tmpA[(gi + 1) % 2]
                        ACTF(t2.ap[0:96, 0:qn_], ps[7].ap[0:96, 0:qn_], AF.Ln, r=[ps[7].dep, epsb.dep], w=[t2.dep],
                             scale=1.0 / QK, bias=epsb.ap[0:96, 0:1])
                        ACTF(t2.ap[0:96, 0:qn_], t2.ap[0:96, 0:qn_], AF.Exp, r=[t2.dep], w=[t2.dep], scale=-0.5)
                        if v == 0:
                            STT(QT.ap[0:96, h, qs], tb_.ap[0:96, 0:qn_], gq2.ap[0:96, 0:1], t2.ap[0:96, 0:qn_], ALU.mult, ALU.mult,
                                r=[tb_.dep, gq2.dep, t2.dep], w=[QT.dep])
                        else:
                            STT(tb_.ap[0:96, 0:qn_], tb_.ap[0:96, 0:qn_], gq2.ap[0:96, 0:1], t2.ap[0:96, 0:qn_], ALU.mult, ALU.mult,
                                r=[tb_.dep, gq2.dep, t2.dep], w=[tb_.dep])
                            CP("pool", tc_.ap[0:96, 0:qn_], tb_.ap[0:96, 0:qn_], r=[tb_.dep], w=[tc_.dep])
                            MM(ps[qb].ap[0:96, 0:qn_], rot96T.ap[0:96, :], tc_.ap[0:96, 0:qn_], r=[rot96T.dep, tc_.dep], w=[ps[qb].dep])
                            TT("dve", t2.ap[0:96, 0:qn_], ps[qb].ap[0:96, 0:qn_], sinq.ap[0:96, qs], ALU.mult,
                               r=[ps[qb].dep, sinq.dep], w=[t2.dep])
                            TT("dve", tb_.ap[0:96, 0:qn_], tb_.ap[0:96, 0:qn_], cosq.ap[0:96, qs], ALU.mult,
                               r=[tb_.dep, cosq.dep], w=[tb_.dep])
                            TT("dve", QT.ap[0:96, h, qs], tb_.ap[0:96, 0:qn_], t2.ap[0:96, 0:qn_], ALU.add,
                               r=[tb_.dep, t2.dep], w=[QT.dep])

                        yield

            qg_ = q_gen()
            for kb in range(nkeys // 512):
                rope_block(kb)
            for kt in range(nkt):
                kv_tile(kt)
                if kt % 3 != 2:
                    next(qg_, None)
            for _ in qg_:
                pass
            PT = [AR.alloc([512], BF16, "PT%d" % i) for i in range(3)]
            rc = AR.alloc([8], F32, "rc")
            if v == 0:
                att_groups = [(s_ * 256, 256, [2 * s_, 2 * s_ + 1]) for s_ in range(4)]
            else:
                att_groups = [(0, 512, list(range(nkt))), (512, 512, list(range(nkt))), (1024, 2, list(range(nkt)))]
            it = 0
            accn = 0
            for h in range(N_HEADS):
                KTh = KT[h % 2]
                for kb in range(nkeys // 512):
                    ks = slice(kb * 512, (kb + 1) * 512)
                    fb = 5 + (kb % 2)
                    for c in range(2):
                        MM(ps[fb].ap[:, :], Wukv.ap[:, c, h * 128:(h + 1) * 128], kvcT.ap[:, c, ks], start=(c == 0), stop=(c == 1),
                           r=[Wukv.dep, kvcT.dep], w=[ps[fb].dep])
                    CP("dve" if kb % 2 else "act", KTh.ap[0:64, ks], ps[fb].ap[0:64, :], r=[ps[fb].dep], w=[KTh.dep])
                for (q0, qn_, kts) in att_groups:
                    ab = 3 + (accn % 2)
                    accn += 1
                    nqt_g = (qn_ + 127) // 128
                    def s_mm(ki_, it_):
                        kt_ = kts[ki_]
                        MM(ps[it_ % 3].ap[:, 0:qn_], KTh.ap[:, kt_ * 128:(kt_ + 1) * 128], QT.ap[:, h, q0:q0 + qn_],
                           r=[KTh.dep, QT.dep], w=[ps[it_ % 3].dep])
                    s_mm(0, it)
                    for ki, kt in enumerate(kts):
                        sb_ = it % 3
                        pt = PT[it % 3]
                        it += 1
                        if ki + 1 < len(kts):
                            s_mm(ki + 1, it)
                        ACTF(pt.ap[:, 0:qn_], ps[sb_].ap[:, 0:qn_], AF.Exp, r=[ps[sb_].dep, kscale.dep], w=[pt.dep],
                             scale=kscale.ap[:, kt, h:h + 1])
                        for qt in range(nqt_g):
                            m_ = min(128, qn_ - qt * 128)
                            MM(ps[ab].ap[0:m_, qt * 65:(qt + 1) * 65], pt.ap[:, qt * 128:qt * 128 + m_], Vext.ap[:, kt, h, :],
                               start=(ki == 0 and qt == 0), stop=(ki == len(kts) - 1), r=[pt.dep, Vext.dep], w=[ps[ab].dep], skip=True)
                    for qt in range(nqt_g):
                        m_ = min(128, qn_ - qt * 128)
                        qti = q0 // 128 + qt
                        RECIP(rc.ap[0:m_, qt:qt + 1], ps[ab].ap[0:m_, qt * 65 + 64:qt * 65 + 65], r=[ps[ab].dep], w=[rc.dep])
                        TS("dve", O_all.ap[0:m_, qti, h * 64:(h + 1) * 64], ps[ab].ap[0:m_, qt * 65:qt * 65 + 64],
                           rc.ap[0:m_, qt:qt + 1], None, ALU.mult, r=[ps[ab].dep, rc.dep], w=[O_all.dep])
            AR.release(g2m)
            d_ = dbg_out("att%d" % v, [128, nqt * 512])
            if d_ is not None:
                DMA(d_, O_all.ap.rearrange("p t f -> p (t f)"), r=[O_all.dep])
            nout2 = 8 if v == 0 else 10
            y2hy = AR.alloc([nout2, 512], F32, "y2hy")
            hm = AR.mark()
            nb = ntok // 128
            nfb = 2 if v == 0 else 32
            nfilt = nfb * 128
            nks = 2 * nfb
            tsuf = "p" if v == 0 else "l"
            if v == 0:
                out1 = list(range(8))
                out2 = list(range(8))
            else:
                out1 = list(range(32))
                out2 = list(range(8)) + [31, 8]
            cg = 16 if v == 0 else 4
            ng = 128 // cg
            nh = 2
            Fb = AR.alloc([512], BF16, "Fb")
            CBt = AR.alloc([4, 128], BF16, "CB")
            R1T = AR.alloc([128], BF16, "R1T")
            E0T = AR.alloc([128], BF16, "E0T")
            DMA(Fb.ap, I["Fb"], w=[Fb.dep])
            DMA(CBt.ap, I["CB_" + tsuf], w=[CBt.dep])
            DMA(R1T.ap, I["R1T"], w=[R1T.dep])
            DMA(E0T.ap, I["E0T"], w=[E0T.dep])
            nrow_d = nb * cg
            FAd = AR.alloc([nh, 3, 128], BF16, "FAd")
            DMA(FAd.ap[0:nrow_d], I["FAd_" + tsuf], w=[FAd.dep])
            nkch = 1 if v == 0 else 2
            krows = 128
            FAk = [AR.alloc([nh, 3, 128], BF16, "FAk%d" % i) for i in range(nkch)]
            for i in range(nkch):
                DMA(FAk[i].ap[0:krows], I["FAk_" + tsuf][i], w=[FAk[i].dep])
            np1 = 2 * len(out1)
            np2 = 2 * len(out2)
            GA1 = AR.alloc([nh, 2, 2 * np1 * cg], BF16, "GA1")
            GA2 = AR.alloc([nh, 2, 2 * np2 * cg], BF16, "GA2")
            DMA(GA1.ap, I["GA1_" + tsuf], w=[GA1.dep])
            DMA(GA2.ap, I["GA2_" + tsuf], w=[GA2.dep])
            h2T = AR.alloc([nfilt], BF16, "h2T")
            W3b = AR.alloc([2048], BF16, "W3b")
            DMA(W3b.ap[0:64, :], I["filt_w3"], w=[W3b.dep], q="pool")
            fm_ = AR.mark()
            zT = AR.alloc([nfilt], F32, "zT")
            DMA(zT.ap[0:33, :], I["zT_" + tsuf], w=[zT.dep])
            w1t = AR.alloc([64], F32, "w1t")
            w2t = AR.alloc([64], F32, "w2t")
            DMA(w1t.ap[0:33, :], I["filt_w1"], w=[w1t.dep])
            DMA(w2t.ap[0:64, :], I["filt_w2"], w=[w2t.dep])
            fcol = AR.alloc([8], F32, "fcol")
            for i_, nm in enumerate(["filt_b1", "filt_freq1", "filt_b2", "filt_freq2"]):
                DMA(fcol.ap[0:64, i_:i_ + 1], I[nm].rearrange("(p o) -> p o", o=1), w=[fcol.dep], slow=True)
            for l_ in range(2):
                TS("dve", fcol.ap[0:64, 4 + 2 * l_:5 + 2 * l_], fcol.ap[0:64, 1 + 2 * l_:2 + 2 * l_], 1.0 / 3.0, None, ALU.mult,
                   r=[fcol.dep], w=[fcol.dep])
                TT("dve", fcol.ap[0:64, 5 + 2 * l_:6 + 2 * l_], fcol.ap[0:64, 4 + 2 * l_:5 + 2 * l_], fcol.ap[0:64, 2 * l_:2 * l_ + 1],
                   ALU.mult, r=[fcol.dep], w=[fcol.dep])
            h1T = AR.alloc([nfilt], F32, "h1T")
            sA = [AR.alloc([512], F32, "sA%d" % i) for i in range(2)]
            sB = [AR.alloc([512], F32, "sB%d" % i) for i in range(2)]
            ncb = max(1, nfilt // 512)
            cw = min(512, nfilt)
            for l_ in range(2):
                for cbk in range(ncb):
                    cs = slice(cbk * cw, (cbk + 1) * cw)
                    fb_ = 6 + (cbk % 2)
                    if l_ == 0:
                        MM(ps[fb_].ap[0:64, 0:cw], w1t.ap[0:33, 0:64], zT.ap[0:33, cs], r=[w1t.dep, zT.dep], w=[ps[fb_].dep])
                    else:
                        MM(ps[fb_].ap[0:64, 0:cw], w2t.ap[0:64, 0:64], h1T.ap[0:64, cs], r=[w2t.dep, h1T.dep], w=[ps[fb_].dep])
                    s_ = sA[cbk % 2]
                    q_ = sB[cbk % 2]
                    ACTF(s_.ap[0:64, 0:cw], ps[fb_].ap[0:64, 0:cw], AF.Sin, r=[ps[fb_].dep, fcol.dep], w=[s_.dep],
                         scale=fcol.ap[0:64, 4 + 2 * l_:5 + 2 * l_], bias=fcol.ap[0:64, 5 + 2 * l_:6 + 2 * l_])
                    TT("dve", q_.ap[0:64, 0:cw], s_.ap[0:64, 0:cw], s_.ap[0:64, 0:cw], ALU.mult, r=[s_.dep], w=[q_.dep])
                    TS("dve", q_.ap[0:64, 0:cw], q_.ap[0:64, 0:cw], -4.0, 3.0, ALU.mult, ALU.add, r=[q_.dep], w=[q_.dep])
                    if l_ == 0:
                        TT("dve", h1T.ap[0:64, cs], q_.ap[0:64, 0:cw], s_.ap[0:64, 0:cw], ALU.mult, r=[q_.dep, s_.dep], w=[h1T.dep])
                    else:
                        TT("dve", h2T.ap[0:64, cs], q_.ap[0:64, 0:cw], s_.ap[0:64, 0:cw], ALU.mult, r=[q_.dep, s_.dep], w=[h2T.dep])
            AR.release(fm_)
            if v == 1:
                seamc = AR.alloc([4], F32, "seamc")
                DMA(seamc.ap, I["seamc"], w=[seamc.dep])
                cw02 = AR.alloc([8], F32, "cw02")

            XB = [2, 3, 6, 7]

            def fwd2d(chunks, FAt, consume, consume_pe=None):
                s1p = [[AR.alloc([2, 2, 256], BF16, "s1p_%d_%d" % (i, c_)) for c_ in range(len(chunks))] for i in range(2)]

                def stage1(g):
                    for ci_, (U, sn) in enumerate(chunks):
                        M = sn * cg
                        sbk = (g % 2) if ci_ == 0 else 4 + (g % 2)
                        MM(ps[sbk].ap[0:M, :], U.ap[:, cg * g:cg * (g + 1), :].rearrange("p c s -> p (c s)"), Fb.ap,
                           r=[U.dep, Fb.dep], w=[ps[sbk].dep])
                        S = s1p[(g // 2) % 2][ci_]
                        CP("act" if ci_ == 0 else "dve", S.ap[0:M, :, g % 2, :], ps[sbk].ap[0:M, :].rearrange("p (t b) -> p t b", b=256),
                           r=[ps[sbk].dep], w=[S.dep])

                def stage2(p):
                    banks = []
                    for h in range(nh):
                        xb_ = XB[(2 * p + h) % 4]
                        k_ = 0
                        nt_ = 2 * len(chunks)
                        for ci_, (U, sn) in enumerate(chunks):
                            M = sn * cg
                            S = s1p[p % 2][ci_]
                            for (ti_, tm_) in ((0, 0), (2, 1)):
                                MM(ps[xb_].ap[:, :], FAt[ci_].ap[0:M, h, ti_, :], S.ap[0:M, tm_, :, :].rearrange("p g b -> p (g b)"),
                                   start=(k_ == 0), stop=(k_ == nt_ - 1), r=[FAt[ci_].dep, S.dep], w=[ps[xb_].dep])
                                k_ += 1
                        banks.append(xb_)
                    return banks

                npair = ng // 2
                stage1(0)
                stage1(1)
                for p in range(npair):
                    if p + 1 < npair:
                        stage1(2 * p + 2)
                        stage1(2 * p + 3)
                    banks = stage2(p)
                    for h in range(nh):
                        consume(2 * p + h, banks[h])
                    if consume_pe is not None and p >= 1:
                        consume_pe(p - 1)
                if consume_pe is not None:
                    consume_pe(npair - 1)

            y2hy_cols = len(out2)
            for cc in range(4):
                ccm = AR.mark()
                Vt = AR.alloc([128, nb], BF16, "Vt")
                X1t = AR.alloc([nb, 128], BF16, "X1t")
                X2t = AR.alloc([nb, 128], BF16, "X2t")
                Z1t = AR.alloc([128, nb], BF16, "Z1t")
                s1m = AR.mark()
                uins = [AR.alloc([ntok], BF16, "uin%d" % i) for i in range(2)]
                uo = AR.alloc([ntok], F32, "uo")
                for idx, dst in enumerate([Vt, X1t, X2t]):
                    rr = idx * 4 + cc
                    ui = uins[idx % 2]
                    if v == 1 and idx == 2:
                        MEMSET("pool", ui.ap[:, 1536:3584], 0.0, w=[ui.dep])
                        DMA(ui.ap[:, 0:1536], Uscr[v][rr * 128:(rr + 1) * 128, 0:1536], r=[Uscr_dep[v]], w=[ui.dep])
                        DMA(ui.ap[:, 3584:4096], Uscr[v][rr * 128:(rr + 1) * 128, 3584:4096], r=[Uscr_dep[v]], w=[ui.dep])
                    else:
                        DMA(ui.ap, Uscr[v][rr * 128:(rr + 1) * 128, :], r=[Uscr_dep[v]], w=[ui.dep])
                    w0c, w1c, w2c = vcol("hy_conv_w", rr), vcol("hy_conv_w", 12 + rr), vcol("hy_conv_w", 24 + rr)
                    TS("dve", uo.ap, ui.ap, w1c, vcol("hy_conv_b", rr), ALU.mult, ALU.add, r=[ui.dep, vec.dep], w=[uo.dep])
                    if v == 0:
                        for s_ in range(4):
                            a0 = s_ * 256
                            STT(uo.ap[:, a0 + 1:a0 + 256], ui.ap[:, a0:a0 + 255], w0c, uo.ap[:, a0 + 1:a0 + 256], ALU.mult, ALU.add,
                                r=[ui.dep, uo.dep, vec.dep], w=[uo.dep])
                            STT(uo.ap[:, a0:a0 + 255], ui.ap[:, a0 + 1:a0 + 256], w2c, uo.ap[:, a0:a0 + 255], ALU.mult, ALU.add,
                                r=[ui.dep, uo.dep, vec.dep], w=[uo.dep])
                    else:
                        STT(uo.ap[:, 1:ntok], ui.ap[:, 0:ntok - 1], w0c, uo.ap[:, 1:ntok], ALU.mult, ALU.add,
                            r=[ui.dep, uo.dep, vec.dep], w=[uo.dep])
                        STT(uo.ap[:, 0:ntok - 1], ui.ap[:, 1:ntok], w2c, uo.ap[:, 0:ntok - 1], ALU.mult, ALU.add,
                            r=[ui.dep, uo.dep, vec.dep], w=[uo.dep])
                        TS("dve", cw02.ap[:, 0:4], seamc.ap, w0c, None, ALU.mult, r=[seamc.dep, vec.dep], w=[cw02.dep])
                        TS("dve", cw02.ap[:, 4:8], seamc.ap, w2c, None, ALU.mult, r=[seamc.dep, vec.dep], w=[cw02.dep])
                        for si in range(4):
                            s_ = si * 1024
                            sm1 = (s_ - 1) % ntok
                            STT(uo.ap[:, s_:s_ + 1], ui.ap[:, sm1:sm1 + 1], cw02.ap[:, si:si + 1], uo.ap[:, s_:s_ + 1], ALU.mult, ALU.add,
                                r=[ui.dep, uo.dep, cw02.dep], w=[uo.dep])
                            STT(uo.ap[:, sm1:sm1 + 1], ui.ap[:, s_:s_ + 1], cw02.ap[:, 4 + si:5 + si], uo.ap[:, sm1:sm1 + 1], ALU.mult, ALU.add,
                                r=[ui.dep, uo.dep, cw02.dep], w=[uo.dep])
                    if idx == 0 and cc == 0:
                        d_ = dbg_out("u%d" % v, [128, ntok])
                        if d_ is not None:
                            DMA(d_, uo.ap, r=[uo.dep])
                    for b4 in range(nb // 4):
                        tb = b4 % 2
                        for bi in range(4):
                            blk = b4 * 4 + bi
                            TR(ps[tb].ap[:, bi * 128:(bi + 1) * 128], uo.ap[:, blk * 128:(blk + 1) * 128], ident_f.ap,
                               r=[uo.dep, ident_f.dep], w=[ps[tb].dep])
                        if idx == 0:
                            CP("act" if b4 % 2 else "dve", dst.ap[:, :, b4 * 4:b4 * 4 + 4],
                               ps[tb].ap.rearrange("p (b c) -> p c b", c=128), r=[ps[tb].dep], w=[dst.dep])
                        else:
                            CP("act" if b4 % 2 else "dve", dst.ap[:, b4 * 4:b4 * 4 + 4, :],
                               ps[tb].ap.rearrange("p (b c) -> p b c", c=128), r=[ps[tb].dep], w=[dst.dep])
                AR.release(s1m)
                for o in range(2):
                    om = AR.mark()
                    Hs = AR.alloc([ng, nh, 2, 128], BF16, "Hs")
                    km = AR.mark()
                    if v == 0:
                        UK4 = AR.alloc([128, 8], BF16, "UK4")
                        MEMSET("pool", UK4.ap[:, :, nks:8], 0.0, w=[UK4.dep])
                        UKa = Buf(UK4.ap[:, :, 0:nfb], "UKa")
                        UKb = Buf(UK4.ap[:, :, nfb:nks], "UKb")
                        UKa.dep = UK4.dep
                        UKb.dep = UK4.dep
                        kch_ = [(UK4, 8)]
                    else:
                        UKa = AR.alloc([128, nfb], BF16, "UKa")
                        UKb = AR.alloc([128, nfb], BF16, "UKb")
                        kch_ = [(UKa, nfb), (UKb, nfb)]
                    hbk = AR.alloc([nfb, 128], BF16, "hbk")
                    wint = AR.alloc([nfb, 128], BF16, "wint")
                    DMA(wint.ap, I["win_" + tsuf].rearrange("(a p) c -> p a c", p=128)[:, :, cc * 128:(cc + 1) * 128], w=[wint.dep])
                    skr = AR.alloc([128], F32, "skr")
                    DMA(skr.ap[0:1, :], I["filt_bias"][o:o + 1, cc * 128:(cc + 1) * 128], w=[skr.dep])
                    for dr in range(2):
                        col0 = (o * 2 + dr) * 512 + cc * 128
                        dstb = UKa if dr == 0 else hbk
                        ngrp = (nfb + 3) // 4
                        for b4 in range(ngrp):
                            nb4 = min(4, nfb - b4 * 4)
                            fb_ = 6 + (b4 % 2)
                            for bi in range(nb4):
                                blk = b4 * 4 + bi
                                MM(ps[fb_].ap[:, bi * 128:(bi + 1) * 128], h2T.ap[0:64, blk * 128:(blk + 1) * 128],
                                   W3b.ap[0:64, col0:col0 + 128], r=[h2T.dep, W3b.dep], w=[ps[fb_].dep])
                            if dr == 0:
                                TT("dve", UKa.ap[:, :, b4 * 4:b4 * 4 + nb4],
                                   ps[fb_].ap[:, 0:nb4 * 128].rearrange("p (b c) -> p c b", c=128),
                                   wint.ap[:, b4 * 4:b4 * 4 + nb4, :].rearrange("p b c -> p c b"),
                                   ALU.mult, r=[ps[fb_].dep, wint.dep], w=[dstb.dep])
                            else:
                                TT("dve", hbk.ap[:, b4 * 4:b4 * 4 + nb4, :],
                                   ps[fb_].ap[:, 0:nb4 * 128].rearrange("p (b c) -> p b c", c=128), wint.ap[:, b4 * 4:b4 * 4 + nb4, :],
                                   ALU.mult, r=[ps[fb_].dep, wint.dep], w=[dstb.dep])
                    TT("dve", UKa.ap[0:1, :, 0], UKa.ap[0:1, :, 0], skr.ap[0:1, :], ALU.add, r=[UKa.dep, skr.dep], w=[UKa.dep])
                    if o == 0 and cc == 0:
                        d_ = dbg_out("hf%d" % v, [128, nfb * 128])
                        if d_ is not None:
                            dtmp = AR.alloc([nfb, 128], F32, "dtmp")
                            CP("dve", dtmp.ap, UKa.ap.rearrange("p c b -> p b c"), r=[UKa.dep], w=[dtmp.dep])
                            DMA(d_, dtmp.ap.rearrange("p b c -> p (b c)"), r=[dtmp.dep])
                    for b4 in range((nfb + 3) // 4):
                        nb4 = min(4, nfb - b4 * 4)
                        fb_ = 6 + (b4 % 2)
                        for bi in range(nb4):
                            ip = b4 * 4 + bi + 1
                            last = (ip == nfb)
                            MM(ps[fb_].ap[:, bi * 128:(bi + 1) * 128], R1T.ap, hbk.ap[:, ip - 1, :], start=True, stop=last,
                               r=[R1T.dep, hbk.dep], w=[ps[fb_].dep])
                            if not last:
                                MM(ps[fb_].ap[:, bi * 128:(bi + 1) * 128], E0T.ap, hbk.ap[:, ip, :], start=False, stop=True,
                                   r=[E0T.dep, hbk.dep], w=[ps[fb_].dep])
                        CP("act", UKb.ap[:, :, b4 * 4:b4 * 4 + nb4],
                           ps[fb_].ap[:, 0:nb4 * 128].rearrange("p (b c) -> p c b", c=128), r=[ps[fb_].dep], w=[UKb.dep])

                    def consume_k(g, xb_):
                        CP("act" if g % 2 else "dve", Hs.ap[:, g].rearrange("p h r b -> p (h r b)"), ps[xb_].ap[:, 0:nh * 256],
                           r=[ps[xb_].dep], w=[Hs.dep])
                    fwd2d(kch_, FAk, consume_k)
                    AR.release(km)
                    src = Vt if o == 0 else Z1t
                    outs = out1 if o == 0 else out2
                    nout = len(outs)
                    npos = 2 * nout
                    GA = GA1 if o == 0 else GA2
                    ncol = 2 * npos * cg
                    Zt = AR.alloc([2, npos, 128], BF16, "Zt")
                    tAB = [AR.alloc([nh, 2, 128], BF16, "tAB%d" % i) for i in range(2)]
                    tCD = [AR.alloc([nh, 2, 128], BF16, "tCD%d" % i) for i in range(2)]
                    Yb = [AR.alloc([nh, 2, 128], BF16, "Yb%d" % i) for i in range(6)]

                    def consume_d(g, xb_):
                        X4 = ps[xb_].ap[:, 0:nh * 256].rearrange("p (h r b) -> p h r b", r=2, b=128)
                        H4 = Hs.ap[:, g]
                        ab, cd, yb = tAB[g % 2], tCD[g % 2], Yb[g % 6]
                        TT("dve", ab.ap, X4, H4, ALU.mult, r=[ps[xb_].dep, Hs.dep], w=[ab.dep])
                        TT("dve", cd.ap[:, :, 0, :], X4[:, :, 0, :], H4[:, :, 1, :], ALU.mult, r=[ps[xb_].dep, Hs.dep], w=[cd.dep])
                        TT("dve", cd.ap[:, :, 1, :], X4[:, :, 1, :], H4[:, :, 0, :], ALU.mult, r=[ps[xb_].dep, Hs.dep], w=[cd.dep])
                        TT("pool", yb.ap[:, :, 0, :], ab.ap[:, :, 0, :], ab.ap[:, :, 1, :], ALU.subtract, r=[ab.dep], w=[yb.dep])
                        TT("pool", yb.ap[:, :, 1, :], cd.ap[:, :, 0, :], cd.ap[:, :, 1, :], ALU.add, r=[cd.dep], w=[yb.dep])

                    def consume_d_pe(p):
                        for gi in range(2):
                            g = 2 * p + gi
                            zb = 4 + gi
                            k_ = 0
                            for h in range(nh):
                                yb = Yb[(2 * p + h) % 6]
                                for ri in range(2):
                                    MM(ps[zb].ap[:, 0:ncol], yb.ap[:, gi, ri, :], GA.ap[:, h, ri, :], start=(k_ == 0), stop=(k_ == 2 * nh - 1),
                                       r=[yb.dep, GA.dep], w=[ps[zb].dep])
                                    k_ += 1
                            CP("act", Zt.ap[:, :, :, cg * g:cg * (g + 1)], ps[zb].ap[:, 0:ncol].rearrange("p (r n c) -> p r n c", r=2, c=cg),
                               r=[ps[zb].dep], w=[Zt.dep])
                    fwd2d([(src, nb)], [FAd], consume_d, consume_d_pe)
                    m0 = 0
                    bi_ = 0
                    while m0 < nout:
                        mn = min(4, nout - m0)
                        yb_ = 6 + (bi_ % 2)
                        bi_ += 1
                        ops_ = [(0, 0, m0), (1, 1, m0), (2, 0, nout + m0), (3, 1, nout + m0)]
                        for k_, (ci_, ri, n0) in enumerate(ops_):
                            MM(ps[yb_].ap[:, 0:mn * 128], CBt.ap[:, ci_, :], Zt.ap[:, ri, n0:n0 + mn, :], start=(k_ == 0), stop=(k_ == 3),
                               r=[CBt.dep, Zt.dep], w=[ps[yb_].dep])
                        yv = ps[yb_].ap[:, 0:mn * 128].rearrange("p (m c) -> p m c", c=128)
                        if o == 0:
                            TT("dve", Z1t.ap[:, :, m0:m0 + mn], yv.rearrange("p m c -> p c m"),
                               X1t.ap[:, m0:m0 + mn, :].rearrange("p m c -> p c m"), ALU.mult,
                               r=[ps[yb_].dep, X1t.dep], w=[Z1t.dep])
                        else:
                            mm_ = 0
                            while mm_ < mn:
                                sl = outs[m0 + mm_]
                                run = 1
                                while mm_ + run < mn and outs[m0 + mm_ + run] == sl + run:
                                    run += 1
                                TT("dve", y2hy.ap[:, m0 + mm_:m0 + mm_ + run, cc * 128:(cc + 1) * 128], yv[:, mm_:mm_ + run, :],
                                   X2t.ap[:, sl:sl + run, :], ALU.mult, r=[ps[yb_].dep, X2t.dep], w=[y2hy.dep])
                                mm_ += run
                        m0 += mn
                    AR.release(om)
                    if o == 0 and cc == 0:
                        d_ = dbg_out("z1_%d" % v, [128, nb * 128])
                        if d_ is not None:
                            dtmp2 = AR.alloc([nb, 128], F32, "dtmp2")
                            CP("dve", dtmp2.ap, Z1t.ap.rearrange("p c b -> p b c"), r=[Z1t.dep], w=[dtmp2.dep])
                            DMA(d_, dtmp2.ap.rearrange("p b c -> p (b c)"), r=[dtmp2.dep])
                AR.release(ccm)
            AR.release(hm)
            d_ = dbg_out("hyo%d" % v, [128, len(out2) * 512])
            if d_ is not None:
                DMA(d_, y2hy.ap.rearrange("p t f -> p (t f)"), r=[y2hy.dep])

            mgm = AR.mark()
            Wout = AR.alloc([KC, D], BF16, "Wout")
            g1bc = AR.alloc([D], F32, "g1bc")
            DMA(g1bc.ap, modD_flat[(v * 48 + 16) * 128:(v * 48 + 24) * 128].partition_broadcast(128), r=[modD_dep], w=[g1bc.dep])
            wout_d = I["w_out"].rearrange("(k p) n -> p k n", p=128)
            for k in range(KC):
                DMA(Wout.ap[:, k, :], wout_d[:, k, :], w=[Wout.dep], q="pool")
            gtmp = [AR.alloc([512], F32, "gtmp%d" % i) for i in range(2)]
            mixb = [AR.alloc([D], BF16, "mix%d" % i) for i in range(2)]
            mixT = [AR.alloc([KC, 128], BF16, "mixT%d" % i) for i in range(2)]
            xr = [AR.alloc([D], F32, "xr%d" % i) for i in range(2)]
            x1t = [AR.alloc([D], F32, "x1t%d" % i) for i in range(2)]
            hyh = AR.alloc([512], F32, "hyh")
            junk3 = AR.alloc([512], F32, "junk3")
            st3 = AR.alloc([8], F32, "st3")
            def merge_a(qi):
                halo = (v == 1 and qi == 8)
                m_ = 2 if halo else 128
                mx = mixb[qi % 2]
                ACTF(junk3.ap[0:m_, :], O_all.ap[0:m_, qi, :], AF.Square, r=[O_all.dep], w=[junk3.dep, st3.dep],
                     accum_out=st3.ap[0:m_, 0:1])
                ACTF(st3.ap[0:m_, 1:2], st3.ap[0:m_, 0:1], AF.Sqrt, r=[st3.dep, epsb.dep], w=[st3.dep], scale=1.0 / 512,
                     bias=epsb.ap[0:m_, 0:1])
                RECIP(st3.ap[0:m_, 2:3], st3.ap[0:m_, 1:2], r=[st3.dep], w=[st3.dep])
                STT(mx.ap[0:m_, 0:512], O_all.ap[0:m_, qi, :], st3.ap[0:m_, 2:3], gatt_bc.ap[0:m_, :], ALU.mult, ALU.mult,
                    r=[O_all.dep, st3.dep, gatt_bc.dep], w=[mx.dep])
                if halo:
                    MM(ps[7].ap[0:2, :], sel.ap[:, 0:2], y2hy.ap[:, 8, :], start=True, stop=False, r=[sel.dep, y2hy.dep], w=[ps[7].dep])
                    MM(ps[7].ap[0:2, :], sel.ap[:, 2:4], y2hy.ap[:, 9, :], start=False, stop=True, r=[sel.dep, y2hy.dep], w=[ps[7].dep])
                    CP("act", hyh.ap[0:2, :], ps[7].ap[0:2, :], r=[ps[7].dep], w=[hyh.dep])
                    hsrc, hdep = hyh.ap[0:2, :], hyh.dep
                else:
                    hsrc, hdep = y2hy.ap[0:m_, qi, :], y2hy.dep
                ACTF(junk3.ap[0:m_, :], hsrc, AF.Square, r=[hdep], w=[junk3.dep, st3.dep], accum_out=st3.ap[0:m_, 3:4])
                ACTF(st3.ap[0:m_, 4:5], st3.ap[0:m_, 3:4], AF.Sqrt, r=[st3.dep, epsb.dep], w=[st3.dep], scale=1.0 / 512,
                     bias=epsb.ap[0:m_, 0:1])
                RECIP(st3.ap[0:m_, 5:6], st3.ap[0:m_, 4:5], r=[st3.dep], w=[st3.dep])
                STT(mx.ap[0:m_, 512:1024], hsrc, st3.ap[0:m_, 5:6], ghy_bc.ap[0:m_, :], ALU.mult, ALU.mult,
                    r=[hdep, st3.dep, ghy_bc.dep], w=[mx.dep])
                xrt = xr[qi % 2]
                if halo:
                    DMA(xrt.ap[0:1, :], xin[4095:4096, :], w=[xrt.dep])
                    DMA(xrt.ap[1:2, :], xin[1024:1025, :], w=[xrt.dep])
                else:
                    DMA(xrt.ap, xin[qi * 128:(qi + 1) * 128, :], w=[xrt.dep])

            def merge_b(qi):
                halo = (v == 1 and qi == 8)
                m_ = 2 if halo else 128
                mx = mixb[qi % 2]
                tb = qi % 2
                mT = mixT[qi % 2]
                for k in range(KC):
                    TR(psb(tb)[:, k * 128:k * 128 + m_], mx.ap[0:m_, k * 128:(k + 1) * 128], ident_b.ap[0:m_, 0:m_],
                       r=[mx.dep, ident_b.dep], w=[ps[tb].dep])
                CP("act", mT.ap[:, :, 0:m_], psb(tb).rearrange("p (k t) -> p k t", t=128)[:, :, 0:m_], r=[ps[tb].dep], w=[mT.dep])
                xrt = xr[qi % 2]
                xo = x1t[qi % 2]
                for half in range(2):
                    ob = 2 + 2 * (qi % 2) + half
                    for k in range(KC):
                        MM(ps[ob].ap[0:m_, :], mT.ap[:, k, 0:m_], Wout.ap[:, k, half * 512:(half + 1) * 512], start=(k == 0),
                           stop=(k == KC - 1), r=[mT.dep, Wout.dep], w=[ps[ob].dep])
                    gt = gtmp[half]
                    TT("dve", gt.ap[0:m_, :], ps[ob].ap[0:m_, :], g1bc.ap[0:m_, half * 512:(half + 1) * 512], ALU.mult,
                       r=[ps[ob].dep, g1bc.dep], w=[gt.dep])
                    TT("pool", xo.ap[0:m_, half * 512:(half + 1) * 512], gt.ap[0:m_, :], xrt.ap[0:m_, half * 512:(half + 1) * 512],
                       ALU.add, r=[gt.dep, xrt.dep], w=[xo.dep])
                r0 = 1024 if halo else qi * 128
                DMA(x1scr[v][r0:r0 + m_, :], xo.ap[0:m_, :], r=[xo.dep], w=[x1scr_dep[v]], q="pool")

            merge_a(0)
            for qi in range(nqt):
                if qi + 1 < nqt:
                    merge_a(qi + 1)
                merge_b(qi)
            AR.release(gm)

        def load_ffn_weights():
            Wup = AR.alloc([KC, 2 * DFF], BF16, "Wup")
            Wdn = AR.alloc([22, D], BF16, "Wdn")
            wup_d = I["w_up"].rearrange("(k p) n -> p k n", p=128)
            for k in range(KC):
                for hh in range(2):
                    DMA(Wup.ap[:, k, hh * DFF:(hh + 1) * DFF], wup_d[:, k, hh * DFF:(hh + 1) * DFF], w=[Wup.dep], q="pool")
            wdn_d = I["w_down"].rearrange("(i p) n -> p i n", p=128)
            for i in range(0, 22, 2):
                DMA(Wdn.ap[:, i:i + 2, :], wdn_d[:, i:i + 2, :], w=[Wdn.dep], q="pool")
            return Wup, Wdn

        def run_ffn(v, Wup, Wdn):
            nqt = 8 if v == 0 else 9
            gm = AR.mark()
            ncolT = 1024 if v == 0 else 1026
            h2Tt = AR.alloc([KC, ncolT], BF16, "h2Tt")
            g2bc = AR.alloc([D], F32, "g2bc")
            DMA(g2bc.ap, modD_flat[(v * 48 + 40) * 128:(v * 48 + 48) * 128].partition_broadcast(128), r=[modD_dep], w=[g2bc.dep])
            n2m = AR.mark()
            xts2 = [AR.alloc([D], F32, "x2t%d" % i) for i in range(2)]
            xns2 = [AR.alloc([D], BF16, "x2n%d" % i) for i in range(2)]
            junk4 = AR.alloc([D], F32, "junk4")
            st4 = AR.alloc([4], F32, "st4")

            def n2_pre(qi):
                halo = (v == 1 and qi == 8)
                m_ = 2 if halo else 128
                r0 = 1024 if halo else qi * 128
                xt = xts2[qi % 2]
                xn = xns2[qi % 2]
                DMA(xt.ap[0:m_, :], x1scr[v][r0:r0 + m_, :], r=[x1scr_dep[v]], w=[xt.dep])
                ACTF(junk4.ap[0:m_, :], xt.ap[0:m_, :], AF.Square, r=[xt.dep], w=[junk4.dep, st4.dep], accum_out=st4.ap[0:m_, 0:1])
                ACTF(st4.ap[0:m_, 1:2], st4.ap[0:m_, 0:1], AF.Sqrt, r=[st4.dep, epsb.dep], w=[st4.dep], scale=1.0 / D,
                     bias=epsb.ap[0:m_, 0:1])
                RECIP(st4.ap[0:m_, 2:3], st4.ap[0:m_, 1:2], r=[st4.dep], w=[st4.dep])
                TS("dve", xn.ap[0:m_, :], xt.ap[0:m_, :], st4.ap[0:m_, 2:3], None, ALU.mult, r=[xt.dep, st4.dep], w=[xn.dep])

            def n2_mid(qi):
                halo = (v == 1 and qi == 8)
                m_ = 2 if halo else 128
                xn = xns2[qi % 2]
                tb = qi % 2
                for k in range(KC):
                    TR(psb(tb)[:, k * 128:k * 128 + m_], xn.ap[0:m_, k * 128:(k + 1) * 128], ident_b.ap[0:m_, 0:m_],
                       r=[xn.dep, ident_b.dep], w=[ps[tb].dep])
                pv3 = psb(tb).rearrange("p (k t) -> p k t", t=128)
                for k in range(KC):
                    if halo:
                        pairs = [(h2Tt.ap[:, k, 0:1], pv3[:, k, 0:1]), (h2Tt.ap[:, k, 1025:1026], pv3[:, k, 1:2])]
                    elif v == 0:
                        pairs = [(h2Tt.ap[:, k, qi * 128:(qi + 1) * 128], pv3[:, k, :])]
                    else:
                        pairs = [(h2Tt.ap[:, k, 1 + qi * 128:1 + (qi + 1) * 128], pv3[:, k, :])]
                    for (o_, i_) in pairs:
                        if qi % 2 == 0:
                            TS("dve", o_, i_, A2.ap[:, k, v:v + 1], mod.ap[:, v, 24 + k:25 + k], ALU.mult, ALU.add,
                               r=[ps[tb].dep, A2.dep, mod.dep], w=[h2Tt.dep])
                        else:
                            ACTF(o_, i_, AF.Identity, r=[ps[tb].dep, A2.dep, mod.dep], w=[h2Tt.dep], scale=A2.ap[:, k, v:v + 1],
                                 bias=mod.ap[:, v, 24 + k:25 + k])

            n2_pre(0)
            for qi in range(nqt):
                if qi + 1 < nqt:
                    n2_pre(qi + 1)
                n2_mid(qi)
            AR.release(n2m)
            actT = AR.alloc([22, 256], BF16, "actT")
            wmk = AR.alloc([2], F32, "wmk")
            cab = [AR.alloc([256], F32, "cab%d" % i) for i in range(4)]
            sab = [AR.alloc([256], F32, "sab%d" % i) for i in range(2)]
            x1w = [AR.alloc([D], F32, "x1w%d" % i) for i in range(2)]
            yw = [AR.alloc([D], F32, "yw%d" % i) for i in range(2)]
            gtmp2 = [AR.alloc([512], F32, "gtmp2_%d" % i) for i in range(2)]
            yout = O["y_prompt"] if v == 0 else O["y_sample"]
            cnt_ = 0
            for s_ in range(4):
                c0 = 256 * s_
                ncol = 256 if v == 0 else 258
                for i in range(22):
                    cas = []
                    for part in range(2):
                        ch = i + 22 * part
                        ub = cnt_ % 4
                        cnt_ += 1
                        for k in range(KC):
                            MM(ps[ub].ap[:, 0:ncol], Wup.ap[:, k, ch * 128:(ch + 1) * 128], h2Tt.ap[:, k, c0:c0 + ncol], start=(k == 0),
                               stop=(k == KC - 1), r=[Wup.dep, h2Tt.dep], w=[ps[ub].dep])
                        ca = cab[ub]
                        pu = ps[ub]
                        w0c, w1c, w2c = vcol("ffn_conv_w", ch), vcol("ffn_conv_w", NFF + ch), vcol("ffn_conv_w", 2 * NFF + ch)
                        bc_ = vcol("ffn_conv_b", ch)
                        if v == 0:
                            ACTF(ca.ap, pu.ap[:, 0:256], AF.Identity, r=[pu.dep, vec.dep], w=[ca.dep], scale=w1c, bias=bc_)
                            STT(ca.ap[:, 1:256], pu.ap[:, 0:255], w0c, ca.ap[:, 1:256], ALU.mult, ALU.add, r=[pu.dep, ca.dep, vec.dep], w=[ca.dep])
                            STT(ca.ap[:, 0:255], pu.ap[:, 1:256], w2c, ca.ap[:, 0:255], ALU.mult, ALU.add, r=[pu.dep, ca.dep, vec.dep], w=[ca.dep])
                        else:
                            ACTF(ca.ap, pu.ap[:, 1:257], AF.Identity, r=[pu.dep, vec.dep], w=[ca.dep], scale=w1c, bias=bc_)
                            lo_ = 1 if s_ == 0 else 0
                            hi_ = 255 if s_ == 3 else 256
                            STT(ca.ap[:, lo_:256], pu.ap[:, lo_:256], w0c, ca.ap[:, lo_:256], ALU.mult, ALU.add,
                                r=[pu.dep, ca.dep, vec.dep], w=[ca.dep])
                            STT(ca.ap[:, 0:hi_], pu.ap[:, 2:2 + hi_], w2c, ca.ap[:, 0:hi_], ALU.mult, ALU.add,
                                r=[pu.dep, ca.dep, vec.dep], w=[ca.dep])
                            if s_ == 0:
                                TS("dve", wmk.ap[:, 0:1], halomask.ap[:, 0:1], w0c, None, ALU.mult, r=[halomask.dep, vec.dep], w=[wmk.dep])
                                STT(ca.ap[:, 0:1], pu.ap[:, 0:1], wmk.ap[:, 0:1], ca.ap[:, 0:1], ALU.mult, ALU.add,
                                    r=[pu.dep, ca.dep, wmk.dep], w=[ca.dep])
                            if s_ == 3:
                                TS("dve", wmk.ap[:, 1:2], halomask.ap[:, 1:2], w2c, None, ALU.mult, r=[halomask.dep, vec.dep], w=[wmk.dep])
                                STT(ca.ap[:, 255:256], pu.ap[:, 257:258], wmk.ap[:, 1:2], ca.ap[:, 255:256], ALU.mult, ALU.add,
                                    r=[pu.dep, ca.dep, wmk.dep], w=[ca.dep])
                        cas.append(ca)
                    sa = sab[i % 2]
                    ACTF(sa.ap, cas[0].ap, AF.Silu, r=[cas[0].dep], w=[sa.dep])
                    TT("dve", actT.ap[:, i, :], sa.ap, cas[1].ap, ALU.mult, r=[sa.dep, cas[1].dep], w=[actT.dep])
                for tl in range(2):
                    ti = s_ * 2 + tl
                    xw = x1w[ti % 2]
                    yo = yw[ti % 2]
                    DMA(xw.ap, x1scr[v][ti * 128:(ti + 1) * 128, :], r=[x1scr_dep[v]], w=[xw.dep])
                    for half in range(2):
                        db = 4 + tl * 2 + half
                        for i in range(22):
                            MM(ps[db].ap, actT.ap[:, i, tl * 128:(tl + 1) * 128], Wdn.ap[:, i, half * 512:(half + 1) * 512], start=(i == 0),
                               stop=(i == 21), r=[actT.dep, Wdn.dep], w=[ps[db].dep])
                        gt2 = gtmp2[half]
                        TT("dve", gt2.ap, ps[db].ap, g2bc.ap[:, half * 512:(half + 1) * 512], ALU.mult,
                           r=[ps[db].dep, g2bc.dep], w=[gt2.dep])
                        TT("pool", yo.ap[:, half * 512:(half + 1) * 512], gt2.ap, xw.ap[:, half * 512:(half + 1) * 512], ALU.add,
                           r=[gt2.dep, xw.dep], w=[yo.dep])
                    DMA(yout[ti * 128:(ti + 1) * 128, :], yo.ap, r=[yo.dep], q="pool")
            AR.release(gm)

        for v_ in GROUPS:
            run_group(v_)
        Wup_, Wdn_ = load_ffn_weights()
        for v_ in GROUPS:
            run_ffn(v_, Wup_, Wdn_)
        P.emit()
    return nc, list(DBG.keys())


_CACHE = {}


def make_in_maps(inputs):
    f32 = lambda a: np.ascontiguousarray(np.asarray(a, dtype=np.float32))
    xpr = f32(inputs["x_prompt"])
    xsm = f32(inputs["x_sample"])
    maps = []
    for core in range(8):
        b, j = core // 4, core % 4
        m = {}
        m["xp"] = xpr[4 * core:4 * core + 4].reshape(TP, D)
        m["xs"] = np.ascontiguousarray(np.roll(xsm[b], -1024 * j, axis=0))
        m["ckv_c"] = f32(inputs["cache_ckv"])[b, 0]
        m["kpe_c"] = f32(inputs["cache_kpe"])[b, 0]
        m["cvec"] = np.ascontiguousarray(np.stack([f32(inputs["c_ctx"]), f32(inputs["c"])[b]], axis=1))
        for nm in ["w_ada", "b_ada", "g_mix", "w_in", "g_qa", "w_uq", "g_kva", "w_ukv", "g_q", "g_k", "hy_conv_w",
                   "hy_conv_b", "filt_w1", "filt_b1", "filt_freq1", "filt_w2", "filt_b2", "filt_freq2", "filt_w3",
                   "filt_bias", "g_out_att", "g_out_hy", "w_out", "g_ffn", "w_up", "ffn_conv_w", "ffn_conv_b", "w_down"]:
            a = f32(inputs[nm])[0]
            if nm in ("hy_conv_w", "ffn_conv_w"):
                a = a.reshape(-1)
            m[nm] = np.ascontiguousarray(a)
        m.update(host_tables(core))
        maps.append(m)
    return maps


def kernel(**inputs):
    if "nc" not in _CACHE:
        _CACHE["nc"] = build_program()[0]
    nc = _CACHE["nc"]
    maps = make_in_maps(inputs)
    res = run_bass_kernel_spmd(nc, maps, core_ids=list(range(8)))
    R = res.results
    y_prompt = np.concatenate([R[c]["y_prompt"].reshape(4, 256, D) for c in range(8)], axis=0)
    y_sample = np.stack([np.concatenate([R[b * 4 + j]["y_sample"] for j in range(4)], axis=0) for b in range(2)], axis=0)
    new_ckv = np.concatenate([R[c]["new_ckv"].reshape(4, 1, 256, KVL) for c in range(8)], axis=0)
    new_kpe = np.concatenate([R[c]["new_kpe"].reshape(4, 1, 256, ROPE) for c in range(8)], axis=0)
    return (y_prompt.astype(np.float32), y_sample.astype(np.float32), new_ckv.astype(np.float32), new_kpe.astype(np.float32))
```

```python
import math
from contextlib import ExitStack

import numpy as np
import ml_dtypes

import concourse.bass as bass
import concourse.mybir as mybir
from concourse.bass_utils import run_bass_kernel_spmd

F32 = mybir.dt.float32
BF16 = mybir.dt.bfloat16
AF = mybir.ActivationFunctionType
ALU = mybir.AluOpType
AX = mybir.AxisListType
NPBF = ml_dtypes.bfloat16

D = 1024
KC = 8
EPS = 1e-6
N_HEADS = 8
QK_NOPE = 64
ROPE = 32
QK = 96
VD = 64
QL = 384
KVL = 256
DHY = 512
DIN = 2208
DFF = 2816
NFF = 44
TP = 1024
TL = 4096
PAST = 512


class Dep:
    __slots__ = ("name", "w", "r", "excl")

    def __init__(self, name="", excl=False):
        self.name = name
        self.w = None
        self.r = {}
        self.excl = excl


class Prog:
    ENG = ("pe", "act", "dve", "pool", "sp")

    def __init__(self, nc, ndma_sems=8):
        self.nc = nc
        self.q = {e: [] for e in self.ENG}
        self.cnt = {e: 0 for e in self.ENG}
        self.known = {e: {} for e in self.ENG}
        self.sems = {}
        self.dma_ring = {}
        self.ndma_sems = ndma_sems

    def setup(self, stack):
        nc = self.nc
        for e in self.ENG:
            self.sems[e] = stack.enter_context(nc.semaphore("s_" + e))
        for e in ("sp", "act", "pool"):
            ring = [stack.enter_context(nc.semaphore("d_%s%d" % (e, i))) for i in range(self.ndma_sems)]
            self.dma_ring[e] = {"sems": ring, "vals": [0] * len(ring), "next": 0}

    def _need(self, eng, ev):
        sem, val, src = ev
        if src == eng and eng == "pe":
            return
        k = self.known[eng]
        if k.get(id(sem), 0) >= val:
            return
        k[id(sem)] = val
        self.q[eng].append(("wait", sem, val))

    def _deps(self, eng, reads, writes):
        for d in reads:
            if d.w is not None:
                self._need(eng, d.w)
        for d in writes:
            if d.w is not None:
                self._need(eng, d.w)
            for ev in d.r.values():
                self._need(eng, ev)

    def _commit(self, ev, reads, writes):
        for d in reads:
            old = d.r.get(id(ev[0]))
            if old is None or old[1] < ev[1]:
                d.r[id(ev[0])] = ev
        for d in writes:
            d.w = ev
            d.r = {}

    def op(self, eng, fn, reads=(), writes=()):
        for d in reads:
            if d.excl:
                for ev in d.r.values():
                    if ev[2] != eng:
                        self._need(eng, ev)
        self._deps(eng, reads, writes)
        self.cnt[eng] += 1
        ev = (self.sems[eng], self.cnt[eng], eng)
        self.q[eng].append(("op", fn, self.sems[eng], 1))
        self._commit(ev, reads, writes)
        return ev

    def dma(self, eng, fn, reads=(), writes=()):
        ring = self.dma_ring[eng]
        i = ring["next"]
        ring["next"] = (i + 1) % len(ring["sems"])
        sem = ring["sems"][i]
        if ring["vals"][i] > 0:
            self._need(eng, (sem, ring["vals"][i], "dma"))
        self._deps(eng, reads, writes)
        ring["vals"][i] += 16
        ev = (sem, ring["vals"][i], "dma")
        self.q[eng].append(("op", fn, sem, 16))
        self._commit(ev, reads, writes)
        return ev

    def barrier(self):
        for e in self.ENG:
            for e2 in self.ENG:
                if e2 != e and self.cnt[e2] > 0:
                    self._need(e, (self.sems[e2], self.cnt[e2], e2))
            for q in ("sp", "act", "pool"):
                ring = self.dma_ring[q]
                for sem, v in zip(ring["sems"], ring["vals"]):
                    if v > 0:
                        self._need(e, (sem, v, "dma"))

    def emit(self):
        nc = self.nc
        for e in ("sp", "act", "pool"):
            ring = self.dma_ring[e]
            for sem, v in zip(ring["sems"], ring["vals"]):
                if v > 0:
                    self._need("sp", (sem, v, "dma"))
        for e in ("pe", "act", "dve", "pool"):
            if self.cnt[e] > 0:
                self._need("sp", (self.sems[e], self.cnt[e], e))
        handles = {"pe": "tensor", "act": "scalar", "dve": "vector", "pool": "gpsimd", "sp": "sync"}
        with nc.Block() as block:
            for e in self.ENG:
                items = self.q[e]

                def body(engh, items=items):
                    for it in items:
                        if it[0] == "wait":
                            engh.wait_ge(it[1], it[2])
                        else:
                            it[1](engh).then_inc(it[2], it[3])
                getattr(block, handles[e])(body)


class Buf:
    def __init__(self, ap, name=""):
        self.ap = ap
        self.dep = Dep(name)

    def __getitem__(self, key):
        return self.ap[key]


class Arena:
    def __init__(self, nc, stack, name, nbytes):
        self.t = stack.enter_context(nc.sbuf_tensor(name, [128, nbytes // 4], F32))
        self.size = nbytes
        self.off = 0
        self.peak = 0
        self.hist = []

    def alloc(self, free_shape, dtype, name=""):
        n = 1
        for s in free_shape:
            n *= s
        esz = 4 if dtype == F32 else 2
        nb = (n * esz + 31) // 32 * 32
        assert self.off + nb <= self.size, "arena overflow %s %d+%d>%d" % (name, self.off, nb, self.size)
        a = self.t[:, self.off // 4:(self.off + nb) // 4]
        if dtype != F32:
            a = a.bitcast(dtype)
        a = a[:, 0:n]
        if len(free_shape) == 2:
            a = a.rearrange("p (a b) -> p a b", b=free_shape[1])
        elif len(free_shape) == 3:
            a = a.rearrange("p (a b c) -> p a b c", b=free_shape[1], c=free_shape[2])
        elif len(free_shape) == 4:
            a = a.rearrange("p (a b c d) -> p a b c d", b=free_shape[1], c=free_shape[2], d=free_shape[3])
        nbuf = Buf(a, name)
        lo, hi = self.off, self.off + nb
        for (o, e_, b) in self.hist:
            if o < hi and lo < e_:
                evs = list(b.dep.r.values())
                if b.dep.w is not None:
                    evs.append(b.dep.w)
                for ev in evs:
                    old = nbuf.dep.r.get(id(ev[0]))
                    if old is None or old[1] < ev[1]:
                        nbuf.dep.r[id(ev[0])] = ev
        self.hist.append((lo, hi, nbuf))
        self.off += nb
        self.peak = max(self.peak, self.off)
        return nbuf

    def mark(self):
        return self.off

    def release(self, m):
        self.off = m


def _rope_tables(ntok_positions):
    pos = np.asarray(ntok_positions)
    row = (pos // 64).astype(np.float32)
    col = (pos % 64).astype(np.float32)
    quarter = ROPE // 4
    freqs = (10000.0 ** (-np.arange(quarter, dtype=np.float32) / quarter)).astype(np.float32)
    ang_r = row[:, None] * freqs
    ang_c = col[:, None] * freqs
    ang = np.concatenate([ang_r, ang_r, ang_c, ang_c], axis=-1).astype(np.float32)
    return np.cos(ang).T.astype(np.float32), np.sin(ang).T.astype(np.float32)


def _rot32():
    R = np.zeros((32, 32), np.float32)
    for i in range(8):
        R[i, 8 + i] = -1.0
        R[8 + i, i] = 1.0
        R[16 + i, 24 + i] = -1.0
        R[24 + i, 16 + i] = 1.0
    return R


def _fa_table(pos, cg, na):
    ns = len(pos)
    at = 128 // cg
    nh = na * cg // 128
    T = np.zeros((ns * cg, nh, 3, 128), np.float64)
    a_t = np.arange(at)
    for s_, p in enumerate(pos):
        for h in range(nh):
            phi = 2 * np.pi * p * (h * at + a_t) / float(na)
            for c in range(cg):
                T[c * ns + s_, h, 0, a_t * cg + c] = np.cos(phi)
                T[c * ns + s_, h, 1, a_t * cg + c] = -np.sin(phi)
                T[c * ns + s_, h, 2, a_t * cg + c] = np.sin(phi)
    return T.astype(NPBF)


def _ga_table(posout, cg, na):
    npos = len(posout)
    at = 128 // cg
    nh = na * cg // 128
    G = np.zeros((128, nh, 2, 2 * npos * cg), np.float64)
    for h in range(nh):
        for a_t in range(at):
            alpha = h * at + a_t
            for n, p in enumerate(posout):
                psi = 2 * np.pi * p * alpha / float(na)
                cr, ci = np.cos(psi), np.sin(psi)
                for c in range(cg):
                    row = a_t * cg + c
                    cre = n * cg + c
                    cim = npos * cg + n * cg + c
                    G[row, h, 0, cre] = cr
                    G[row, h, 0, cim] = ci
                    G[row, h, 1, cre] = -ci
                    G[row, h, 1, cim] = cr
    return G.astype(NPBF)


def _filter_feats(n):
    t = np.linspace(0.0, 1.0, n, dtype=np.float32)[:, None]
    w = (2.0 * math.pi * np.arange(n, dtype=np.float32)[:, None] / n).astype(np.float32)
    bands = np.linspace(1e-4, 15.0, 16, dtype=np.float32)
    z = np.concatenate([t, np.cos(bands * w), -np.sin(bands * w)], axis=-1).astype(np.float32)
    max_decay = math.log(1e-2) / 0.3
    min_decay = math.log(1e-2) / 1.5
    deltas = np.abs(np.linspace(min_decay, max_decay, 512, dtype=np.float32))
    window = (np.exp(-t * deltas) + 0.05).astype(np.float32)
    return np.ascontiguousarray(z.T), window.astype(NPBF)


def _conv_tables(j):
    t = {}
    b = np.arange(128, dtype=np.float64)[:, None]
    be = np.arange(128, dtype=np.float64)[None, :]
    th = 2 * np.pi * b * (be + 0.5) / 256.0
    t["Fb"] = np.concatenate([np.cos(th), -np.sin(th), -np.sin(th), -np.cos(th)], axis=1).astype(NPBF)
    beK = np.arange(128, dtype=np.float64)[:, None]
    bp = np.arange(128, dtype=np.float64)[None, :]
    om_lo = 2 * np.pi * bp * (beK + 0.5) / 256.0
    om_hi = 2 * np.pi * (bp + 128) * (beK + 0.5) / 256.0
    for nm_, s in (("CB_p", 2.0 ** -11), ("CB_l", 2.0 ** -13)):
        t[nm_] = np.stack([s * np.cos(om_lo), -s * np.sin(om_lo), s * np.cos(om_hi), -s * np.sin(om_hi)], axis=1).astype(NPBF)
    r1 = np.zeros((128, 128), np.float32)
    for i in range(1, 128):
        r1[i, 128 - i] = 1.0
    e0 = np.zeros((128, 128), np.float32)
    e0[0, 0] = 1.0
    t["R1T"] = r1.astype(NPBF)
    t["E0T"] = e0.astype(NPBF)
    pos_p = [4 * (s_ // 2) + (s_ % 2) for s_ in range(8)]
    t["FAd_p"] = _fa_table(pos_p, 16, 16)
    t["FAk_p"] = _fa_table([0, 1, 15, 14, 0, 0, 0, 0], 16, 16)[None]
    po = pos_p + [(p - 1) % 16 for p in pos_p]
    t["GA1_p"] = _ga_table(po, 16, 16)
    t["GA2_p"] = _ga_table(po, 16, 16)
    pos_l = [(s_ + 8 * j) % 32 for s_ in range(32)]
    t["FAd_l"] = _fa_table(pos_l, 4, 64)
    t["FAk_l"] = np.stack([_fa_table(list(range(32)), 4, 64), _fa_table([64 - i for i in range(1, 33)], 4, 64)], axis=0)
    t["GA1_l"] = _ga_table(pos_l + [(p - 1) % 64 for p in pos_l], 4, 64)
    out2 = list(range(8)) + [31, 8]
    p2 = [pos_l[s_] for s_ in out2]
    t["GA2_l"] = _ga_table(p2 + [(p - 1) % 64 for p in p2], 4, 64)
    t["zT_p"], t["win_p"] = _filter_feats(256)
    t["zT_l"], t["win_l"] = _filter_feats(4096)
    seam = (4096 - 1024 * j) % 4096
    co = np.zeros(4, np.float32)
    for si in range(4):
        flag = 1.0 if si * 1024 == seam else 0.0
        co[si] = (1.0 - flag) if si == 0 else -flag
    t["seamc"] = np.tile(co[None, :], (128, 1)).astype(np.float32)
    hm = np.array([0.0 if j == 0 else 1.0, 0.0 if j == 3 else 1.0], np.float32)
    t["halomask"] = np.tile(hm[None, :], (128, 1)).astype(np.float32)
    return t


def host_tables(core):
    j = core % 4
    t = {}
    t["ident_f"] = np.eye(128, dtype=np.float32)
    t["ident_b"] = np.eye(128, dtype=np.float32).astype(NPBF)
    R = _rot32()
    sh = np.zeros((32, 96), np.float32)
    shr = np.zeros((32, 96), np.float32)
    r96 = np.zeros((96, 96), np.float32)
    for k in range(32):
        sh[k, 64 + k] = 1.0
        for m in range(32):
            shr[k, 64 + m] = R[m, k]
            r96[64 + k, 64 + m] = R[m, k]
    t["shiftT"] = sh.astype(NPBF)
    t["shiftrotT"] = shr.astype(NPBF)
    t["rot96T"] = r96.astype(NPBF)
    st_tok = (np.arange(TL) + 1024 * j) % TL
    ck, sk = _rope_tables(st_tok)
    t["cosk"] = np.concatenate([ck, np.ones((32, PAST), np.float32)], axis=1)
    t["sink"] = np.concatenate([sk, np.zeros((32, PAST), np.float32)], axis=1)
    qtok = np.concatenate([1024 * j + np.arange(1024), [(1024 * j - 1) % TL, (1024 * j + 1024) % TL]])
    cq, sq = _rope_tables(qtok)
    t["cosq"] = np.concatenate([np.ones((64, 1026), np.float32), cq], axis=0)
    t["sinq"] = np.concatenate([np.zeros((64, 1026), np.float32), sq], axis=0)
    t.update(_conv_tables(j))
    return t


VEC_LAYOUT = [("b_ada", 48), ("g_mix", 8), ("g_ffn", 8), ("g_qa", 3), ("hy_conv_w", 36), ("hy_conv_b", 12),
              ("ffn_conv_w", 132), ("ffn_conv_b", 44)]


def build_program(debug=(), GROUPS=(0, 1)):
    nc = bass.Bass("TRN2", target_bir_lowering=False)
    P = Prog(nc)

    def din(name, shape, dt=F32):
        return nc.dram_tensor(name, list(shape), dt, kind="ExternalInput").ap()

    def dout(name, shape, dt=F32):
        return nc.dram_tensor(name, list(shape), dt, kind="ExternalOutput").ap()

    I = {}
    I["xp"] = din("xp", [TP, D])
    I["xs"] = din("xs", [TL, D])
    I["ckv_c"] = din("ckv_c", [PAST, KVL])
    I["kpe_c"] = din("kpe_c", [PAST, ROPE])
    I["cvec"] = din("cvec", [D, 2])
    for nm, shp in [("w_ada", [D, 6 * D]), ("b_ada", [6 * D]), ("g_mix", [D]), ("w_in", [D, DIN]), ("g_qa", [QL]),
                    ("w_uq", [QL, N_HEADS * QK]), ("g_kva", [KVL]), ("w_ukv", [KVL, 1024]), ("g_q", [QK]),
                    ("g_k", [QK]), ("hy_conv_w", [3 * 1536]), ("hy_conv_b", [1536]), ("filt_w1", [33, 64]),
                    ("filt_b1", [64]), ("filt_freq1", [64]), ("filt_w2", [64, 64]), ("filt_b2", [64]),
                    ("filt_freq2", [64]), ("filt_w3", [64, 2048]), ("filt_bias", [2, 512]), ("g_out_att", [512]),
                    ("g_out_hy", [512]), ("w_out", [D, D]), ("g_ffn", [D]), ("w_up", [D, 2 * DFF]),
                    ("ffn_conv_w", [3 * 2 * DFF]), ("ffn_conv_b", [2 * DFF]), ("w_down", [DFF, D])]:
        I[nm] = din(nm, shp)
    I["shiftT"] = din("shiftT", [32, 96], BF16)
    I["shiftrotT"] = din("shiftrotT", [32, 96], BF16)
    I["rot96T"] = din("rot96T", [96, 96], BF16)
    I["cosk"] = din("cosk", [32, TL + PAST])
    I["sink"] = din("sink", [32, TL + PAST])
    I["cosq"] = din("cosq", [96, 1026])
    I["sinq"] = din("sinq", [96, 1026])
    I["Fb"] = din("Fb", [128, 512], BF16)
    I["CB_p"] = din("CB_p", [128, 4, 128], BF16)
    I["CB_l"] = din("CB_l", [128, 4, 128], BF16)
    I["R1T"] = din("R1T", [128, 128], BF16)
    I["E0T"] = din("E0T", [128, 128], BF16)
    I["FAd_p"] = din("FAd_p", [128, 2, 3, 128], BF16)
    I["FAd_l"] = din("FAd_l", [128, 2, 3, 128], BF16)
    I["FAk_p"] = din("FAk_p", [1, 128, 2, 3, 128], BF16)
    I["FAk_l"] = din("FAk_l", [2, 128, 2, 3, 128], BF16)
    I["GA1_p"] = din("GA1_p", [128, 2, 2, 512], BF16)
    I["GA2_p"] = din("GA2_p", [128, 2, 2, 512], BF16)
    I["GA1_l"] = din("GA1_l", [128, 2, 2, 512], BF16)
    I["GA2_l"] = din("GA2_l", [128, 2, 2, 160], BF16)
    I["zT_p"] = din("zT_p", [33, 256])
    I["zT_l"] = din("zT_l", [33, 4096])
    I["win_p"] = din("win_p", [256, 512], BF16)
    I["win_l"] = din("win_l", [4096, 512], BF16)
    I["seamc"] = din("seamc", [128, 4])
    I["halomask"] = din("halomask", [128, 2])
    I["ident_f"] = din("ident_f", [128, 128])
    I["ident_b"] = din("ident_b", [128, 128], BF16)

    O = {}
    O["y_prompt"] = dout("y_prompt", [TP, D])
    O["y_sample"] = dout("y_sample", [1024, D])
    O["new_ckv"] = dout("new_ckv", [TP, KVL])
    O["new_kpe"] = dout("new_kpe", [TP, ROPE])
    DBG = {}

    def dbg_out(name, shape):
        if name in debug:
            DBG[name] = dout("dbg_" + name, shape)
            return DBG[name]
        return None

    with ExitStack() as st:
        P.setup(st)
        AR = Arena(nc, st, "arena", 207 * 1024)
        ps = []
        for i in range(8):
            t_ = st.enter_context(nc.psum_tensor("psb%d" % i, [128, 512], F32))
            ps.append(Buf(t_[:], "ps%d" % i))
            ps[-1].dep.excl = True

        def psb(i):
            return ps[i].ap.bitcast(BF16)

        ident_f = AR.alloc([128], F32, "ident_f")
        ident_b = AR.alloc([128], BF16, "ident_b")
        P.dma("sp", lambda e: e.dma_start(out=ident_f.ap, in_=I["ident_f"]), writes=[ident_f.dep])
        P.dma("sp", lambda e: e.dma_start(out=ident_b.ap, in_=I["ident_b"]), writes=[ident_b.dep])
        epsb = AR.alloc([1], F32, "epsb")
        P.op("pool", lambda e: e.memset(epsb.ap, EPS), writes=[epsb.dep])
        ones_b = AR.alloc([128], BF16, "ones_b")
        P.op("pool", lambda e: e.memset(ones_b.ap, 1.0), writes=[ones_b.dep])
        ones_f = AR.alloc([128], F32, "ones_f")
        P.op("pool", lambda e: e.memset(ones_f.ap, 1.0), writes=[ones_f.dep])

        nrows = sum(n for _, n in VEC_LAYOUT)
        nblk = (nrows + 127) // 128
        vec = AR.alloc([nblk * 128], F32, "vec")
        voff = {}
        m0 = AR.mark()
        rows = AR.alloc([nblk, 128], F32, "vecrows")
        P.op("pool", lambda e: e.memset(rows.ap, 0.0), writes=[rows.dep])
        r = 0
        for nm, n in VEC_LAYOUT:
            voff[nm] = r
            src = I[nm].rearrange("(r p) -> r p", p=128)
            done = 0
            while done < n:
                blk, p0 = (r + done) // 128, (r + done) % 128
                cnt = min(n - done, 128 - p0)
                P.dma("sp", lambda e, blk=blk, p0=p0, cnt=cnt, src=src, done=done:
                      e.dma_start(out=rows.ap[p0:p0 + cnt, blk, :], in_=src[done:done + cnt, :]),
                      writes=[rows.dep])
                done += cnt
            r += n
        for blk in range(nblk):
            P.op("pe", lambda e, blk=blk: e.transpose(ps[7].ap[:, blk * 128:(blk + 1) * 128], rows.ap[:, blk, :], ident_f.ap),
                 reads=[rows.dep, ident_f.dep], writes=[ps[7].dep])
        P.op("dve", lambda e: e.tensor_copy(out=vec.ap, in_=ps[7].ap[:, 0:nblk * 128]), reads=[ps[7].dep], writes=[vec.dep])
        AR.release(m0)

        def vcol(nm, i):
            return vec.ap[:, voff[nm] + i: voff[nm] + i + 1]

        mod = AR.alloc([2, 48], F32, "mod")
        m0 = AR.mark()
        cv = AR.alloc([KC, 2], F32, "cv")
        P.dma("sp", lambda e: e.dma_start(out=cv.ap, in_=I["cvec"].rearrange("(k p) v -> p k v", p=128)), writes=[cv.dep])
        sc = AR.alloc([KC, 2], F32, "sc")
        P.op("act", lambda e: e.activation(out=sc.ap, in_=cv.ap, func=AF.Silu), reads=[cv.dep], writes=[sc.dep])
        wa = [AR.alloc([6 * D], F32, "wa%d" % i) for i in range(2)]
        wab = [AR.alloc([6 * D], BF16, "wab%d" % i) for i in range(2)]
        scb = AR.alloc([KC, 2], BF16, "scb")
        P.op("dve", lambda e: e.tensor_copy(out=scb.ap, in_=sc.ap), reads=[sc.dep], writes=[scb.dep])
        for k in range(KC):
            wb = wa[k % 2]
            wbb = wab[k % 2]
            P.dma("sp", lambda e, wb=wb, k=k: e.dma_start(out=wb.ap, in_=I["w_ada"][k * 128:(k + 1) * 128, :]), writes=[wb.dep])
            P.op("dve", lambda e, wb=wb, wbb=wbb: e.tensor_copy(out=wbb.ap[:, 0:2048], in_=wb.ap[:, 0:2048]), reads=[wb.dep], writes=[wbb.dep])
            P.op("act", lambda e, wb=wb, wbb=wbb: e.activation(out=wbb.ap[:, 2048:4096], in_=wb.ap[:, 2048:4096], func=AF.Copy),
                 reads=[wb.dep], writes=[wbb.dep])
            P.op("pool", lambda e, wb=wb, wbb=wbb: e.tensor_copy(out=wbb.ap[:, 4096:6144], in_=wb.ap[:, 4096:6144]), reads=[wb.dep], writes=[wbb.dep])
            for mc in range(48):
                P.op("pe", lambda e, wbb=wbb, mc=mc, k=k: e.matmul(
                    ps[6].ap[:, 2 * mc:2 * mc + 2], lhsT=wbb.ap[:, mc * 128:(mc + 1) * 128], rhs=scb.ap[:, k, :],
                    start=(k == 0 and mc == 0), stop=(k == KC - 1), skip_group_check=True), reads=[wbb.dep, scb.dep], writes=[ps[6].dep])
        P.op("dve", lambda e: e.tensor_tensor(
            out=mod.ap.rearrange("p v m -> p m v"), in0=ps[6].ap[:, 0:96].rearrange("p (m v) -> p m v", v=2),
            in1=vec.ap[:, voff["b_ada"]:voff["b_ada"] + 48].unsqueeze(2).to_broadcast([128, 48, 2]), op=ALU.add),
            reads=[ps[6].dep, vec.dep], writes=[mod.dep])
        AR.release(m0)
        A1 = AR.alloc([KC, 2], F32, "A1")
        A2 = AR.alloc([KC, 2], F32, "A2")
        for v in range(2):
            P.op("dve", lambda e, v=v: e.scalar_tensor_tensor(
                out=A1.ap[:, :, v], in0=mod.ap[:, v, 8:16], scalar=1.0, in1=vec.ap[:, voff["g_mix"]:voff["g_mix"] + 8],
                op0=ALU.add, op1=ALU.mult), reads=[mod.dep, vec.dep], writes=[A1.dep])
            P.op("dve", lambda e, v=v: e.scalar_tensor_tensor(
                out=A2.ap[:, :, v], in0=mod.ap[:, v, 32:40], scalar=1.0, in1=vec.ap[:, voff["g_ffn"]:voff["g_ffn"] + 8],
                op0=ALU.add, op1=ALU.mult), reads=[mod.dep, vec.dep], writes=[A2.dep])

        d2_ = dbg_out("vec", [128, nblk * 128])
        if d2_ is not None:
            P.dma("sp", lambda e: e.dma_start(out=d2_, in_=vec.ap), reads=[vec.dep])
        d_ = dbg_out("mod", [128, 96])
        if d_ is not None:
            P.dma("sp", lambda e: e.dma_start(out=d_, in_=mod.ap.rearrange("p v m -> p (v m)")), reads=[mod.dep])


        modD = nc.dram_tensor("modD", [96, 128], F32).ap()
        modD_dep = Dep("modD")
        m0 = AR.mark()
        modTt = AR.alloc([128], F32, "modTt")
        P.op("pe", lambda e: e.transpose(ps[7].ap[0:96, 0:128], mod.ap.rearrange("p v m -> p (v m)"), ident_f.ap),
             reads=[mod.dep, ident_f.dep], writes=[ps[7].dep])
        P.op("dve", lambda e: e.tensor_copy(out=modTt.ap[0:96, :], in_=ps[7].ap[0:96, 0:128]), reads=[ps[7].dep], writes=[modTt.dep])
        P.dma("sp", lambda e: e.dma_start(out=modD, in_=modTt.ap[0:96, :]), reads=[modTt.dep], writes=[modD_dep])
        AR.release(m0)
        modD_flat = modD.rearrange("r p -> (r p)")
        sel = AR.alloc([4], F32, "sel")
        P.op("pool", lambda e: e.memset(sel.ap, 0.0), writes=[sel.dep])
        P.op("pool", lambda e: e.tensor_copy(out=sel.ap[:, 0:1], in_=ident_f.ap[:, 127:128]), reads=[ident_f.dep], writes=[sel.dep])
        P.op("pool", lambda e: e.tensor_copy(out=sel.ap[:, 3:4], in_=ident_f.ap[:, 0:1]), reads=[ident_f.dep], writes=[sel.dep])
        halomask = AR.alloc([2], F32, "halomask")
        P.dma("sp", lambda e: e.dma_start(out=halomask.ap, in_=I["halomask"]), writes=[halomask.dep])
        x1scr = [nc.dram_tensor("x1scr0", [TP, D], F32).ap(), nc.dram_tensor("x1scr1", [1026, D], F32).ap()]
        x1scr_dep = [Dep("x1scr0"), Dep("x1scr1")]
        gkva_bc = AR.alloc([KVL], F32, "gkva_bc")
        P.dma("sp", lambda e: e.dma_start(out=gkva_bc.ap, in_=I["g_kva"].partition_broadcast(128)), writes=[gkva_bc.dep])

        def MM(out, lhsT, rhs, start=True, stop=True, r=(), w=(), skip=False):
            P.op("pe", lambda e: e.matmul(out, lhsT=lhsT, rhs=rhs, start=start, stop=stop, skip_group_check=skip),
                 reads=r, writes=w)

        def TR(out, in_, ident, r=(), w=()):
            P.op("pe", lambda e: e.transpose(out, in_, ident), reads=r, writes=w)

        def ACTF(out, in_, func, r=(), w=(), eng="act", **kw):
            P.op(eng, lambda e: e.activation(out=out, in_=in_, func=func, **kw), reads=r, writes=w)

        def TT(eng, out, in0, in1, op, r=(), w=()):
            P.op(eng, lambda e: e.tensor_tensor(out=out, in0=in0, in1=in1, op=op), reads=r, writes=w)

        def TS(eng, out, in0, s1, s2, op0, op1=None, r=(), w=()):
            if op1 is None:
                P.op(eng, lambda e: e.tensor_scalar(out=out, in0=in0, scalar1=s1, scalar2=None, op0=op0), reads=r, writes=w)
            else:
                P.op(eng, lambda e: e.tensor_scalar(out=out, in0=in0, scalar1=s1, scalar2=s2, op0=op0, op1=op1), reads=r, writes=w)

        def STT(out, in0, scalar, in1, op0, op1, r=(), w=()):
            P.op("dve", lambda e: e.scalar_tensor_tensor(out=out, in0=in0, scalar=scalar, in1=in1, op0=op0, op1=op1),
                 reads=r, writes=w)

        def CP(eng, out, in_, r=(), w=()):
            if eng == "act":
                P.op("act", lambda e: e.activation(out=out, in_=in_, func=AF.Copy), reads=r, writes=w)
            else:
                P.op(eng, lambda e: e.tensor_copy(out=out, in_=in_), reads=r, writes=w)

        def DMA(out, in_, r=(), w=(), q="sp", slow=False):
            if slow:
                P.dma(q, lambda e: e.dma_start(out=out, in_=in_, allow_slow_non_contiguous=True), reads=r, writes=w)
            else:
                P.dma(q, lambda e: e.dma_start(out=out, in_=in_), reads=r, writes=w)

        def RECIP(out, in_, r=(), w=()):
            P.op("dve", lambda e: e.reciprocal(out=out, in_=in_), reads=r, writes=w)

        def MEMSET(eng, out, val, w=()):
            P.op(eng, lambda e: e.memset(out, val), writes=w)

        gq_col = AR.alloc([1], F32, "gq_col")
        gk_col = AR.alloc([1], F32, "gk_col")
        gq2 = AR.alloc([1], F32, "gq2")
        gkpe = AR.alloc([1], F32, "gkpe")
        DMA(gq_col.ap[0:96, :], I["g_q"].rearrange("(p o) -> p o", o=1), w=[gq_col.dep], slow=True)
        DMA(gk_col.ap[0:96, :], I["g_k"].rearrange("(p o) -> p o", o=1), w=[gk_col.dep], slow=True)
        DMA(gkpe.ap[0:32, :], I["g_k"][64:96].rearrange("(p o) -> p o", o=1), w=[gkpe.dep], slow=True)
        CP("dve", gq2.ap[0:96, :], gq_col.ap[0:96, :], r=[gq_col.dep], w=[gq2.dep])
        TT("dve", gq2.ap[0:64, :], gq_col.ap[0:64, :], gk_col.ap[0:64, :], ALU.mult, r=[gq_col.dep, gk_col.dep], w=[gq2.dep])
        shiftT = AR.alloc([96], BF16, "shiftT")
        shiftrotT = AR.alloc([96], BF16, "shiftrotT")
        rot96T = AR.alloc([96], BF16, "rot96T")
        DMA(shiftT.ap[0:32, :], I["shiftT"], w=[shiftT.dep])
        DMA(shiftrotT.ap[0:32, :], I["shiftrotT"], w=[shiftrotT.dep])
        DMA(rot96T.ap[0:96, :], I["rot96T"], w=[rot96T.dep])
        gatt_bc = AR.alloc([512], F32, "gatt_bc")
        ghy_bc = AR.alloc([512], F32, "ghy_bc")
        DMA(gatt_bc.ap, I["g_out_att"].partition_broadcast(128), w=[gatt_bc.dep])
        DMA(ghy_bc.ap, I["g_out_hy"].partition_broadcast(128), w=[ghy_bc.dep])

        Uscr = [nc.dram_tensor("Uscr0", [1536, TP], BF16).ap(), nc.dram_tensor("Uscr1", [1536, TL], BF16).ap()]
        Uscr_dep = [Dep("Uscr0"), Dep("Uscr1")]

        def run_group(v):
            ntok = TP if v == 0 else TL
            nq = TP if v == 0 else 1026
            nqt = 8 if v == 0 else 9
            xin = I["xp"] if v == 0 else I["xs"]
            nkeys = ntok + (PAST if v == 1 else 0)
            nkt = nkeys // 128
            gm = AR.mark()
            O_all = AR.alloc([nqt, 512], F32, "O_all")
            g2m = AR.mark()
            bias_fm = AR.alloc([15], F32, "bias_fm")
            bias_row = AR.alloc([288], BF16, "bias_row")
            kvcT = AR.alloc([2, nkeys], BF16, "kvcT")
            kpeT = AR.alloc([nkeys], BF16, "kpeT")
            q_cT = AR.alloc([3, nq], F32, "q_cT")
            sspe = AR.alloc([nkt], F32, "sspe")
            hThalo = AR.alloc([KC, 2], BF16, "hThalo")
            p1m = AR.mark()
            Win = AR.alloc([KC, DIN], BF16, "Win")
            win_d = I["w_in"].rearrange("(k p) n -> p k n", p=128)
            for k in range(KC):
                DMA(Win.ap[:, k, :], win_d[:, k, :], w=[Win.dep], q="pool")

            xts = [AR.alloc([D], F32, "xt%d" % i) for i in range(4)]
            xns = [AR.alloc([D], BF16, "xn%d" % i) for i in range(3)]
            junk = AR.alloc([D], F32, "junk")
            stat = AR.alloc([8], F32, "stat")
            hTg = [AR.alloc([KC, 512], BF16, "hTg%d" % i) for i in range(2)]
            ckvo = [AR.alloc([KVL + ROPE], F32, "ckvo%d" % i) for i in range(2)]
            ckvb = [AR.alloc([KVL + ROPE], BF16, "ckvb%d" % i) for i in range(2)]
            ustg = [AR.alloc([512], BF16, "ustg%d" % i) for i in range(3)]
            ngroups = ntok // 512
            ucount = 0
            hTdeps = [[Dep("hT%d_%d" % (i, k)) for k in range(KC)] for i in range(2)]
            for i in range(2):
                for k in range(KC):
                    hTdeps[i][k].r = dict(hTg[i].dep.r)
            statB = AR.alloc([8], F32, "statB")

            def tile_pre(g, tt):
                ti = g * 4 + tt
                xt = xts[ti % 4]
                xn = xns[ti % 3]
                DMA(xt.ap, xin[ti * 128:(ti + 1) * 128, :], w=[xt.dep])
                ACTF(junk.ap, xt.ap, AF.Square, r=[xt.dep], w=[junk.dep, stat.dep], accum_out=stat.ap[:, 0:1])
                ACTF(stat.ap[:, 1:2], stat.ap[:, 0:1], AF.Sqrt, r=[stat.dep, epsb.dep], w=[stat.dep], scale=1.0 / D,
                     bias=epsb.ap[:, 0:1])
                RECIP(stat.ap[:, 2:3], stat.ap[:, 1:2], r=[stat.dep], w=[stat.dep])
                TS("dve", xn.ap, xt.ap, stat.ap[:, 2:3], None, ALU.mult, r=[xt.dep, stat.dep], w=[xn.dep])

            def tile_mid(g, tt):
                hT = hTg[g % 2]
                hd = hTdeps[g % 2]
                ti = g * 4 + tt
                xn = xns[ti % 3]
                pb = ti % 2
                for k in range(KC):
                    TR(psb(pb)[:, k * 128:(k + 1) * 128], xn.ap[:, k * 128:(k + 1) * 128], ident_b.ap,
                       r=[xn.dep, ident_b.dep], w=[ps[pb].dep])
                for k in range(KC):
                    if ti % 2 == 0:
                        TS("dve", hT.ap[:, k, tt * 128:(tt + 1) * 128], psb(pb)[:, k * 128:(k + 1) * 128], A1.ap[:, k, v:v + 1],
                           mod.ap[:, v, k:k + 1], ALU.mult, ALU.add, r=[ps[pb].dep, A1.dep, mod.dep], w=[hd[k]])
                    else:
                        ACTF(hT.ap[:, k, tt * 128:(tt + 1) * 128], psb(pb)[:, k * 128:(k + 1) * 128], AF.Identity,
                             r=[ps[pb].dep, A1.dep, mod.dep], w=[hd[k]], scale=A1.ap[:, k, v:v + 1], bias=mod.ap[:, v, k:k + 1])
                kb = 2 + (ti % 2)
                for k in range(KC):
                    MM(ps[kb].ap[:, 0:288], hT.ap[:, k, tt * 128:(tt + 1) * 128], Win.ap[:, k, 384:672],
                       start=(k == 0), stop=(k == KC - 1), r=[hd[k], Win.dep], w=[ps[kb].dep])

            def tile_post_a(g, tt):
                ti = g * 4 + tt
                kb = 2 + (ti % 2)
                co = ckvo[ti % 2]
                cb = ckvb[ti % 2]
                ACTF(junk.ap[:, 0:256], ps[kb].ap[:, 0:256], AF.Square, r=[ps[kb].dep], w=[junk.dep, statB.dep],
                     accum_out=statB.ap[:, 3:4])
                ACTF(statB.ap[:, 4:5], statB.ap[:, 3:4], AF.Sqrt, r=[statB.dep, epsb.dep], w=[statB.dep], scale=1.0 / KVL,
                     bias=epsb.ap[:, 0:1])
                RECIP(statB.ap[:, 5:6], statB.ap[:, 4:5], r=[statB.dep], w=[statB.dep])
                STT(co.ap[:, 0:256], ps[kb].ap[:, 0:256], statB.ap[:, 5:6], gkva_bc.ap, ALU.mult, ALU.mult,
                    r=[ps[kb].dep, statB.dep, gkva_bc.dep], w=[co.dep])
                CP("act", co.ap[:, 256:288], ps[kb].ap[:, 256:288], r=[ps[kb].dep], w=[co.dep])
                ACTF(junk.ap[:, 0:32], co.ap[:, 256:288], AF.Square, r=[co.dep], w=[junk.dep, sspe.dep],
                     accum_out=sspe.ap[:, ti:ti + 1])
                if v == 0:
                    DMA(O["new_ckv"][ti * 128:(ti + 1) * 128, :], co.ap[:, 0:256], r=[co.dep], q="pool")
                    DMA(O["new_kpe"][ti * 128:(ti + 1) * 128, :], co.ap[:, 256:288], r=[co.dep], q="pool")
                CP("dve", cb.ap, co.ap, r=[co.dep], w=[cb.dep])

            def tile_post_b(g, tt):
                ti = g * 4 + tt
                cb = ckvb[ti % 2]
                tb = 6 + (ti % 2)
                TR(psb(tb)[:, 0:128], cb.ap[:, 0:128], ident_b.ap, r=[cb.dep, ident_b.dep], w=[ps[tb].dep])
                TR(psb(tb)[:, 128:256], cb.ap[:, 128:256], ident_b.ap, r=[cb.dep, ident_b.dep], w=[ps[tb].dep])
                TR(psb(tb)[0:32, 256:384], cb.ap[:, 256:288], ident_b.ap, r=[cb.dep, ident_b.dep], w=[ps[tb].dep])
                CP("act", kvcT.ap[:, :, ti * 128:(ti + 1) * 128], psb(tb)[:, 0:256].rearrange("p (c t) -> p c t", t=128),
                   r=[ps[tb].dep], w=[kvcT.dep])
                CP("act", kpeT.ap[0:32, ti * 128:(ti + 1) * 128], psb(tb)[0:32, 256:384], r=[ps[tb].dep], w=[kpeT.dep])

            ucount_box = [0]

            def proj_ops(g, part):
                hT = hTg[g % 2]
                chunks = list(range(3, 15))
                if v == 1 and g in (3, 4, 5, 6):
                    chunks = list(range(3, 11))
                if v == 0 or g < 2:
                    chunks = [0, 1, 2] + chunks
                for ci in chunks[part::4]:
                    ucount = ucount_box[0]
                    c0 = ci * 128 if ci < 3 else 672 + (ci - 3) * 128
                    fb = 4 + (ucount % 2)
                    for k in range(KC):
                        MM(ps[fb].ap, Win.ap[:, k, c0:c0 + 128], hT.ap[:, k, :], start=(k == 0), stop=(k == KC - 1),
                           r=[Win.dep, hTdeps[g % 2][k]], w=[ps[fb].dep])
                    if ci < 3:
                        CP("act", q_cT.ap[:, ci, g * 512:(g + 1) * 512], ps[fb].ap, r=[ps[fb].dep], w=[q_cT.dep])
                    else:
                        us = ustg[ucount % 3]
                        CP("act" if ucount % 2 else "dve", us.ap, ps[fb].ap, r=[ps[fb].dep], w=[us.dep])
                        DMA(Uscr[v][(ci - 3) * 128:(ci - 2) * 128, g * 512:(g + 1) * 512], us.ap, r=[us.dep], w=[Uscr_dep[v]], q="pool")
                    ucount_box[0] += 1
                if part == 3:
                    if v == 1 and g == 2:
                        CP("pool", hThalo.ap[:, :, 1:2], hT.ap[:, :, 0:1], r=hTdeps[g % 2], w=[hThalo.dep])
                    if v == 1 and g == 7:
                        CP("pool", hThalo.ap[:, :, 0:1], hT.ap[:, :, 511:512], r=hTdeps[g % 2], w=[hThalo.dep])

            ntl = ngroups * 4
            tile_pre(0, 0)
            tile_pre(0, 1)
            for i in range(ntl + 4):
                g, tt = i // 4, i % 4
                if i + 2 < ntl:
                    tile_pre((i + 2) // 4, (i + 2) % 4)
                if i < ntl:
                    tile_mid(g, tt)
                if i < ntl:
                    tile_post_a(g, tt)
                if g >= 1:
                    proj_ops(g - 1, tt)
                if i < ntl:
                    tile_post_b(g, tt)
            MEMSET("pool", hTg[0].ap[:, 0, 0:1], 0.0, w=hTdeps[0] + [hTg[0].dep])
            MEMSET("pool", hTg[1].ap[:, 0, 0:1], 0.0, w=hTdeps[1] + [hTg[1].dep])
            if v == 1:
                for ci in range(3):
                    for k in range(KC):
                        MM(ps[4].ap[:, 0:2], Win.ap[:, k, ci * 128:(ci + 1) * 128], hThalo.ap[:, k, :], start=(k == 0),
                           stop=(k == KC - 1), r=[Win.dep, hThalo.dep], w=[ps[4].dep])
                    CP("act", q_cT.ap[:, ci, 1024:1026], ps[4].ap[:, 0:2], r=[ps[4].dep], w=[q_cT.dep])
                cst = AR.alloc([4, KVL + ROPE], F32, "cst")
                csb = AR.alloc([4, KVL + ROPE], BF16, "csb")
                DMA(cst.ap[:, :, 0:256], I["ckv_c"].rearrange("(t p) f -> p t f", p=128), w=[cst.dep])
                DMA(cst.ap[:, :, 256:288], I["kpe_c"].rearrange("(t p) f -> p t f", p=128), w=[cst.dep])
                CP("pool", csb.ap, cst.ap, r=[cst.dep], w=[csb.dep])
                for t4 in range(4):
                    ti = 32 + t4
                    ACTF(junk.ap[:, 0:32], cst.ap[:, t4, 256:288], AF.Square, r=[cst.dep], w=[junk.dep, sspe.dep],
                         accum_out=sspe.ap[:, ti:ti + 1])
                    tb = 6 + (t4 % 2)
                    TR(psb(tb)[:, 0:128], csb.ap[:, t4, 0:128], ident_b.ap, r=[csb.dep, ident_b.dep], w=[ps[tb].dep])
                    TR(psb(tb)[:, 128:256], csb.ap[:, t4, 128:256], ident_b.ap, r=[csb.dep, ident_b.dep], w=[ps[tb].dep])
                    TR(psb(tb)[0:32, 256:384], csb.ap[:, t4, 256:288], ident_b.ap, r=[csb.dep, ident_b.dep], w=[ps[tb].dep])
                    CP("act", kvcT.ap[:, :, ti * 128:(ti + 1) * 128], psb(tb)[:, 0:256].rearrange("p (c t) -> p c t", t=128),
                       r=[ps[tb].dep], w=[kvcT.dep])
                    CP("act", kpeT.ap[0:32, ti * 128:(ti + 1) * 128], psb(tb)[0:32, 256:384], r=[ps[tb].dep], w=[kpeT.dep])
            AR.release(p1m)

            am = AR.mark()
            Wuq = AR.alloc([3, N_HEADS * QK], BF16, "Wuq")
            Wukv = AR.alloc([2, 1024], BF16, "Wukv")
            DMA(Wuq.ap, I["w_uq"].rearrange("(k p) n -> p k n", p=128), w=[Wuq.dep], q="pool")
            DMA(Wukv.ap, I["w_ukv"].rearrange("(k p) n -> p k n", p=128), w=[Wukv.dep], q="pool")
            Vext = AR.alloc([nkt, N_HEADS, 65], BF16, "Vext")
            MEMSET("pool", Vext.ap[:, :, :, 64:65], 1.0, w=[Vext.dep])
            kscale = AR.alloc([nkt, N_HEADS], F32, "kscale")
            KT = [AR.alloc([nkeys], BF16, "KT%d" % i) for i in range(2)]
            QT = AR.alloc([N_HEADS, nq], BF16, "QT")
            MEMSET("pool", QT.ap[96:128, :, :], 0.0, w=[QT.dep])
            for i_ in range(2):
                MEMSET("pool", KT[i_].ap[96:128, :], 0.0, w=[KT[i_].dep])
            tmpA = [AR.alloc([512], F32, "tmpA%d" % i) for i in range(2)]
            tmpB = [AR.alloc([512], F32, "tmpB%d" % i) for i in range(2)]
            tmpC = [AR.alloc([512], BF16, "tmpC%d" % i) for i in range(2)]
            junk2 = AR.alloc([512], F32, "junk2")
            st2 = AR.alloc([32], F32, "st2")
            gkb = [AR.alloc([512], BF16, "gkb%d" % i) for i in range(2)]
            if v == 1:
                ctab = [AR.alloc([512], F32, "ctab%d" % i) for i in range(2)]
                stab = [AR.alloc([512], F32, "stab%d" % i) for i in range(2)]
            def rope_block(kb):
                ks = slice(kb * 512, (kb + 1) * 512)
                gk = gkb[kb % 2]
                TS("pool", gk.ap[0:32, :], kpeT.ap[0:32, ks], gkpe.ap[0:32, 0:1], 0.0, ALU.mult, ALU.add,
                   r=[kpeT.dep, gkpe.dep], w=[gk.dep])
                MM(ps[5].ap[0:96, :], shiftT.ap[0:32, :], gk.ap[0:32, :], r=[shiftT.dep, gk.dep], w=[ps[5].dep])
                if v == 0:
                    CP("act", KT[0].ap[64:96, ks], ps[5].ap[64:96, :], r=[ps[5].dep], w=[KT[0].dep])
                    CP("dve", KT[1].ap[64:96, ks], ps[5].ap[64:96, :], r=[ps[5].dep], w=[KT[1].dep])
                else:
                    ct, stb = ctab[kb % 2], stab[kb % 2]
                    DMA(ct.ap[64:96, :], I["cosk"][:, ks], w=[ct.dep])
                    DMA(stb.ap[64:96, :], I["sink"][:, ks], w=[stb.dep])
                    MM(ps[6].ap[0:96, :], shiftrotT.ap[0:32, :], gk.ap[0:32, :], r=[shiftrotT.dep, gk.dep], w=[ps[6].dep])
                    ta, tb_ = tmpA[kb % 2], tmpB[kb % 2]
                    TT("dve", ta.ap[64:96, :], ps[5].ap[64:96, :], ct.ap[64:96, :], ALU.mult, r=[ps[5].dep, ct.dep], w=[ta.dep])
                    TT("dve", tb_.ap[64:96, :], ps[6].ap[64:96, :], stb.ap[64:96, :], ALU.mult, r=[ps[6].dep, stb.dep], w=[tb_.dep])
                    TT("dve", KT[0].ap[64:96, ks], ta.ap[64:96, :], tb_.ap[64:96, :], ALU.add, r=[ta.dep, tb_.dep], w=[KT[0].dep])
                    TT("dve", KT[1].ap[64:96, ks], ta.ap[64:96, :], tb_.ap[64:96, :], ALU.add, r=[ta.dep, tb_.dep], w=[KT[1].dep])

            def kv_tile(kt):
                for half in range(2):
                    bk = (kt % 2) * 2 + half
                    for c in range(2):
                        MM(ps[bk].ap, kvcT.ap[:, c, kt * 128:(kt + 1) * 128], Wukv.ap[:, c, half * 512:(half + 1) * 512],
                           start=(c == 0), stop=(c == 1), r=[kvcT.dep, Wukv.dep], w=[ps[bk].dep])
                    pv_ = ps[bk].ap.rearrange("p (h x) -> p h x", x=128)
                    CP("act", Vext.ap[:, kt, half * 4:(half + 1) * 4, 0:64], pv_[:, :, 64:128], r=[ps[bk].dep], w=[Vext.dep])
                    ACTF(junk2.ap[:, 0:256].rearrange("p (h x) -> p h x", x=64), pv_[:, :, 0:64], AF.Square,
                         r=[ps[bk].dep], w=[junk2.dep])
                    P.op("dve", lambda e, half=half: e.reduce_sum(
                        out=st2.ap[:, half * 4:(half + 1) * 4], in_=junk2.ap[:, 0:256].rearrange("p (h x) -> p h x", x=64),
                        axis=AX.X), reads=[junk2.dep], writes=[st2.dep])
                TS("dve", st2.ap[:, 8:16], st2.ap[:, 0:8], sspe.ap[:, kt:kt + 1], 1.0 / QK, ALU.add, ALU.mult,
                   r=[st2.dep, sspe.dep], w=[st2.dep])
                ACTF(st2.ap[:, 16:24], st2.ap[:, 8:16], AF.Sqrt, r=[st2.dep, epsb.dep], w=[st2.dep], bias=epsb.ap[:, 0:1])
                RECIP(st2.ap[:, 24:32], st2.ap[:, 16:24], r=[st2.dep], w=[st2.dep])
                TS("dve", kscale.ap[:, kt, :], st2.ap[:, 24:32], float(QK) ** -0.5, None, ALU.mult, r=[st2.dep], w=[kscale.dep])


            qnT = AR.alloc([3, nq], BF16, "qnT")
            sqb = AR.alloc([3, 512], BF16, "sqb")
            if v == 1:
                cosq = AR.alloc([nq], F32, "cosq")
                sinq = AR.alloc([nq], F32, "sinq")
                DMA(cosq.ap[0:96, :], I["cosq"], w=[cosq.dep])
                DMA(sinq.ap[0:96, :], I["sinq"], w=[sinq.dep])
            qgroups = [(0, 512), (512, 512)] + ([(1024, 2)] if v == 1 else [])

            def q_gen():
                for gi, (q0, qn_) in enumerate(qgroups):
                    qs = slice(q0, q0 + qn_)
                    ACTF(sqb.ap[:, :, 0:qn_], q_cT.ap[:, :, qs], AF.Square, r=[q_cT.dep], w=[sqb.dep])
                    for c in range(3):
                        MM(ps[7].ap[:, 0:qn_], ones_b.ap, sqb.ap[:, c, 0:qn_], start=(c == 0), stop=(c == 2),
                           r=[ones_b.dep, sqb.dep], w=[ps[7].dep])
                    ta = tmpA[gi % 2]
                    ACTF(ta.ap[:, 0:qn_], ps[7].ap[:, 0:qn_], AF.Ln, r=[ps[7].dep, epsb.dep], w=[ta.dep], scale=1.0 / QL,
                         bias=epsb.ap[:, 0:1])
                    ACTF(ta.ap[:, 0:qn_], ta.ap[:, 0:qn_], AF.Exp, r=[ta.dep], w=[ta.dep], scale=-0.5)
                    for c in range(3):
                        STT(qnT.ap[:, c, qs], q_cT.ap[:, c, qs], vcol("g_qa", c), ta.ap[:, 0:qn_], ALU.mult, ALU.mult,
                            r=[q_cT.dep, vec.dep, ta.dep], w=[qnT.dep])
                    yield
                    for h in range(N_HEADS):
                        qb = 5 + (h % 2)
                        for c in range(3):
                            MM(ps[qb].ap[0:96, 0:qn_], Wuq.ap[:, c, h * QK:(h + 1) * QK], qnT.ap[:, c, qs], start=(c == 0),
                               stop=(c == 2), r=[Wuq.dep, qnT.dep], w=[ps[qb].dep])
                        tb_ = tmpB[h % 2]
                        tc_ = tmpC[h % 2]
                        CP("act", tb_.ap[0:96, 0:qn_], ps[qb].ap[0:96, 0:qn_], r=[ps[qb].dep], w=[tb_.dep])
                        ACTF(tc_.ap[0:96, 0:qn_], ps[qb].ap[0:96, 0:qn_], AF.Square, r=[ps[qb].dep], w=[tc_.dep])
                        MM(ps[7].ap[0:96, 0:qn_], ones_b.ap[0:96, 0:96], tc_.ap[0:96, 0:qn_], r=[ones_b.dep, tc_.dep], w=[ps[7].dep])
                        t2 = tmpA[(gi + 1) % 2]
                        ACTF(t2.ap[0:96, 0:qn_], ps[7].ap[0:96, 0:qn_], AF.Ln, r=[ps[7].dep, epsb.dep], w=[t2.dep],
                             scale=1.0 / QK, bias=epsb.ap[0:96, 0:1])
                        ACTF(t2.ap[0:96, 0:qn_], t2.ap[0:96, 0:qn_], AF.Exp, r=[t2.dep], w=[t2.dep], scale=-0.5)
                        if v == 0:
                            STT(QT.ap[0:96, h, qs], tb_.ap[0:96, 0:qn_], gq2.ap[0:96, 0:1], t2.ap[0:96, 0:qn_], ALU.mult, ALU.mult,
                                r=[tb_.dep, gq2.dep, t2.dep], w=[QT.dep])
                        else:
                            STT(tb_.ap[0:96, 0:qn_], tb_.ap[0:96, 0:qn_], gq2.ap[0:96, 0:1], t2.ap[0:96, 0:qn_], ALU.mult, ALU.mult,
                                r=[tb_.dep, gq2.dep, t2.dep], w=[tb_.dep])
                            CP("pool", tc_.ap[0:96, 0:qn_], tb_.ap[0:96, 0:qn_], r=[tb_.dep], w=[tc_.dep])
                            MM(ps[qb].ap[0:96, 0:qn_], rot96T.ap[0:96, :], tc_.ap[0:96, 0:qn_], r=[rot96T.dep, tc_.dep], w=[ps[qb].dep])
                            TT("dve", t2.ap[0:96, 0:qn_], ps[qb].ap[0:96, 0:qn_], sinq.ap[0:96, qs], ALU.mult,
                               r=[ps[qb].dep, sinq.dep], w=[t2.dep])
                            TT("dve", tb_.ap[0:96, 0:qn_], tb_.ap[0:96, 0:qn_], cosq.ap[0:96, qs], ALU.mult,
                               r=[tb_.dep, cosq.dep], w=[tb_.dep])
                            TT("dve", QT.ap[0:96, h, qs], tb_.ap[0:96, 0:qn_], t2.ap[0:96, 0:qn_], ALU.add,
                               r=[tb_.dep, t2.dep], w=[QT.dep])

                        yield

            qg_ = q_gen()
            for kb in range(nkeys // 512):
                rope_block(kb)
            for kt in range(nkt):
                kv_tile(kt)
                if kt % 3 != 2:
                    next(qg_, None)
            for _ in qg_:
                pass
            PT = [AR.alloc([512], BF16, "PT%d" % i) for i in range(3)]
            rc = AR.alloc([8], F32, "rc")
            if v == 0:
                att_groups = [(s_ * 256, 256, [2 * s_, 2 * s_ + 1]) for s_ in range(4)]
            else:
                att_groups = [(0, 512, list(range(nkt))), (512, 512, list(range(nkt))), (1024, 2, list(range(nkt)))]
            it = 0
            accn = 0
            for h in range(N_HEADS):
                KTh = KT[h % 2]
                for kb in range(nkeys // 512):
                    ks = slice(kb * 512, (kb + 1) * 512)
                    fb = 5 + (kb % 2)
                    for c in range(2):
                        MM(ps[fb].ap[:, :], Wukv.ap[:, c, h * 128:(h + 1) * 128], kvcT.ap[:, c, ks], start=(c == 0), stop=(c == 1),
                           r=[Wukv.dep, kvcT.dep], w=[ps[fb].dep])
                    CP("dve" if kb % 2 else "act", KTh.ap[0:64, ks], ps[fb].ap[0:64, :], r=[ps[fb].dep], w=[KTh.dep])
                for (q0, qn_, kts) in att_groups:
                    ab = 3 + (accn % 2)
                    accn += 1
                    nqt_g = (qn_ + 127) // 128
                    def s_mm(ki_, it_):
                        kt_ = kts[ki_]
                        MM(ps[it_ % 3].ap[:, 0:qn_], KTh.ap[:, kt_ * 128:(kt_ + 1) * 128], QT.ap[:, h, q0:q0 + qn_],
                           r=[KTh.dep, QT.dep], w=[ps[it_ % 3].dep])
                    s_mm(0, it)
                    for ki, kt in enumerate(kts):
                        sb_ = it % 3
                        pt = PT[it % 3]
                        it += 1
                        if ki + 1 < len(kts):
                            s_mm(ki + 1, it)
                        ACTF(pt.ap[:, 0:qn_], ps[sb_].ap[:, 0:qn_], AF.Exp, r=[ps[sb_].dep, kscale.dep], w=[pt.dep],
                             scale=kscale.ap[:, kt, h:h + 1])
                        for qt in range(nqt_g):
                            m_ = min(128, qn_ - qt * 128)
                            MM(ps[ab].ap[0:m_, qt * 65:(qt + 1) * 65], pt.ap[:, qt * 128:qt * 128 + m_], Vext.ap[:, kt, h, :],
                               start=(ki == 0 and qt == 0), stop=(ki == len(kts) - 1), r=[pt.dep, Vext.dep], w=[ps[ab].dep], skip=True)
                    for qt in range(nqt_g):
                        m_ = min(128, qn_ - qt * 128)
                        qti = q0 // 128 + qt
                        RECIP(rc.ap[0:m_, qt:qt + 1], ps[ab].ap[0:m_, qt * 65 + 64:qt * 65 + 65], r=[ps[ab].dep], w=[rc.dep])
                        TS("dve", O_all.ap[0:m_, qti, h * 64:(h + 1) * 64], ps[ab].ap[0:m_, qt * 65:qt * 65 + 64],
                           rc.ap[0:m_, qt:qt + 1], None, ALU.mult, r=[ps[ab].dep, rc.dep], w=[O_all.dep])
            AR.release(g2m)
            d_ = dbg_out("att%d" % v, [128, nqt * 512])
            if d_ is not None:
                DMA(d_, O_all.ap.rearrange("p t f -> p (t f)"), r=[O_all.dep])
            nout2 = 8 if v == 0 else 10
            y2hy = AR.alloc([nout2, 512], F32, "y2hy")
            hm = AR.mark()
            nb = ntok // 128
            nfb = 2 if v == 0 else 32
            nfilt = nfb * 128
            nks = 2 * nfb
            tsuf = "p" if v == 0 else "l"
            if v == 0:
                out1 = list(range(8))
                out2 = list(range(8))
            else:
                out1 = list(range(32))
                out2 = list(range(8)) + [31, 8]
            cg = 16 if v == 0 else 4
            ng = 128 // cg
            nh = 2
            Fb = AR.alloc([512], BF16, "Fb")
            CBt = AR.alloc([4, 128], BF16, "CB")
            R1T = AR.alloc([128], BF16, "R1T")
            E0T = AR.alloc([128], BF16, "E0T")
            DMA(Fb.ap, I["Fb"], w=[Fb.dep])
            DMA(CBt.ap, I["CB_" + tsuf], w=[CBt.dep])
            DMA(R1T.ap, I["R1T"], w=[R1T.dep])
            DMA(E0T.ap, I["E0T"], w=[E0T.dep])
            nrow_d = nb * cg
            FAd = AR.alloc([nh, 3, 128], BF16, "FAd")
            DMA(FAd.ap[0:nrow_d], I["FAd_" + tsuf], w=[FAd.dep])
            nkch = 1 if v == 0 else 2
            krows = 128
            FAk = [AR.alloc([nh, 3, 128], BF16, "FAk%d" % i) for i in range(nkch)]
            for i in range(nkch):
                DMA(FAk[i].ap[0:krows], I["FAk_" + tsuf][i], w=[FAk[i].dep])
            np1 = 2 * len(out1)
            np2 = 2 * len(out2)
            GA1 = AR.alloc([nh, 2, 2 * np1 * cg], BF16, "GA1")
            GA2 = AR.alloc([nh, 2, 2 * np2 * cg], BF16, "GA2")
            DMA(GA1.ap, I["GA1_" + tsuf], w=[GA1.dep])
            DMA(GA2.ap, I["GA2_" + tsuf], w=[GA2.dep])
            h2T = AR.alloc([nfilt], BF16, "h2T")
            W3b = AR.alloc([2048], BF16, "W3b")
            DMA(W3b.ap[0:64, :], I["filt_w3"], w=[W3b.dep], q="pool")
            fm_ = AR.mark()
            zT = AR.alloc([nfilt], F32, "zT")
            DMA(zT.ap[0:33, :], I["zT_" + tsuf], w=[zT.dep])
            w1t = AR.alloc([64], F32, "w1t")
            w2t = AR.alloc([64], F32, "w2t")
            DMA(w1t.ap[0:33, :], I["filt_w1"], w=[w1t.dep])
            DMA(w2t.ap[0:64, :], I["filt_w2"], w=[w2t.dep])
            fcol = AR.alloc([8], F32, "fcol")
            for i_, nm in enumerate(["filt_b1", "filt_freq1", "filt_b2", "filt_freq2"]):
                DMA(fcol.ap[0:64, i_:i_ + 1], I[nm].rearrange("(p o) -> p o", o=1), w=[fcol.dep], slow=True)
            for l_ in range(2):
                TS("dve", fcol.ap[0:64, 4 + 2 * l_:5 + 2 * l_], fcol.ap[0:64, 1 + 2 * l_:2 + 2 * l_], 1.0 / 3.0, None, ALU.mult,
                   r=[fcol.dep], w=[fcol.dep])
                TT("dve", fcol.ap[0:64, 5 + 2 * l_:6 + 2 * l_], fcol.ap[0:64, 4 + 2 * l_:5 + 2 * l_], fcol.ap[0:64, 2 * l_:2 * l_ + 1],
                   ALU.mult, r=[fcol.dep], w=[fcol.dep])
            h1T = AR.alloc([nfilt], F32, "h1T")
            sA = [AR.alloc([512], F32, "sA%d" % i) for i in range(2)]
            sB = [AR.alloc([512], F32, "sB%d" % i) for i in range(2)]
            ncb = max(1, nfilt // 512)
            cw = min(512, nfilt)
            for l_ in range(2):
                for cbk in range(ncb):
                    cs = slice(cbk * cw, (cbk + 1) * cw)
                    fb_ = 6 + (cbk % 2)
                    if l_ == 0:
                        MM(ps[fb_].ap[0:64, 0:cw], w1t.ap[0:33, 0:64], zT.ap[0:33, cs], r=[w1t.dep, zT.dep], w=[ps[fb_].dep])
                    else:
                        MM(ps[fb_].ap[0:64, 0:cw], w2t.ap[0:64, 0:64], h1T.ap[0:64, cs], r=[w2t.dep, h1T.dep], w=[ps[fb_].dep])
                    s_ = sA[cbk % 2]
                    q_ = sB[cbk % 2]
                    ACTF(s_.ap[0:64, 0:cw], ps[fb_].ap[0:64, 0:cw], AF.Sin, r=[ps[fb_].dep, fcol.dep], w=[s_.dep],
                         scale=fcol.ap[0:64, 4 + 2 * l_:5 + 2 * l_], bias=fcol.ap[0:64, 5 + 2 * l_:6 + 2 * l_])
                    TT("dve", q_.ap[0:64, 0:cw], s_.ap[0:64, 0:cw], s_.ap[0:64, 0:cw], ALU.mult, r=[s_.dep], w=[q_.dep])
                    TS("dve", q_.ap[0:64, 0:cw], q_.ap[0:64, 0:cw], -4.0, 3.0, ALU.mult, ALU.add, r=[q_.dep], w=[q_.dep])
                    if l_ == 0:
                        TT("dve", h1T.ap[0:64, cs], q_.ap[0:64, 0:cw], s_.ap[0:64, 0:cw], ALU.mult, r=[q_.dep, s_.dep], w=[h1T.dep])
                    else:
                        TT("dve", h2T.ap[0:64, cs], q_.ap[0:64, 0:cw], s_.ap[0:64, 0:cw], ALU.mult, r=[q_.dep, s_.dep], w=[h2T.dep])
            AR.release(fm_)
            if v == 1:
                seamc = AR.alloc([4], F32, "seamc")
                DMA(seamc.ap, I["seamc"], w=[seamc.dep])
                cw02 = AR.alloc([8], F32, "cw02")

            XB = [2, 3, 6, 7]

            def fwd2d(chunks, FAt, consume, consume_pe=None):
                s1p = [[AR.alloc([2, 2, 256], BF16, "s1p_%d_%d" % (i, c_)) for c_ in range(len(chunks))] for i in range(2)]

                def stage1(g):
                    for ci_, (U, sn) in enumerate(chunks):
                        M = sn * cg
                        sbk = (g % 2) if ci_ == 0 else 4 + (g % 2)
                        MM(ps[sbk].ap[0:M, :], U.ap[:, cg * g:cg * (g + 1), :].rearrange("p c s -> p (c s)"), Fb.ap,
                           r=[U.dep, Fb.dep], w=[ps[sbk].dep])
                        S = s1p[(g // 2) % 2][ci_]
                        CP("act" if ci_ == 0 else "dve", S.ap[0:M, :, g % 2, :], ps[sbk].ap[0:M, :].rearrange("p (t b) -> p t b", b=256),
                           r=[ps[sbk].dep], w=[S.dep])

                def stage2(p):
                    banks = []
                    for h in range(nh):
                        xb_ = XB[(2 * p + h) % 4]
                        k_ = 0
                        nt_ = 2 * len(chunks)
                        for ci_, (U, sn) in enumerate(chunks):
                            M = sn * cg
                            S = s1p[p % 2][ci_]
                            for (ti_, tm_) in ((0, 0), (2, 1)):
                                MM(ps[xb_].ap[:, :], FAt[ci_].ap[0:M, h, ti_, :], S.ap[0:M, tm_, :, :].rearrange("p g b -> p (g b)"),
                                   start=(k_ == 0), stop=(k_ == nt_ - 1), r=[FAt[ci_].dep, S.dep], w=[ps[xb_].dep])
                                k_ += 1
                        banks.append(xb_)
                    return banks

                npair = ng // 2
                stage1(0)
                stage1(1)
                for p in range(npair):
                    if p + 1 < npair:
                        stage1(2 * p + 2)
                        stage1(2 * p + 3)
                    banks = stage2(p)
                    for h in range(nh):
                        consume(2 * p + h, banks[h])
                    if consume_pe is not None and p >= 1:
                        consume_pe(p - 1)
                if consume_pe is not None:
                    consume_pe(npair - 1)

            y2hy_cols = len(out2)
            for cc in range(4):
                ccm = AR.mark()
                Vt = AR.alloc([128, nb], BF16, "Vt")
                X1t = AR.alloc([nb, 128], BF16, "X1t")
                X2t = AR.alloc([nb, 128], BF16, "X2t")
                Z1t = AR.alloc([128, nb], BF16, "Z1t")
                s1m = AR.mark()
                uins = [AR.alloc([ntok], BF16, "uin%d" % i) for i in range(2)]
                uo = AR.alloc([ntok], F32, "uo")
                for idx, dst in enumerate([Vt, X1t, X2t]):
                    rr = idx * 4 + cc
                    ui = uins[idx % 2]
                    if v == 1 and idx == 2:
                        MEMSET("pool", ui.ap[:, 1536:3584], 0.0, w=[ui.dep])
                        DMA(ui.ap[:, 0:1536], Uscr[v][rr * 128:(rr + 1) * 128, 0:1536], r=[Uscr_dep[v]], w=[ui.dep])
                        DMA(ui.ap[:, 3584:4096], Uscr[v][rr * 128:(rr + 1) * 128, 3584:4096], r=[Uscr_dep[v]], w=[ui.dep])
                    else:
                        DMA(ui.ap, Uscr[v][rr * 128:(rr + 1) * 128, :], r=[Uscr_dep[v]], w=[ui.dep])
                    w0c, w1c, w2c = vcol("hy_conv_w", rr), vcol("hy_conv_w", 12 + rr), vcol("hy_conv_w", 24 + rr)
                    TS("dve", uo.ap, ui.ap, w1c, vcol("hy_conv_b", rr), ALU.mult, ALU.add, r=[ui.dep, vec.dep], w=[uo.dep])
                    if v == 0:
                        for s_ in range(4):
                            a0 = s_ * 256
                            STT(uo.ap[:, a0 + 1:a0 + 256], ui.ap[:, a0:a0 + 255], w0c, uo.ap[:, a0 + 1:a0 + 256], ALU.mult, ALU.add,
                                r=[ui.dep, uo.dep, vec.dep], w=[uo.dep])
                            STT(uo.ap[:, a0:a0 + 255], ui.ap[:, a0 + 1:a0 + 256], w2c, uo.ap[:, a0:a0 + 255], ALU.mult, ALU.add,
                                r=[ui.dep, uo.dep, vec.dep], w=[uo.dep])
                    else:
                        STT(uo.ap[:, 1:ntok], ui.ap[:, 0:ntok - 1], w0c, uo.ap[:, 1:ntok], ALU.mult, ALU.add,
                            r=[ui.dep, uo.dep, vec.dep], w=[uo.dep])
                        STT(uo.ap[:, 0:ntok - 1], ui.ap[:, 1:ntok], w2c, uo.ap[:, 0:ntok - 1], ALU.mult, ALU.add,
                            r=[ui.dep, uo.dep, vec.dep], w=[uo.dep])
                        TS("dve", cw02.ap[:, 0:4], seamc.ap, w0c, None, ALU.mult, r=[seamc.dep, vec.dep], w=[cw02.dep])
                        TS("dve", cw02.ap[:, 4:8], seamc.ap, w2c, None, ALU.mult, r=[seamc.dep, vec.dep], w=[cw02.dep])
                        for si in range(4):
                            s_ = si * 1024
                            sm1 = (s_ - 1) % ntok
                            STT(uo.ap[:, s_:s_ + 1], ui.ap[:, sm1:sm1 + 1], cw02.ap[:, si:si + 1], uo.ap[:, s_:s_ + 1], ALU.mult, ALU.add,
                                r=[ui.dep, uo.dep, cw02.dep], w=[uo.dep])
                            STT(uo.ap[:, sm1:sm1 + 1], ui.ap[:, s_:s_ + 1], cw02.ap[:, 4 + si:5 + si], uo.ap[:, sm1:sm1 + 1], ALU.mult, ALU.add,
                                r=[ui.dep, uo.dep, cw02.dep], w=[uo.dep])
                    if idx == 0 and cc == 0:
                        d_ = dbg_out("u%d" % v, [128, ntok])
                        if d_ is not None:
                            DMA(d_, uo.ap, r=[uo.dep])
                    for b4 in range(nb // 4):
                        tb = b4 % 2
                        for bi in range(4):
                            blk = b4 * 4 + bi
                            TR(ps[tb].ap[:, bi * 128:(bi + 1) * 128], uo.ap[:, blk * 128:(blk + 1) * 128], ident_f.ap,
                               r=[uo.dep, ident_f.dep], w=[ps[tb].dep])
                        if idx == 0:
                            CP("act" if b4 % 2 else "dve", dst.ap[:, :, b4 * 4:b4 * 4 + 4],
                               ps[tb].ap.rearrange("p (b c) -> p c b", c=128), r=[ps[tb].dep], w=[dst.dep])
                        else:
                            CP("act" if b4 % 2 else "dve", dst.ap[:, b4 * 4:b4 * 4 + 4, :],
                               ps[tb].ap.rearrange("p (b c) -> p b c", c=128), r=[ps[tb].dep], w=[dst.dep])
                AR.release(s1m)
                for o in range(2):
                    om = AR.mark()
                    Hs = AR.alloc([ng, nh, 2, 128], BF16, "Hs")
                    km = AR.mark()
                    if v == 0:
                        UK4 = AR.alloc([128, 8], BF16, "UK4")
                        MEMSET("pool", UK4.ap[:, :, nks:8], 0.0, w=[UK4.dep])
                        UKa = Buf(UK4.ap[:, :, 0:nfb], "UKa")
                        UKb = Buf(UK4.ap[:, :, nfb:nks], "UKb")
                        UKa.dep = UK4.dep
                        UKb.dep = UK4.dep
                        kch_ = [(UK4, 8)]
                    else:
                        UKa = AR.alloc([128, nfb], BF16, "UKa")
                        UKb = AR.alloc([128, nfb], BF16, "UKb")
                        kch_ = [(UKa, nfb), (UKb, nfb)]
                    hbk = AR.alloc([nfb, 128], BF16, "hbk")
                    wint = AR.alloc([nfb, 128], BF16, "wint")
                    DMA(wint.ap, I["win_" + tsuf].rearrange("(a p) c -> p a c", p=128)[:, :, cc * 128:(cc + 1) * 128], w=[wint.dep])
                    skr = AR.alloc([128], F32, "skr")
                    DMA(skr.ap[0:1, :], I["filt_bias"][o:o + 1, cc * 128:(cc + 1) * 128], w=[skr.dep])
                    for dr in range(2):
                        col0 = (o * 2 + dr) * 512 + cc * 128
                        dstb = UKa if dr == 0 else hbk
                        ngrp = (nfb + 3) // 4
                        for b4 in range(ngrp):
                            nb4 = min(4, nfb - b4 * 4)
                            fb_ = 6 + (b4 % 2)
                            for bi in range(nb4):
                                blk = b4 * 4 + bi
                                MM(ps[fb_].ap[:, bi * 128:(bi + 1) * 128], h2T.ap[0:64, blk * 128:(blk + 1) * 128],
                                   W3b.ap[0:64, col0:col0 + 128], r=[h2T.dep, W3b.dep], w=[ps[fb_].dep])
                            if dr == 0:
                                TT("dve", UKa.ap[:, :, b4 * 4:b4 * 4 + nb4],
                                   ps[fb_].ap[:, 0:nb4 * 128].rearrange("p (b c) -> p c b", c=128),
                                   wint.ap[:, b4 * 4:b4 * 4 + nb4, :].rearrange("p b c -> p c b"),
                                   ALU.mult, r=[ps[fb_].dep, wint.dep], w=[dstb.dep])
                            else:
                                TT("dve", hbk.ap[:, b4 * 4:b4 * 4 + nb4, :],
                                   ps[fb_].ap[:, 0:nb4 * 128].rearrange("p (b c) -> p b c", c=128), wint.ap[:, b4 * 4:b4 * 4 + nb4, :],
                                   ALU.mult, r=[ps[fb_].dep, wint.dep], w=[dstb.dep])
                    TT("dve", UKa.ap[0:1, :, 0], UKa.ap[0:1, :, 0], skr.ap[0:1, :], ALU.add, r=[UKa.dep, skr.dep], w=[UKa.dep])
                    if o == 0 and cc == 0:
                        d_ = dbg_out("hf%d" % v, [128, nfb * 128])
                        if d_ is not None:
                            dtmp = AR.alloc([nfb, 128], F32, "dtmp")
                            CP("dve", dtmp.ap, UKa.ap.rearrange("p c b -> p b c"), r=[UKa.dep], w=[dtmp.dep])
                            DMA(d_, dtmp.ap.rearrange("p b c -> p (b c)"), r=[dtmp.dep])
                    for b4 in range((nfb + 3) // 4):
                        nb4 = min(4, nfb - b4 * 4)
                        fb_ = 6 + (b4 % 2)
                        for bi in range(nb4):
                            ip = b4 * 4 + bi + 1
                            last = (ip == nfb)
                            MM(ps[fb_].ap[:, bi * 128:(bi + 1) * 128], R1T.ap, hbk.ap[:, ip - 1, :], start=True, stop=last,
                               r=[R1T.dep, hbk.dep], w=[ps[fb_].dep])
                            if not last:
                                MM(ps[fb_].ap[:, bi * 128:(bi + 1) * 128], E0T.ap, hbk.ap[:, ip, :], start=False, stop=True,
                                   r=[E0T.dep, hbk.dep], w=[ps[fb_].dep])
                        CP("act", UKb.ap[:, :, b4 * 4:b4 * 4 + nb4],
                           ps[fb_].ap[:, 0:nb4 * 128].rearrange("p (b c) -> p c b", c=128), r=[ps[fb_].dep], w=[UKb.dep])

                    def consume_k(g, xb_):
                        CP("act" if g % 2 else "dve", Hs.ap[:, g].rearrange("p h r b -> p (h r b)"), ps[xb_].ap[:, 0:nh * 256],
                           r=[ps[xb_].dep], w=[Hs.dep])
                    fwd2d(kch_, FAk, consume_k)
                    AR.release(km)
                    src = Vt if o == 0 else Z1t
                    outs = out1 if o == 0 else out2
                    nout = len(outs)
                    npos = 2 * nout
                    GA = GA1 if o == 0 else GA2
                    ncol = 2 * npos * cg
                    Zt = AR.alloc([2, npos, 128], BF16, "Zt")
                    tAB = [AR.alloc([nh, 2, 128], BF16, "tAB%d" % i) for i in range(2)]
                    tCD = [AR.alloc([nh, 2, 128], BF16, "tCD%d" % i) for i in range(2)]
                    Yb = [AR.alloc([nh, 2, 128], BF16, "Yb%d" % i) for i in range(6)]

                    def consume_d(g, xb_):
                        X4 = ps[xb_].ap[:, 0:nh * 256].rearrange("p (h r b) -> p h r b", r=2, b=128)
                        H4 = Hs.ap[:, g]
                        ab, cd, yb = tAB[g % 2], tCD[g % 2], Yb[g % 6]
                        TT("dve", ab.ap, X4, H4, ALU.mult, r=[ps[xb_].dep, Hs.dep], w=[ab.dep])
                        TT("dve", cd.ap[:, :, 0, :], X4[:, :, 0, :], H4[:, :, 1, :], ALU.mult, r=[ps[xb_].dep, Hs.dep], w=[cd.dep])
                        TT("dve", cd.ap[:, :, 1, :], X4[:, :, 1, :], H4[:, :, 0, :], ALU.mult, r=[ps[xb_].dep, Hs.dep], w=[cd.dep])
                        TT("pool", yb.ap[:, :, 0, :], ab.ap[:, :, 0, :], ab.ap[:, :, 1, :], ALU.subtract, r=[ab.dep], w=[yb.dep])
                        TT("pool", yb.ap[:, :, 1, :], cd.ap[:, :, 0, :], cd.ap[:, :, 1, :], ALU.add, r=[cd.dep], w=[yb.dep])

                    def consume_d_pe(p):
                        for gi in range(2):
                            g = 2 * p + gi
                            zb = 4 + gi
                            k_ = 0
                            for h in range(nh):
                                yb = Yb[(2 * p + h) % 6]
                                for ri in range(2):
                                    MM(ps[zb].ap[:, 0:ncol], yb.ap[:, gi, ri, :], GA.ap[:, h, ri, :], start=(k_ == 0), stop=(k_ == 2 * nh - 1),
                                       r=[yb.dep, GA.dep], w=[ps[zb].dep])
                                    k_ += 1
                            CP("act", Zt.ap[:, :, :, cg * g:cg * (g + 1)], ps[zb].ap[:, 0:ncol].rearrange("p (r n c) -> p r n c", r=2, c=cg),
                               r=[ps[zb].dep], w=[Zt.dep])
                    fwd2d([(src, nb)], [FAd], consume_d, consume_d_pe)
                    m0 = 0
                    bi_ = 0
                    while m0 < nout:
                        mn = min(4, nout - m0)
                        yb_ = 6 + (bi_ % 2)
                        bi_ += 1
                        ops_ = [(0, 0, m0), (1, 1, m0), (2, 0, nout + m0), (3, 1, nout + m0)]
                        for k_, (ci_, ri, n0) in enumerate(ops_):
                            MM(ps[yb_].ap[:, 0:mn * 128], CBt.ap[:, ci_, :], Zt.ap[:, ri, n0:n0 + mn, :], start=(k_ == 0), stop=(k_ == 3),
                               r=[CBt.dep, Zt.dep], w=[ps[yb_].dep])
                        yv = ps[yb_].ap[:, 0:mn * 128].rearrange("p (m c) -> p m c", c=128)
                        if o == 0:
                            TT("dve", Z1t.ap[:, :, m0:m0 + mn], yv.rearrange("p m c -> p c m"),
                               X1t.ap[:, m0:m0 + mn, :].rearrange("p m c -> p c m"), ALU.mult,
                               r=[ps[yb_].dep, X1t.dep], w=[Z1t.dep])
                        else:
                            mm_ = 0
                            while mm_ < mn:
                                sl = outs[m0 + mm_]
                                run = 1
                                while mm_ + run < mn and outs[m0 + mm_ + run] == sl + run:
                                    run += 1
                                TT("dve", y2hy.ap[:, m0 + mm_:m0 + mm_ + run, cc * 128:(cc + 1) * 128], yv[:, mm_:mm_ + run, :],
                                   X2t.ap[:, sl:sl + run, :], ALU.mult, r=[ps[yb_].dep, X2t.dep], w=[y2hy.dep])
                                mm_ += run
                        m0 += mn
                    AR.release(om)
                    if o == 0 and cc == 0:
                        d_ = dbg_out("z1_%d" % v, [128, nb * 128])
                        if d_ is not None:
                            dtmp2 = AR.alloc([nb, 128], F32, "dtmp2")
                            CP("dve", dtmp2.ap, Z1t.ap.rearrange("p c b -> p b c"), r=[Z1t.dep], w=[dtmp2.dep])
                            DMA(d_, dtmp2.ap.rearrange("p b c -> p (b c)"), r=[dtmp2.dep])
                AR.release(ccm)
            AR.release(hm)
            d_ = dbg_out("hyo%d" % v, [128, len(out2) * 512])
            if d_ is not None:
                DMA(d_, y2hy.ap.rearrange("p t f -> p (t f)"), r=[y2hy.dep])

            mgm = AR.mark()
            Wout = AR.alloc([KC, D], BF16, "Wout")
            g1bc = AR.alloc([D], F32, "g1bc")
            DMA(g1bc.ap, modD_flat[(v * 48 + 16) * 128:(v * 48 + 24) * 128].partition_broadcast(128), r=[modD_dep], w=[g1bc.dep])
            wout_d = I["w_out"].rearrange("(k p) n -> p k n", p=128)
            for k in range(KC):
                DMA(Wout.ap[:, k, :], wout_d[:, k, :], w=[Wout.dep], q="pool")
            gtmp = [AR.alloc([512], F32, "gtmp%d" % i) for i in range(2)]
            mixb = [AR.alloc([D], BF16, "mix%d" % i) for i in range(2)]
            mixT = [AR.alloc([KC, 128], BF16, "mixT%d" % i) for i in range(2)]
            xr = [AR.alloc([D], F32, "xr%d" % i) for i in range(2)]
            x1t = [AR.alloc([D], F32, "x1t%d" % i) for i in range(2)]
            hyh = AR.alloc([512], F32, "hyh")
            junk3 = AR.alloc([512], F32, "junk3")
            st3 = AR.alloc([8], F32, "st3")
            def merge_a(qi):
                halo = (v == 1 and qi == 8)
                m_ = 2 if halo else 128
                mx = mixb[qi % 2]
                ACTF(junk3.ap[0:m_, :], O_all.ap[0:m_, qi, :], AF.Square, r=[O_all.dep], w=[junk3.dep, st3.dep],
                     accum_out=st3.ap[0:m_, 0:1])
                ACTF(st3.ap[0:m_, 1:2], st3.ap[0:m_, 0:1], AF.Sqrt, r=[st3.dep, epsb.dep], w=[st3.dep], scale=1.0 / 512,
                     bias=epsb.ap[0:m_, 0:1])
                RECIP(st3.ap[0:m_, 2:3], st3.ap[0:m_, 1:2], r=[st3.dep], w=[st3.dep])
                STT(mx.ap[0:m_, 0:512], O_all.ap[0:m_, qi, :], st3.ap[0:m_, 2:3], gatt_bc.ap[0:m_, :], ALU.mult, ALU.mult,
                    r=[O_all.dep, st3.dep, gatt_bc.dep], w=[mx.dep])
                if halo:
                    MM(ps[7].ap[0:2, :], sel.ap[:, 0:2], y2hy.ap[:, 8, :], start=True, stop=False, r=[sel.dep, y2hy.dep], w=[ps[7].dep])
                    MM(ps[7].ap[0:2, :], sel.ap[:, 2:4], y2hy.ap[:, 9, :], start=False, stop=True, r=[sel.dep, y2hy.dep], w=[ps[7].dep])
                    CP("act", hyh.ap[0:2, :], ps[7].ap[0:2, :], r=[ps[7].dep], w=[hyh.dep])
                    hsrc, hdep = hyh.ap[0:2, :], hyh.dep
                else:
                    hsrc, hdep = y2hy.ap[0:m_, qi, :], y2hy.dep
                ACTF(junk3.ap[0:m_, :], hsrc, AF.Square, r=[hdep], w=[junk3.dep, st3.dep], accum_out=st3.ap[0:m_, 3:4])
                ACTF(st3.ap[0:m_, 4:5], st3.ap[0:m_, 3:4], AF.Sqrt, r=[st3.dep, epsb.dep], w=[st3.dep], scale=1.0 / 512,
                     bias=epsb.ap[0:m_, 0:1])
                RECIP(st3.ap[0:m_, 5:6], st3.ap[0:m_, 4:5], r=[st3.dep], w=[st3.dep])
                STT(mx.ap[0:m_, 512:1024], hsrc, st3.ap[0:m_, 5:6], ghy_bc.ap[0:m_, :], ALU.mult, ALU.mult,
                    r=[hdep, st3.dep, ghy_bc.dep], w=[mx.dep])
                xrt = xr[qi % 2]
                if halo:
                    DMA(xrt.ap[0:1, :], xin[4095:4096, :], w=[xrt.dep])
                    DMA(xrt.ap[1:2, :], xin[1024:1025, :], w=[xrt.dep])
                else:
                    DMA(xrt.ap, xin[qi * 128:(qi + 1) * 128, :], w=[xrt.dep])

            def merge_b(qi):
                halo = (v == 1 and qi == 8)
                m_ = 2 if halo else 128
                mx = mixb[qi % 2]
                tb = qi % 2
                mT = mixT[qi % 2]
                for k in range(KC):
                    TR(psb(tb)[:, k * 128:k * 128 + m_], mx.ap[0:m_, k * 128:(k + 1) * 128], ident_b.ap[0:m_, 0:m_],
                       r=[mx.dep, ident_b.dep], w=[ps[tb].dep])
                CP("act", mT.ap[:, :, 0:m_], psb(tb).rearrange("p (k t) -> p k t", t=128)[:, :, 0:m_], r=[ps[tb].dep], w=[mT.dep])
                xrt = xr[qi % 2]
                xo = x1t[qi % 2]
                for half in range(2):
                    ob = 2 + 2 * (qi % 2) + half
                    for k in range(KC):
                        MM(ps[ob].ap[0:m_, :], mT.ap[:, k, 0:m_], Wout.ap[:, k, half * 512:(half + 1) * 512], start=(k == 0),
                           stop=(k == KC - 1), r=[mT.dep, Wout.dep], w=[ps[ob].dep])
                    gt = gtmp[half]
                    TT("dve", gt.ap[0:m_, :], ps[ob].ap[0:m_, :], g1bc.ap[0:m_, half * 512:(half + 1) * 512], ALU.mult,
                       r=[ps[ob].dep, g1bc.dep], w=[gt.dep])
                    TT("pool", xo.ap[0:m_, half * 512:(half + 1) * 512], gt.ap[0:m_, :], xrt.ap[0:m_, half * 512:(half + 1) * 512],
                       ALU.add, r=[gt.dep, xrt.dep], w=[xo.dep])
                r0 = 1024 if halo else qi * 128
                DMA(x1scr[v][r0:r0 + m_, :], xo.ap[0:m_, :], r=[xo.dep], w=[x1scr_dep[v]], q="pool")

            merge_a(0)
            for qi in range(nqt):
                if qi + 1 < nqt:
                    merge_a(qi + 1)
                merge_b(qi)
            AR.release(gm)

        def load_ffn_weights():
            Wup = AR.alloc([KC, 2 * DFF], BF16, "Wup")
            Wdn = AR.alloc([22, D], BF16, "Wdn")
            wup_d = I["w_up"].rearrange("(k p) n -> p k n", p=128)
            for k in range(KC):
                for hh in range(2):
                    DMA(Wup.ap[:, k, hh * DFF:(hh + 1) * DFF], wup_d[:, k, hh * DFF:(hh + 1) * DFF], w=[Wup.dep], q="pool")
            wdn_d = I["w_down"].rearrange("(i p) n -> p i n", p=128)
            for i in range(0, 22, 2):
                DMA(Wdn.ap[:, i:i + 2, :], wdn_d[:, i:i + 2, :], w=[Wdn.dep], q="pool")
            return Wup, Wdn

        def run_ffn(v, Wup, Wdn):
            nqt = 8 if v == 0 else 9
            gm = AR.mark()
            ncolT = 1024 if v == 0 else 1026
            h2Tt = AR.alloc([KC, ncolT], BF16, "h2Tt")
            g2bc = AR.alloc([D], F32, "g2bc")
            DMA(g2bc.ap, modD_flat[(v * 48 + 40) * 128:(v * 48 + 48) * 128].partition_broadcast(128), r=[modD_dep], w=[g2bc.dep])
            n2m = AR.mark()
            xts2 = [AR.alloc([D], F32, "x2t%d" % i) for i in range(2)]
            xns2 = [AR.alloc([D], BF16, "x2n%d" % i) for i in range(2)]
            junk4 = AR.alloc([D], F32, "junk4")
            st4 = AR.alloc([4], F32, "st4")

            def n2_pre(qi):
                halo = (v == 1 and qi == 8)
                m_ = 2 if halo else 128
                r0 = 1024 if halo else qi * 128
                xt = xts2[qi % 2]
                xn = xns2[qi % 2]
                DMA(xt.ap[0:m_, :], x1scr[v][r0:r0 + m_, :], r=[x1scr_dep[v]], w=[xt.dep])
                ACTF(junk4.ap[0:m_, :], xt.ap[0:m_, :], AF.Square, r=[xt.dep], w=[junk4.dep, st4.dep], accum_out=st4.ap[0:m_, 0:1])
                ACTF(st4.ap[0:m_, 1:2], st4.ap[0:m_, 0:1], AF.Sqrt, r=[st4.dep, epsb.dep], w=[st4.dep], scale=1.0 / D,
                     bias=epsb.ap[0:m_, 0:1])
                RECIP(st4.ap[0:m_, 2:3], st4.ap[0:m_, 1:2], r=[st4.dep], w=[st4.dep])
                TS("dve", xn.ap[0:m_, :], xt.ap[0:m_, :], st4.ap[0:m_, 2:3], None, ALU.mult, r=[xt.dep, st4.dep], w=[xn.dep])

            def n2_mid(qi):
                halo = (v == 1 and qi == 8)
                m_ = 2 if halo else 128
                xn = xns2[qi % 2]
                tb = qi % 2
                for k in range(KC):
                    TR(psb(tb)[:, k * 128:k * 128 + m_], xn.ap[0:m_, k * 128:(k + 1) * 128], ident_b.ap[0:m_, 0:m_],
                       r=[xn.dep, ident_b.dep], w=[ps[tb].dep])
                pv3 = psb(tb).rearrange("p (k t) -> p k t", t=128)
                for k in range(KC):
                    if halo:
                        pairs = [(h2Tt.ap[:, k, 0:1], pv3[:, k, 0:1]), (h2Tt.ap[:, k, 1025:1026], pv3[:, k, 1:2])]
                    elif v == 0:
                        pairs = [(h2Tt.ap[:, k, qi * 128:(qi + 1) * 128], pv3[:, k, :])]
                    else:
                        pairs = [(h2Tt.ap[:, k, 1 + qi * 128:1 + (qi + 1) * 128], pv3[:, k, :])]
                    for (o_, i_) in pairs:
                        if qi % 2 == 0:
                            TS("dve", o_, i_, A2.ap[:, k, v:v + 1], mod.ap[:, v, 24 + k:25 + k], ALU.mult, ALU.add,
                               r=[ps[tb].dep, A2.dep, mod.dep], w=[h2Tt.dep])
                        else:
                            ACTF(o_, i_, AF.Identity, r=[ps[tb].dep, A2.dep, mod.dep], w=[h2Tt.dep], scale=A2.ap[:, k, v:v + 1],
                                 bias=mod.ap[:, v, 24 + k:25 + k])

            n2_pre(0)
            for qi in range(nqt):
                if qi + 1 < nqt:
                    n2_pre(qi + 1)
                n2_mid(qi)
            AR.release(n2m)
            actT = AR.alloc([22, 256], BF16, "actT")
            wmk = AR.alloc([2], F32, "wmk")
            cab = [AR.alloc([256], F32, "cab%d" % i) for i in range(4)]
            sab = [AR.alloc([256], F32, "sab%d" % i) for i in range(2)]
            x1w = [AR.alloc([D], F32, "x1w%d" % i) for i in range(2)]
            yw = [AR.alloc([D], F32, "yw%d" % i) for i in range(2)]
            gtmp2 = [AR.alloc([512], F32, "gtmp2_%d" % i) for i in range(2)]
            yout = O["y_prompt"] if v == 0 else O["y_sample"]
            cnt_ = 0
            for s_ in range(4):
                c0 = 256 * s_
                ncol = 256 if v == 0 else 258
                for i in range(22):
                    cas = []
                    for part in range(2):
                        ch = i + 22 * part
                        ub = cnt_ % 4
                        cnt_ += 1
                        for k in range(KC):
                            MM(ps[ub].ap[:, 0:ncol], Wup.ap[:, k, ch * 128:(ch + 1) * 128], h2Tt.ap[:, k, c0:c0 + ncol], start=(k == 0),
                               stop=(k == KC - 1), r=[Wup.dep, h2Tt.dep], w=[ps[ub].dep])
                        ca = cab[ub]
                        pu = ps[ub]
                        w0c, w1c, w2c = vcol("ffn_conv_w", ch), vcol("ffn_conv_w", NFF + ch), vcol("ffn_conv_w", 2 * NFF + ch)
                        bc_ = vcol("ffn_conv_b", ch)
                        if v == 0:
                            ACTF(ca.ap, pu.ap[:, 0:256], AF.Identity, r=[pu.dep, vec.dep], w=[ca.dep], scale=w1c, bias=bc_)
                            STT(ca.ap[:, 1:256], pu.ap[:, 0:255], w0c, ca.ap[:, 1:256], ALU.mult, ALU.add, r=[pu.dep, ca.dep, vec.dep], w=[ca.dep])
                            STT(ca.ap[:, 0:255], pu.ap[:, 1:256], w2c, ca.ap[:, 0:255], ALU.mult, ALU.add, r=[pu.dep, ca.dep, vec.dep], w=[ca.dep])
                        else:
                            ACTF(ca.ap, pu.ap[:, 1:257], AF.Identity, r=[pu.dep, vec.dep], w=[ca.dep], scale=w1c, bias=bc_)
                            lo_ = 1 if s_ == 0 else 0
                            hi_ = 255 if s_ == 3 else 256
                            STT(ca.ap[:, lo_:256], pu.ap[:, lo_:256], w0c, ca.ap[:, lo_:256], ALU.mult, ALU.add,
                                r=[pu.dep, ca.dep, vec.dep], w=[ca.dep])
                            STT(ca.ap[:, 0:hi_], pu.ap[:, 2:2 + hi_], w2c, ca.ap[:, 0:hi_], ALU.mult, ALU.add,
                                r=[pu.dep, ca.dep, vec.dep], w=[ca.dep])
                            if s_ == 0:
                                TS("dve", wmk.ap[:, 0:1], halomask.ap[:, 0:1], w0c, None, ALU.mult, r=[halomask.dep, vec.dep], w=[wmk.dep])
                                STT(ca.ap[:, 0:1], pu.ap[:, 0:1], wmk.ap[:, 0:1], ca.ap[:, 0:1], ALU.mult, ALU.add,
                                    r=[pu.dep, ca.dep, wmk.dep], w=[ca.dep])
                            if s_ == 3:
                                TS("dve", wmk.ap[:, 1:2], halomask.ap[:, 1:2], w2c, None, ALU.mult, r=[halomask.dep, vec.dep], w=[wmk.dep])
                                STT(ca.ap[:, 255:256], pu.ap[:, 257:258], wmk.ap[:, 1:2], ca.ap[:, 255:256], ALU.mult, ALU.add,
                                    r=[pu.dep, ca.dep, wmk.dep], w=[ca.dep])
                        cas.append(ca)
                    sa = sab[i % 2]
                    ACTF(sa.ap, cas[0].ap, AF.Silu, r=[cas[0].dep], w=[sa.dep])
                    TT("dve", actT.ap[:, i, :], sa.ap, cas[1].ap, ALU.mult, r=[sa.dep, cas[1].dep], w=[actT.dep])
                for tl in range(2):
                    ti = s_ * 2 + tl
                    xw = x1w[ti % 2]
                    yo = yw[ti % 2]
                    DMA(xw.ap, x1scr[v][ti * 128:(ti + 1) * 128, :], r=[x1scr_dep[v]], w=[xw.dep])
                    for half in range(2):
                        db = 4 + tl * 2 + half
                        for i in range(22):
                            MM(ps[db].ap, actT.ap[:, i, tl * 128:(tl + 1) * 128], Wdn.ap[:, i, half * 512:(half + 1) * 512], start=(i == 0),
                               stop=(i == 21), r=[actT.dep, Wdn.dep], w=[ps[db].dep])
                        gt2 = gtmp2[half]
                        TT("dve", gt2.ap, ps[db].ap, g2bc.ap[:, half * 512:(half + 1) * 512], ALU.mult,
                           r=[ps[db].dep, g2bc.dep], w=[gt2.dep])
                        TT("pool", yo.ap[:, half * 512:(half + 1) * 512], gt2.ap, xw.ap[:, half * 512:(half + 1) * 512], ALU.add,
                           r=[gt2.dep, xw.dep], w=[yo.dep])
                    DMA(yout[ti * 128:(ti + 1) * 128, :], yo.ap, r=[yo.dep], q="pool")
            AR.release(gm)

        for v_ in GROUPS:
            run_group(v_)
        Wup_, Wdn_ = load_ffn_weights()
        for v_ in GROUPS:
            run_ffn(v_, Wup_, Wdn_)
        P.emit()
    return nc, list(DBG.keys())


_CACHE = {}


def make_in_maps(inputs):
    f32 = lambda a: np.ascontiguousarray(np.asarray(a, dtype=np.float32))
    xpr = f32(inputs["x_prompt"])
    xsm = f32(inputs["x_sample"])
    maps = []
    for core in range(8):
        b, j = core // 4, core % 4
        m = {}
        m["xp"] = xpr[4 * core:4 * core + 4].reshape(TP, D)
        m["xs"] = np.ascontiguousarray(np.roll(xsm[b], -1024 * j, axis=0))
        m["ckv_c"] = f32(inputs["cache_ckv"])[b, 0]
        m["kpe_c"] = f32(inputs["cache_kpe"])[b, 0]
        m["cvec"] = np.ascontiguousarray(np.stack([f32(inputs["c_ctx"]), f32(inputs["c"])[b]], axis=1))
        for nm in ["w_ada", "b_ada", "g_mix", "w_in", "g_qa", "w_uq", "g_kva", "w_ukv", "g_q", "g_k", "hy_conv_w",
                   "hy_conv_b", "filt_w1", "filt_b1", "filt_freq1", "filt_w2", "filt_b2", "filt_freq2", "filt_w3",
                   "filt_bias", "g_out_att", "g_out_hy", "w_out", "g_ffn", "w_up", "ffn_conv_w", "ffn_conv_b", "w_down"]:
            a = f32(inputs[nm])[0]
            if nm in ("hy_conv_w", "ffn_conv_w"):
                a = a.reshape(-1)
            m[nm] = np.ascontiguousarray(a)
        m.update(host_tables(core))
        maps.append(m)
    return maps


def kernel(**inputs):
    if "nc" not in _CACHE:
        _CACHE["nc"] = build_program()[0]
    nc = _CACHE["nc"]
    maps = make_in_maps(inputs)
    res = run_bass_kernel_spmd(nc, maps, core_ids=list(range(8)))
    R = res.results
    y_prompt = np.concatenate([R[c]["y_prompt"].reshape(4, 256, D) for c in range(8)], axis=0)
    y_sample = np.stack([np.concatenate([R[b * 4 + j]["y_sample"] for j in range(4)], axis=0) for b in range(2)], axis=0)
    new_ckv = np.concatenate([R[c]["new_ckv"].reshape(4, 1, 256, KVL) for c in range(8)], axis=0)
    new_kpe = np.concatenate([R[c]["new_kpe"].reshape(4, 1, 256, ROPE) for c in range(8)], axis=0)
    return (y_prompt.astype(np.float32), y_sample.astype(np.float32), new_ckv.astype(np.float32), new_kpe.astype(np.float32))
```

```python
import math
from contextlib import ExitStack

import numpy as np
import ml_dtypes

import concourse.bass as bass
import concourse.mybir as mybir
from concourse.bass_utils import run_bass_kernel_spmd

F32 = mybir.dt.float32
BF16 = mybir.dt.bfloat16
AF = mybir.ActivationFunctionType
ALU = mybir.AluOpType
AX = mybir.AxisListType
NPBF = ml_dtypes.bfloat16

D = 1024
KC = 8
EPS = 1e-6
N_HEADS = 8
QK_NOPE = 64
ROPE = 32
QK = 96
VD = 64
QL = 384
KVL = 256
DHY = 512
DIN = 2208
DFF = 2816
NFF = 44
TP = 1024
TL = 4096
PAST = 512


class Dep:
    __slots__ = ("name", "w", "r", "excl")

    def __init__(self, name="", excl=False):
        self.name = name
        self.w = None
        self.r = {}
        self.excl = excl


class Prog:
    ENG = ("pe", "act", "dve", "pool", "sp")

    def __init__(self, nc, ndma_sems=8):
        self.nc = nc
        self.q = {e: [] for e in self.ENG}
        self.cnt = {e: 0 for e in self.ENG}
        self.known = {e: {} for e in self.ENG}
        self.sems = {}
        self.dma_ring = {}
        self.ndma_sems = ndma_sems

    def setup(self, stack):
        nc = self.nc
        for e in self.ENG:
            self.sems[e] = stack.enter_context(nc.semaphore("s_" + e))
        for e in ("sp", "act", "pool"):
            ring = [stack.enter_context(nc.semaphore("d_%s%d" % (e, i))) for i in range(self.ndma_sems)]
            self.dma_ring[e] = {"sems": ring, "vals": [0] * len(ring), "next": 0}

    def _need(self, eng, ev):
        sem, val, src = ev
        if src == eng and eng == "pe":
            return
        k = self.known[eng]
        if k.get(id(sem), 0) >= val:
            return
        k[id(sem)] = val
        self.q[eng].append(("wait", sem, val))

    def _deps(self, eng, reads, writes):
        for d in reads:
            if d.w is not None:
                self._need(eng, d.w)
        for d in writes:
            if d.w is not None:
                self._need(eng, d.w)
            for ev in d.r.values():
                self._need(eng, ev)

    def _commit(self, ev, reads, writes):
        for d in reads:
            old = d.r.get(id(ev[0]))
            if old is None or old[1] < ev[1]:
                d.r[id(ev[0])] = ev
        for d in writes:
            d.w = ev
            d.r = {}

    def op(self, eng, fn, reads=(), writes=()):
        for d in reads:
            if d.excl:
                for ev in d.r.values():
                    if ev[2] != eng:
                        self._need(eng, ev)
        self._deps(eng, reads, writes)
        self.cnt[eng] += 1
        ev = (self.sems[eng], self.cnt[eng], eng)
        self.q[eng].append(("op", fn, self.sems[eng], 1))
        self._commit(ev, reads, writes)
        return ev

    def dma(self, eng, fn, reads=(), writes=()):
        ring = self.dma_ring[eng]
        i = ring["next"]
        ring["next"] = (i + 1) % len(ring["sems"])
        sem = ring["sems"][i]
        if ring["vals"][i] > 0:
            self._need(eng, (sem, ring["vals"][i], "dma"))
        self._deps(eng, reads, writes)
        ring["vals"][i] += 16
        ev = (sem, ring["vals"][i], "dma")
        self.q[eng].append(("op", fn, sem, 16))
        self._commit(ev, reads, writes)
        return ev

    def barrier(self):
        for e in self.ENG:
            for e2 in self.ENG:
                if e2 != e and self.cnt[e2] > 0:
                    self._need(e, (self.sems[e2], self.cnt[e2], e2))
            for q in ("sp", "act", "pool"):
                ring = self.dma_ring[q]
                for sem, v in zip(ring["sems"], ring["vals"]):
                    if v > 0:
                        self._need(e, (sem, v, "dma"))

    def emit(self):
        nc = self.nc
        for e in ("sp", "act", "pool"):
            ring = self.dma_ring[e]
            for sem, v in zip(ring["sems"], ring["vals"]):
                if v > 0:
                    self._need("sp", (sem, v, "dma"))
        for e in ("pe", "act", "dve", "pool"):
            if self.cnt[e] > 0:
                self._need("sp", (self.sems[e], self.cnt[e], e))
        handles = {"pe": "tensor", "act": "scalar", "dve": "vector", "pool": "gpsimd", "sp": "sync"}
        with nc.Block() as block:
            for e in self.ENG:
                items = self.q[e]

                def body(engh, items=items):
                    for it in items:
                        if it[0] == "wait":
                            engh.wait_ge(it[1], it[2])
                        else:
                            it[1](engh).then_inc(it[2], it[3])
                getattr(block, handles[e])(body)


class Buf:
    def __init__(self, ap, name=""):
        self.ap = ap
        self.dep = Dep(name)

    def __getitem__(self, key):
        return self.ap[key]


class Arena:
    def __init__(self, nc, stack, name, nbytes):
        self.t = stack.enter_context(nc.sbuf_tensor(name, [128, nbytes // 4], F32))
        self.size = nbytes
        self.off = 0
        self.peak = 0
        self.hist = []

    def alloc(self, free_shape, dtype, name=""):
        n = 1
        for s in free_shape:
            n *= s
        esz = 4 if dtype == F32 else 2
        nb = (n * esz + 31) // 32 * 32
        assert self.off + nb <= self.size, "arena overflow %s %d+%d>%d" % (name, self.off, nb, self.size)
        a = self.t[:, self.off // 4:(self.off + nb) // 4]
        if dtype != F32:
            a = a.bitcast(dtype)
        a = a[:, 0:n]
        if len(free_shape) == 2:
            a = a.rearrange("p (a b) -> p a b", b=free_shape[1])
        elif len(free_shape) == 3:
            a = a.rearrange("p (a b c) -> p a b c", b=free_shape[1], c=free_shape[2])
        elif len(free_shape) == 4:
            a = a.rearrange("p (a b c d) -> p a b c d", b=free_shape[1], c=free_shape[2], d=free_shape[3])
        nbuf = Buf(a, name)
        lo, hi = self.off, self.off + nb
        for (o, e_, b) in self.hist:
            if o < hi and lo < e_:
                evs = list(b.dep.r.values())
                if b.dep.w is not None:
                    evs.append(b.dep.w)
                for ev in evs:
                    old = nbuf.dep.r.get(id(ev[0]))
                    if old is None or old[1] < ev[1]:
                        nbuf.dep.r[id(ev[0])] = ev
        self.hist.append((lo, hi, nbuf))
        self.off += nb
        self.peak = max(self.peak, self.off)
        return nbuf

    def mark(self):
        return self.off

    def release(self, m):
        self.off = m


def _rope_tables(ntok_positions):
    pos = np.asarray(ntok_positions)
    row = (pos // 64).astype(np.float32)
    col = (pos % 64).astype(np.float32)
    quarter = ROPE // 4
    freqs = (10000.0 ** (-np.arange(quarter, dtype=np.float32) / quarter)).astype(np.float32)
    ang_r = row[:, None] * freqs
    ang_c = col[:, None] * freqs
    ang = np.concatenate([ang_r, ang_r, ang_c, ang_c], axis=-1).astype(np.float32)
    return np.cos(ang).T.astype(np.float32), np.sin(ang).T.astype(np.float32)


def _rot32():
    R = np.zeros((32, 32), np.float32)
    for i in range(8):
        R[i, 8 + i] = -1.0
        R[8 + i, i] = 1.0
        R[16 + i, 24 + i] = -1.0
        R[24 + i, 16 + i] = 1.0
    return R


def _fa_table(pos, cg, na):
    ns = len(pos)
    at = 128 // cg
    nh = na * cg // 128
    T = np.zeros((ns * cg, nh, 3, 128), np.float64)
    a_t = np.arange(at)
    for s_, p in enumerate(pos):
        for h in range(nh):
            phi = 2 * np.pi * p * (h * at + a_t) / float(na)
            for c in range(cg):
                T[c * ns + s_, h, 0, a_t * cg + c] = np.cos(phi)
                T[c * ns + s_, h, 1, a_t * cg + c] = -np.sin(phi)
                T[c * ns + s_, h, 2, a_t * cg + c] = np.sin(phi)
    return T.astype(NPBF)


def _ga_table(posout, cg, na):
    npos = len(posout)
    at = 128 // cg
    nh = na * cg // 128
    G = np.zeros((128, nh, 2, 2 * npos * cg), np.float64)
    for h in range(nh):
        for a_t in range(at):
            alpha = h * at + a_t
            for n, p in enumerate(posout):
                psi = 2 * np.pi * p * alpha / float(na)
                cr, ci = np.cos(psi), np.sin(psi)
                for c in range(cg):
                    row = a_t * cg + c
                    cre = n * cg + c
                    cim = npos * cg + n * cg + c
                    G[row, h, 0, cre] = cr
                    G[row, h, 0, cim] = ci
                    G[row, h, 1, cre] = -ci
                    G[row, h, 1, cim] = cr
    return G.astype(NPBF)


def _filter_feats(n):
    t = np.linspace(0.0, 1.0, n, dtype=np.float32)[:, None]
    w = (2.0 * math.pi * np.arange(n, dtype=np.float32)[:, None] / n).astype(np.float32)
    bands = np.linspace(1e-4, 15.0, 16, dtype=np.float32)
    z = np.concatenate([t, np.cos(bands * w), -np.sin(bands * w)], axis=-1).astype(np.float32)
    max_decay = math.log(1e-2) / 0.3
    min_decay = math.log(1e-2) / 1.5
    deltas = np.abs(np.linspace(min_decay, max_decay, 512, dtype=np.float32))
    window = (np.exp(-t * deltas) + 0.05).astype(np.float32)
    return np.ascontiguousarray(z.T), window.astype(NPBF)


def _conv_tables(j):
    t = {}
    b = np.arange(128, dtype=np.float64)[:, None]
    be = np.arange(128, dtype=np.float64)[None, :]
    th = 2 * np.pi * b * (be + 0.5) / 256.0
    t["Fb"] = np.concatenate([np.cos(th), -np.sin(th), -np.sin(th), -np.cos(th)], axis=1).astype(NPBF)
    beK = np.arange(128, dtype=np.float64)[:, None]
    bp = np.arange(128, dtype=np.float64)[None, :]
    om_lo = 2 * np.pi * bp * (beK + 0.5) / 256.0
    om_hi = 2 * np.pi * (bp + 128) * (beK + 0.5) / 256.0
    for nm_, s in (("CB_p", 2.0 ** -11), ("CB_l", 2.0 ** -13)):
        t[nm_] = np.stack([s * np.cos(om_lo), -s * np.sin(om_lo), s * np.cos(om_hi), -s * np.sin(om_hi)], axis=1).astype(NPBF)
    r1 = np.zeros((128, 128), np.float32)
    for i in range(1, 128):
        r1[i, 128 - i] = 1.0
    e0 = np.zeros((128, 128), np.float32)
    e0[0, 0] = 1.0
    t["R1T"] = r1.astype(NPBF)
    t["E0T"] = e0.astype(NPBF)
    pos_p = [4 * (s_ // 2) + (s_ % 2) for s_ in range(8)]
    t["FAd_p"] = _fa_table(pos_p, 16, 16)
    t["FAk_p"] = _fa_table([0, 1, 15, 14, 0, 0, 0, 0], 16, 16)[None]
    po = pos_p + [(p - 1) % 16 for p in pos_p]
    t["GA1_p"] = _ga_table(po, 16, 16)
    t["GA2_p"] = _ga_table(po, 16, 16)
    pos_l = [(s_ + 8 * j) % 32 for s_ in range(32)]
    t["FAd_l"] = _fa_table(pos_l, 4, 64)
    t["FAk_l"] = np.stack([_fa_table(list(range(32)), 4, 64), _fa_table([64 - i for i in range(1, 33)], 4, 64)], axis=0)
    t["GA1_l"] = _ga_table(pos_l + [(p - 1) % 64 for p in pos_l], 4, 64)
    out2 = list(range(8)) + [31, 8]
    p2 = [pos_l[s_] for s_ in out2]
    t["GA2_l"] = _ga_table(p2 + [(p - 1) % 64 for p in p2], 4, 64)
    t["zT_p"], t["win_p"] = _filter_feats(256)
    t["zT_l"], t["win_l"] = _filter_feats(4096)
    seam = (4096 - 1024 * j) % 4096
    co = np.zeros(4, np.float32)
    for si in range(4):
        flag = 1.0 if si * 1024 == seam else 0.0
        co[si] = (1.0 - flag) if si == 0 else -flag
    t["seamc"] = np.tile(co[None, :], (128, 1)).astype(np.float32)
    hm = np.array([0.0 if j == 0 else 1.0, 0.0 if j == 3 else 1.0], np.float32)
    t["halomask"] = np.tile(hm[None, :], (128, 1)).astype(np.float32)
    return t


def host_tables(core):
    j = core % 4
    t = {}
    t["ident_f"] = np.eye(128, dtype=np.float32)
    t["ident_b"] = np.eye(128, dtype=np.float32).astype(NPBF)
    R = _rot32()
    sh = np.zeros((32, 96), np.float32)
    shr = np.zeros((32, 96), np.float32)
    r96 = np.zeros((96, 96), np.float32)
    for k in range(32):
        sh[k, 64 + k] = 1.0
        for m in range(32):
            shr[k, 64 + m] = R[m, k]
            r96[64 + k, 64 + m] = R[m, k]
    t["shiftT"] = sh.astype(NPBF)
    t["shiftrotT"] = shr.astype(NPBF)
    t["rot96T"] = r96.astype(NPBF)
    st_tok = (np.arange(TL) + 1024 * j) % TL
    ck, sk = _rope_tables(st_tok)
    t["cosk"] = np.concatenate([ck, np.ones((32, PAST), np.float32)], axis=1)
    t["sink"] = np.concatenate([sk, np.zeros((32, PAST), np.float32)], axis=1)
    qtok = np.concatenate([1024 * j + np.arange(1024), [(1024 * j - 1) % TL, (1024 * j + 1024) % TL]])
    cq, sq = _rope_tables(qtok)
    t["cosq"] = np.concatenate([np.ones((64, 1026), np.float32), cq], axis=0)
    t["sinq"] = np.concatenate([np.zeros((64, 1026), np.float32), sq], axis=0)
    t.update(_conv_tables(j))
    return t


VEC_LAYOUT = [("b_ada", 48), ("g_mix", 8), ("g_ffn", 8), ("g_qa", 3), ("hy_conv_w", 36), ("hy_conv_b", 12),
              ("ffn_conv_w", 132), ("ffn_conv_b", 44)]


def build_program(debug=(), GROUPS=(0, 1)):
    nc = bass.Bass("TRN2", target_bir_lowering=False)
    P = Prog(nc)

    def din(name, shape, dt=F32):
        return nc.dram_tensor(name, list(shape), dt, kind="ExternalInput").ap()

    def dout(name, shape, dt=F32):
        return nc.dram_tensor(name, list(shape), dt, kind="ExternalOutput").ap()

    I = {}
    I["xp"] = din("xp", [TP, D])
    I["xs"] = din("xs", [TL, D])
    I["ckv_c"] = din("ckv_c", [PAST, KVL])
    I["kpe_c"] = din("kpe_c", [PAST, ROPE])
    I["cvec"] = din("cvec", [D, 2])
    for nm, shp in [("w_ada", [D, 6 * D]), ("b_ada", [6 * D]), ("g_mix", [D]), ("w_in", [D, DIN]), ("g_qa", [QL]),
                    ("w_uq", [QL, N_HEADS * QK]), ("g_kva", [KVL]), ("w_ukv", [KVL, 1024]), ("g_q", [QK]),
                    ("g_k", [QK]), ("hy_conv_w", [3 * 1536]), ("hy_conv_b", [1536]), ("filt_w1", [33, 64]),
                    ("filt_b1", [64]), ("filt_freq1", [64]), ("filt_w2", [64, 64]), ("filt_b2", [64]),
                    ("filt_freq2", [64]), ("filt_w3", [64, 2048]), ("filt_bias", [2, 512]), ("g_out_att", [512]),
                    ("g_out_hy", [512]), ("w_out", [D, D]), ("g_ffn", [D]), ("w_up", [D, 2 * DFF]),
                    ("ffn_conv_w", [3 * 2 * DFF]), ("ffn_conv_b", [2 * DFF]), ("w_down", [DFF, D])]:
        I[nm] = din(nm, shp)
    I["shiftT"] = din("shiftT", [32, 96], BF16)
    I["shiftrotT"] = din("shiftrotT", [32, 96], BF16)
    I["rot96T"] = din("rot96T", [96, 96], BF16)
    I["cosk"] = din("cosk", [32, TL + PAST])
    I["sink"] = din("sink", [32, TL + PAST])
    I["cosq"] = din("cosq", [96, 1026])
    I["sinq"] = din("sinq", [96, 1026])
    I["Fb"] = din("Fb", [128, 512], BF16)
    I["CB_p"] = din("CB_p", [128, 4, 128], BF16)
    I["CB_l"] = din("CB_l", [128, 4, 128], BF16)
    I["R1T"] = din("R1T", [128, 128], BF16)
    I["E0T"] = din("E0T", [128, 128], BF16)
    I["FAd_p"] = din("FAd_p", [128, 2, 3, 128], BF16)
    I["FAd_l"] = din("FAd_l", [128, 2, 3, 128], BF16)
    I["FAk_p"] = din("FAk_p", [1, 128, 2, 3, 128], BF16)
    I["FAk_l"] = din("FAk_l", [2, 128, 2, 3, 128], BF16)
    I["GA1_p"] = din("GA1_p", [128, 2, 2, 512], BF16)
    I["GA2_p"] = din("GA2_p", [128, 2, 2, 512], BF16)
    I["GA1_l"] = din("GA1_l", [128, 2, 2, 512], BF16)
    I["GA2_l"] = din("GA2_l", [128, 2, 2, 160], BF16)
    I["zT_p"] = din("zT_p", [33, 256])
    I["zT_l"] = din("zT_l", [33, 4096])
    I["win_p"] = din("win_p", [256, 512], BF16)
    I["win_l"] = din("win_l", [4096, 512], BF16)
    I["seamc"] = din("seamc", [128, 4])
    I["halomask"] = din("halomask", [128, 2])
    I["ident_f"] = din("ident_f", [128, 128])
    I["ident_b"] = din("ident_b", [128, 128], BF16)

    O = {}
    O["y_prompt"] = dout("y_prompt", [TP, D])
    O["y_sample"] = dout("y_sample", [1024, D])
    O["new_ckv"] = dout("new_ckv", [TP, KVL])
    O["new_kpe"] = dout("new_kpe", [TP, ROPE])
    DBG = {}

    def dbg_out(name, shape):
        if name in debug:
            DBG[name] = dout("dbg_" + name, shape)
            return DBG[name]
        return None

    with ExitStack() as st:
        P.setup(st)
        AR = Arena(nc, st, "arena", 207 * 1024)
        ps = []
        for i in range(8):
            t_ = st.enter_context(nc.psum_tensor("psb%d" % i, [128, 512], F32))
            ps.append(Buf(t_[:], "ps%d" % i))
            ps[-1].dep.excl = True

        def psb(i):
            return ps[i].ap.bitcast(BF16)

        ident_f = AR.alloc([128], F32, "ident_f")
        ident_b = AR.alloc([128], BF16, "ident_b")
        P.dma("sp", lambda e: e.dma_start(out=ident_f.ap, in_=I["ident_f"]), writes=[ident_f.dep])
        P.dma("sp", lambda e: e.dma_start(out=ident_b.ap, in_=I["ident_b"]), writes=[ident_b.dep])
        epsb = AR.alloc([1], F32, "epsb")
        P.op("pool", lambda e: e.memset(epsb.ap, EPS), writes=[epsb.dep])
        ones_b = AR.alloc([128], BF16, "ones_b")
        P.op("pool", lambda e: e.memset(ones_b.ap, 1.0), writes=[ones_b.dep])
        ones_f = AR.alloc([128], F32, "ones_f")
        P.op("pool", lambda e: e.memset(ones_f.ap, 1.0), writes=[ones_f.dep])

        nrows = sum(n for _, n in VEC_LAYOUT)
        nblk = (nrows + 127) // 128
        vec = AR.alloc([nblk * 128], F32, "vec")
        voff = {}
        m0 = AR.mark()
        rows = AR.alloc([nblk, 128], F32, "vecrows")
        P.op("pool", lambda e: e.memset(rows.ap, 0.0), writes=[rows.dep])
        r = 0
        for nm, n in VEC_LAYOUT:
            voff[nm] = r
            src = I[nm].rearrange("(r p) -> r p", p=128)
            done = 0
            while done < n:
                blk, p0 = (r + done) // 128, (r + done) % 128
                cnt = min(n - done, 128 - p0)
                P.dma("sp", lambda e, blk=blk, p0=p0, cnt=cnt, src=src, done=done:
                      e.dma_start(out=rows.ap[p0:p0 + cnt, blk, :], in_=src[done:done + cnt, :]),
                      writes=[rows.dep])
                done += cnt
            r += n
        for blk in range(nblk):
            P.op("pe", lambda e, blk=blk: e.transpose(ps[7].ap[:, blk * 128:(blk + 1) * 128], rows.ap[:, blk, :], ident_f.ap),
                 reads=[rows.dep, ident_f.dep], writes=[ps[7].dep])
        P.op("dve", lambda e: e.tensor_copy(out=vec.ap, in_=ps[7].ap[:, 0:nblk * 128]), reads=[ps[7].dep], writes=[vec.dep])
        AR.release(m0)

        def vcol(nm, i):
            return vec.ap[:, voff[nm] + i: voff[nm] + i + 1]

        mod = AR.alloc([2, 48], F32, "mod")
        m0 = AR.mark()
        cv = AR.alloc([KC, 2], F32, "cv")
        P.dma("sp", lambda e: e.dma_start(out=cv.ap, in_=I["cvec"].rearrange("(k p) v -> p k v", p=128)), writes=[cv.dep])
        sc = AR.alloc([KC, 2], F32, "sc")
        P.op("act", lambda e: e.activation(out=sc.ap, in_=cv.ap, func=AF.Silu), reads=[cv.dep], writes=[sc.dep])
        wa = [AR.alloc([6 * D], F32, "wa%d" % i) for i in range(2)]
        wab = [AR.alloc([6 * D], BF16, "wab%d" % i) for i in range(2)]
        scb = AR.alloc([KC, 2], BF16, "scb")
        P.op("dve", lambda e: e.tensor_copy(out=scb.ap, in_=sc.ap), reads=[sc.dep], writes=[scb.dep])
        for k in range(KC):
            wb = wa[k % 2]
            wbb = wab[k % 2]
            P.dma("sp", lambda e, wb=wb, k=k: e.dma_start(out=wb.ap, in_=I["w_ada"][k * 128:(k + 1) * 128, :]), writes=[wb.dep])
            P.op("dve", lambda e, wb=wb, wbb=wbb: e.tensor_copy(out=wbb.ap[:, 0:2048], in_=wb.ap[:, 0:2048]), reads=[wb.dep], writes=[wbb.dep])
            P.op("act", lambda e, wb=wb, wbb=wbb: e.activation(out=wbb.ap[:, 2048:4096], in_=wb.ap[:, 2048:4096], func=AF.Copy),
                 reads=[wb.dep], writes=[wbb.dep])
            P.op("pool", lambda e, wb=wb, wbb=wbb: e.tensor_copy(out=wbb.ap[:, 4096:6144], in_=wb.ap[:, 4096:6144]), reads=[wb.dep], writes=[wbb.dep])
            for mc in range(48):
                P.op("pe", lambda e, wbb=wbb, mc=mc, k=k: e.matmul(
                    ps[6].ap[:, 2 * mc:2 * mc + 2], lhsT=wbb.ap[:, mc * 128:(mc + 1) * 128], rhs=scb.ap[:, k, :],
                    start=(k == 0 and mc == 0), stop=(k == KC - 1), skip_group_check=True), reads=[wbb.dep, scb.dep], writes=[ps[6].dep])
        P.op("dve", lambda e: e.tensor_tensor(
            out=mod.ap.rearrange("p v m -> p m v"), in0=ps[6].ap[:, 0:96].rearrange("p (m v) -> p m v", v=2),
            in1=vec.ap[:, voff["b_ada"]:voff["b_ada"] + 48].unsqueeze(2).to_broadcast([128, 48, 2]), op=ALU.add),
            reads=[ps[6].dep, vec.dep], writes=[mod.dep])
        AR.release(m0)
        A1 = AR.alloc([KC, 2], F32, "A1")
        A2 = AR.alloc([KC, 2], F32, "A2")
        for v in range(2):
            P.op("dve", lambda e, v=v: e.scalar_tensor_tensor(
                out=A1.ap[:, :, v], in0=mod.ap[:, v, 8:16], scalar=1.0, in1=vec.ap[:, voff["g_mix"]:voff["g_mix"] + 8],
                op0=ALU.add, op1=ALU.mult), reads=[mod.dep, vec.dep], writes=[A1.dep])
            P.op("dve", lambda e, v=v: e.scalar_tensor_tensor(
                out=A2.ap[:, :, v], in0=mod.ap[:, v, 32:40], scalar=1.0, in1=vec.ap[:, voff["g_ffn"]:voff["g_ffn"] + 8],
                op0=ALU.add, op1=ALU.mult), reads=[mod.dep, vec.dep], writes=[A2.dep])

        d2_ = dbg_out("vec", [128, nblk * 128])
        if d2_ is not None:
            P.dma("sp", lambda e: e.dma_start(out=d2_, in_=vec.ap), reads=[vec.dep])
        d_ = dbg_out("mod", [128, 96])
        if d_ is not None:
            P.dma("sp", lambda e: e.dma_start(out=d_, in_=mod.ap.rearrange("p v m -> p (v m)")), reads=[mod.dep])


        modD = nc.dram_tensor("modD", [96, 128], F32).ap()
        modD_dep = Dep("modD")
        m0 = AR.mark()
        modTt = AR.alloc([128], F32, "modTt")
        P.op("pe", lambda e: e.transpose(ps[7].ap[0:96, 0:128], mod.ap.rearrange("p v m -> p (v m)"), ident_f.ap),
             reads=[mod.dep, ident_f.dep], writes=[ps[7].dep])
        P.op("dve", lambda e: e.tensor_copy(out=modTt.ap[0:96, :], in_=ps[7].ap[0:96, 0:128]), reads=[ps[7].dep], writes=[modTt.dep])
        P.dma("sp", lambda e: e.dma_start(out=modD, in_=modTt.ap[0:96, :]), reads=[modTt.dep], writes=[modD_dep])
        AR.release(m0)
        modD_flat = modD.rearrange("r p -> (r p)")
        sel = AR.alloc([4], F32, "sel")
        P.op("pool", lambda e: e.memset(sel.ap, 0.0), writes=[sel.dep])
        P.op("pool", lambda e: e.tensor_copy(out=sel.ap[:, 0:1], in_=ident_f.ap[:, 127:128]), reads=[ident_f.dep], writes=[sel.dep])
        P.op("pool", lambda e: e.tensor_copy(out=sel.ap[:, 3:4], in_=ident_f.ap[:, 0:1]), reads=[ident_f.dep], writes=[sel.dep])
        halomask = AR.alloc([2], F32, "halomask")
        P.dma("sp", lambda e: e.dma_start(out=halomask.ap, in_=I["halomask"]), writes=[halomask.dep])
        x1scr = [nc.dram_tensor("x1scr0", [TP, D], F32).ap(), nc.dram_tensor("x1scr1", [1026, D], F32).ap()]
        x1scr_dep = [Dep("x1scr0"), Dep("x1scr1")]
        gkva_bc = AR.alloc([KVL], F32, "gkva_bc")
        P.dma("sp", lambda e: e.dma_start(out=gkva_bc.ap, in_=I["g_kva"].partition_broadcast(128)), writes=[gkva_bc.dep])

        def MM(out, lhsT, rhs, start=True, stop=True, r=(), w=(), skip=False):
            P.op("pe", lambda e: e.matmul(out, lhsT=lhsT, rhs=rhs, start=start, stop=stop, skip_group_check=skip),
                 reads=r, writes=w)

        def TR(out, in_, ident, r=(), w=()):
            P.op("pe", lambda e: e.transpose(out, in_, ident), reads=r, writes=w)

        def ACTF(out, in_, func, r=(), w=(), eng="act", **kw):
            P.op(eng, lambda e: e.activation(out=out, in_=in_, func=func, **kw), reads=r, writes=w)

        def TT(eng, out, in0, in1, op, r=(), w=()):
            P.op(eng, lambda e: e.tensor_tensor(out=out, in0=in0, in1=in1, op=op), reads=r, writes=w)

        def TS(eng, out, in0, s1, s2, op0, op1=None, r=(), w=()):
            if op1 is None:
                P.op(eng, lambda e: e.tensor_scalar(out=out, in0=in0, scalar1=s1, scalar2=None, op0=op0), reads=r, writes=w)
            else:
                P.op(eng, lambda e: e.tensor_scalar(out=out, in0=in0, scalar1=s1, scalar2=s2, op0=op0, op1=op1), reads=r, writes=w)

        def STT(out, in0, scalar, in1, op0, op1, r=(), w=()):
            P.op("dve", lambda e: e.scalar_tensor_tensor(out=out, in0=in0, scalar=scalar, in1=in1, op0=op0, op1=op1),
                 reads=r, writes=w)

        def CP(eng, out, in_, r=(), w=()):
            if eng == "act":
                P.op("act", lambda e: e.activation(out=out, in_=in_, func=AF.Copy), reads=r, writes=w)
            else:
                P.op(eng, lambda e: e.tensor_copy(out=out, in_=in_), reads=r, writes=w)

        def DMA(out, in_, r=(), w=(), q="sp", slow=False):
            if slow:
                P.dma(q, lambda e: e.dma_start(out=out, in_=in_, allow_slow_non_contiguous=True), reads=r, writes=w)
            else:
                P.dma(q, lambda e: e.dma_start(out=out, in_=in_), reads=r, writes=w)

        def RECIP(out, in_, r=(), w=()):
            P.op("dve", lambda e: e.reciprocal(out=out, in_=in_), reads=r, writes=w)

        def MEMSET(eng, out, val, w=()):
            P.op(eng, lambda e: e.memset(out, val), writes=w)

        gq_col = AR.alloc([1], F32, "gq_col")
        gk_col = AR.alloc([1], F32, "gk_col")
        gq2 = AR.alloc([1], F32, "gq2")
        gkpe = AR.alloc([1], F32, "gkpe")
        DMA(gq_col.ap[0:96, :], I["g_q"].rearrange("(p o) -> p o", o=1), w=[gq_col.dep], slow=True)
        DMA(gk_col.ap[0:96, :], I["g_k"].rearrange("(p o) -> p o", o=1), w=[gk_col.dep], slow=True)
        DMA(gkpe.ap[0:32, :], I["g_k"][64:96].rearrange("(p o) -> p o", o=1), w=[gkpe.dep], slow=True)
        CP("dve", gq2.ap[0:96, :], gq_col.ap[0:96, :], r=[gq_col.dep], w=[gq2.dep])
        TT("dve", gq2.ap[0:64, :], gq_col.ap[0:64, :], gk_col.ap[0:64, :], ALU.mult, r=[gq_col.dep, gk_col.dep], w=[gq2.dep])
        shiftT = AR.alloc([96], BF16, "shiftT")
        shiftrotT = AR.alloc([96], BF16, "shiftrotT")
        rot96T = AR.alloc([96], BF16, "rot96T")
        DMA(shiftT.ap[0:32, :], I["shiftT"], w=[shiftT.dep])
        DMA(shiftrotT.ap[0:32, :], I["shiftrotT"], w=[shiftrotT.dep])
        DMA(rot96T.ap[0:96, :], I["rot96T"], w=[rot96T.dep])
        gatt_bc = AR.alloc([512], F32, "gatt_bc")
        ghy_bc = AR.alloc([512], F32, "ghy_bc")
        DMA(gatt_bc.ap, I["g_out_att"].partition_broadcast(128), w=[gatt_bc.dep])
        DMA(ghy_bc.ap, I["g_out_hy"].partition_broadcast(128), w=[ghy_bc.dep])

        Uscr = [nc.dram_tensor("Uscr0", [1536, TP], BF16).ap(), nc.dram_tensor("Uscr1", [1536, TL], BF16).ap()]
        Uscr_dep = [Dep("Uscr0"), Dep("Uscr1")]

        def run_group(v):
            ntok = TP if v == 0 else TL
            nq = TP if v == 0 else 1026
            nqt = 8 if v == 0 else 9
            xin = I["xp"] if v == 0 else I["xs"]
            nkeys = ntok + (PAST if v == 1 else 0)
            nkt = nkeys // 128
            gm = AR.mark()
            O_all = AR.alloc([nqt, 512], F32, "O_all")
            g2m = AR.mark()
            bias_fm = AR.alloc([15], F32, "bias_fm")
            bias_row = AR.alloc([288], BF16, "bias_row")
            kvcT = AR.alloc([2, nkeys], BF16, "kvcT")
            kpeT = AR.alloc([nkeys], BF16, "kpeT")
            q_cT = AR.alloc([3, nq], F32, "q_cT")
            sspe = AR.alloc([nkt], F32, "sspe")
            hThalo = AR.alloc([KC, 2], BF16, "hThalo")
            p1m = AR.mark()
            Win = AR.alloc([KC, DIN], BF16, "Win")
            win_d = I["w_in"].rearrange("(k p) n -> p k n", p=128)
            for k in range(KC):
                DMA(Win.ap[:, k, :], win_d[:, k, :], w=[Win.dep], q="pool")

            xts = [AR.alloc([D], F32, "xt%d" % i) for i in range(4)]
            xns = [AR.alloc([D], BF16, "xn%d" % i) for i in range(3)]
            junk = AR.alloc([D], F32, "junk")
            stat = AR.alloc([8], F32, "stat")
            hTg = [AR.alloc([KC, 512], BF16, "hTg%d" % i) for i in range(2)]
            ckvo = [AR.alloc([KVL + ROPE], F32, "ckvo%d" % i) for i in range(2)]
            ckvb = [AR.alloc([KVL + ROPE], BF16, "ckvb%d" % i) for i in range(2)]
            ustg = [AR.alloc([512], BF16, "ustg%d" % i) for i in range(3)]
            ngroups = ntok // 512
            ucount = 0
            hTdeps = [[Dep("hT%d_%d" % (i, k)) for k in range(KC)] for i in range(2)]
            for i in range(2):
                for k in range(KC):
                    hTdeps[i][k].r = dict(hTg[i].dep.r)
            statB = AR.alloc([8], F32, "statB")

            def tile_pre(g, tt):
                ti = g * 4 + tt
                xt = xts[ti % 4]
                xn = xns[ti % 3]
                DMA(xt.ap, xin[ti * 128:(ti + 1) * 128, :], w=[xt.dep])
                ACTF(junk.ap, xt.ap, AF.Square, r=[xt.dep], w=[junk.dep, stat.dep], accum_out=stat.ap[:, 0:1])
                ACTF(stat.ap[:, 1:2], stat.ap[:, 0:1], AF.Sqrt, r=[stat.dep, epsb.dep], w=[stat.dep], scale=1.0 / D,
                     bias=epsb.ap[:, 0:1])
                RECIP(stat.ap[:, 2:3], stat.ap[:, 1:2], r=[stat.dep], w=[stat.dep])
                TS("dve", xn.ap, xt.ap, stat.ap[:, 2:3], None, ALU.mult, r=[xt.dep, stat.dep], w=[xn.dep])

            def tile_mid(g, tt):
                hT = hTg[g % 2]
                hd = hTdeps[g % 2]
                ti = g * 4 + tt
                xn = xns[ti % 3]
                pb = ti % 2
                for k in range(KC):
                    TR(psb(pb)[:, k * 128:(k + 1) * 128], xn.ap[:, k * 128:(k + 1) * 128], ident_b.ap,
                       r=[xn.dep, ident_b.dep], w=[ps[pb].dep])
                for k in range(KC):
                    if ti % 2 == 0:
                        TS("dve", hT.ap[:, k, tt * 128:(tt + 1) * 128], psb(pb)[:, k * 128:(k + 1) * 128], A1.ap[:, k, v:v + 1],
                           mod.ap[:, v, k:k + 1], ALU.mult, ALU.add, r=[ps[pb].dep, A1.dep, mod.dep], w=[hd[k]])
                    else:
                        ACTF(hT.ap[:, k, tt * 128:(tt + 1) * 128], psb(pb)[:, k * 128:(k + 1) * 128], AF.Identity,
                             r=[ps[pb].dep, A1.dep, mod.dep], w=[hd[k]], scale=A1.ap[:, k, v:v + 1], bias=mod.ap[:, v, k:k + 1])
                kb = 2 + (ti % 2)
                for k in range(KC):
                    MM(ps[kb].ap[:, 0:288], hT.ap[:, k, tt * 128:(tt + 1) * 128], Win.ap[:, k, 384:672],
                       start=(k == 0), stop=(k == KC - 1), r=[hd[k], Win.dep], w=[ps[kb].dep])

            def tile_post_a(g, tt):
                ti = g * 4 + tt
                kb = 2 + (ti % 2)
                co = ckvo[ti % 2]
                cb = ckvb[ti % 2]
                ACTF(junk.ap[:, 0:256], ps[kb].ap[:, 0:256], AF.Square, r=[ps[kb].dep], w=[junk.dep, statB.dep],
                     accum_out=statB.ap[:, 3:4])
                ACTF(statB.ap[:, 4:5], statB.ap[:, 3:4], AF.Sqrt, r=[statB.dep, epsb.dep], w=[statB.dep], scale=1.0 / KVL,
                     bias=epsb.ap[:, 0:1])
                RECIP(statB.ap[:, 5:6], statB.ap[:, 4:5], r=[statB.dep], w=[statB.dep])
                STT(co.ap[:, 0:256], ps[kb].ap[:, 0:256], statB.ap[:, 5:6], gkva_bc.ap, ALU.mult, ALU.mult,
                    r=[ps[kb].dep, statB.dep, gkva_bc.dep], w=[co.dep])
                CP("act", co.ap[:, 256:288], ps[kb].ap[:, 256:288], r=[ps[kb].dep], w=[co.dep])
                ACTF(junk.ap[:, 0:32], co.ap[:, 256:288], AF.Square, r=[co.dep], w=[junk.dep, sspe.dep],
                     accum_out=sspe.ap[:, ti:ti + 1])
                if v == 0:
                    DMA(O["new_ckv"][ti * 128:(ti + 1) * 128, :], co.ap[:, 0:256], r=[co.dep], q="pool")
                    DMA(O["new_kpe"][ti * 128:(ti + 1) * 128, :], co.ap[:, 256:288], r=[co.dep], q="pool")
                CP("dve", cb.ap, co.ap, r=[co.dep], w=[cb.dep])

            def tile_post_b(g, tt):
                ti = g * 4 + tt
                cb = ckvb[ti % 2]
                tb = 6 + (ti % 2)
                TR(psb(tb)[:, 0:128], cb.ap[:, 0:128], ident_b.ap, r=[cb.dep, ident_b.dep], w=[ps[tb].dep])
                TR(psb(tb)[:, 128:256], cb.ap[:, 128:256], ident_b.ap, r=[cb.dep, ident_b.dep], w=[ps[tb].dep])
                TR(psb(tb)[0:32, 256:384], cb.ap[:, 256:288], ident_b.ap, r=[cb.dep, ident_b.dep], w=[ps[tb].dep])
                CP("act", kvcT.ap[:, :, ti * 128:(ti + 1) * 128], psb(tb)[:, 0:256].rearrange("p (c t) -> p c t", t=128),
                   r=[ps[tb].dep], w=[kvcT.dep])
                CP("act", kpeT.ap[0:32, ti * 128:(ti + 1) * 128], psb(tb)[0:32, 256:384], r=[ps[tb].dep], w=[kpeT.dep])

            ucount_box = [0]

            def proj_ops(g, part):
                hT = hTg[g % 2]
                chunks = list(range(3, 15))
                if v == 1 and g in (3, 4, 5, 6):
                    chunks = list(range(3, 11))
                if v == 0 or g < 2:
                    chunks = [0, 1, 2] + chunks
                for ci in chunks[part::4]:
                    ucount = ucount_box[0]
                    c0 = ci * 128 if ci < 3 else 672 + (ci - 3) * 128
                    fb = 4 + (ucount % 2)
                    for k in range(KC):
                        MM(ps[fb].ap, Win.ap[:, k, c0:c0 + 128], hT.ap[:, k, :], start=(k == 0), stop=(k == KC - 1),
                           r=[Win.dep, hTdeps[g % 2][k]], w=[ps[fb].dep])
                    if ci < 3:
                        CP("act", q_cT.ap[:, ci, g * 512:(g + 1) * 512], ps[fb].ap, r=[ps[fb].dep], w=[q_cT.dep])
                    else:
                        us = ustg[ucount % 3]
                        CP("act" if ucount % 2 else "dve", us.ap, ps[fb].ap, r=[ps[fb].dep], w=[us.dep])
                        DMA(Uscr[v][(ci - 3) * 128:(ci - 2) * 128, g * 512:(g + 1) * 512], us.ap, r=[us.dep], w=[Uscr_dep[v]], q="pool")
                    ucount_box[0] += 1
                if part == 3:
                    if v == 1 and g == 2:
                        CP("pool", hThalo.ap[:, :, 1:2], hT.ap[:, :, 0:1], r=hTdeps[g % 2], w=[hThalo.dep])
                    if v == 1 and g == 7:
                        CP("pool", hThalo.ap[:, :, 0:1], hT.ap[:, :, 511:512], r=hTdeps[g % 2], w=[hThalo.dep])

            ntl = ngroups * 4
            tile_pre(0, 0)
            tile_pre(0, 1)
            for i in range(ntl + 4):
                g, tt = i // 4, i % 4
                if i + 2 < ntl:
                    tile_pre((i + 2) // 4, (i + 2) % 4)
                if i < ntl:
                    tile_mid(g, tt)
                if i < ntl:
                    tile_post_a(g, tt)
                if g >= 1:
                    proj_ops(g - 1, tt)
                if i < ntl:
                    tile_post_b(g, tt)
            MEMSET("pool", hTg[0].ap[:, 0, 0:1], 0.0, w=hTdeps[0] + [hTg[0].dep])
            MEMSET("pool", hTg[1].ap[:, 0, 0:1], 0.0, w=hTdeps[1] + [hTg[1].dep])
            if v == 1:
                for ci in range(3):
                    for k in range(KC):
                        MM(ps[4].ap[:, 0:2], Win.ap[:, k, ci * 128:(ci + 1) * 128], hThalo.ap[:, k, :], start=(k == 0),
                           stop=(k == KC - 1), r=[Win.dep, hThalo.dep], w=[ps[4].dep])
                    CP("act", q_cT.ap[:, ci, 1024:1026], ps[4].ap[:, 0:2], r=[ps[4].dep], w=[q_cT.dep])
                cst = AR.alloc([4, KVL + ROPE], F32, "cst")
                csb = AR.alloc([4, KVL + ROPE], BF16, "csb")
                DMA(cst.ap[:, :, 0:256], I["ckv_c"].rearrange("(t p) f -> p t f", p=128), w=[cst.dep])
                DMA(cst.ap[:, :, 256:288], I["kpe_c"].rearrange("(t p) f -> p t f", p=128), w=[cst.dep])
                CP("pool", csb.ap, cst.ap, r=[cst.dep], w=[csb.dep])
                for t4 in range(4):
                    ti = 32 + t4
                    ACTF(junk.ap[:, 0:32], cst.ap[:, t4, 256:288], AF.Square, r=[cst.dep], w=[junk.dep, sspe.dep],
                         accum_out=sspe.ap[:, ti:ti + 1])
                    tb = 6 + (t4 % 2)
                    TR(psb(tb)[:, 0:128], csb.ap[:, t4, 0:128], ident_b.ap, r=[csb.dep, ident_b.dep], w=[ps[tb].dep])
                    TR(psb(tb)[:, 128:256], csb.ap[:, t4, 128:256], ident_b.ap, r=[csb.dep, ident_b.dep], w=[ps[tb].dep])
                    TR(psb(tb)[0:32, 256:384], csb.ap[:, t4, 256:288], ident_b.ap, r=[csb.dep, ident_b.dep], w=[ps[tb].dep])
                    CP("act", kvcT.ap[:, :, ti * 128:(ti + 1) * 128], psb(tb)[:, 0:256].rearrange("p (c t) -> p c t", t=128),
                       r=[ps[tb].dep], w=[kvcT.dep])
                    CP("act", kpeT.ap[0:32, ti * 128:(ti + 1) * 128], psb(tb)[0:32, 256:384], r=[ps[tb].dep], w=[kpeT.dep])
            AR.release(p1m)

            am = AR.mark()
            Wuq = AR.alloc([3, N_HEADS * QK], BF16, "Wuq")
            Wukv = AR.alloc([2, 1024], BF16, "Wukv")
            DMA(Wuq.ap, I["w_uq"].rearrange("(k p) n -> p k n", p=128), w=[Wuq.dep], q="pool")
            DMA(Wukv.ap, I["w_ukv"].rearrange("(k p) n -> p k n", p=128), w=[Wukv.dep], q="pool")
            Vext = AR.alloc([nkt, N_HEADS, 65], BF16, "Vext")
            MEMSET("pool", Vext.ap[:, :, :, 64:65], 1.0, w=[Vext.dep])
            kscale = AR.alloc([nkt, N_HEADS], F32, "kscale")
            KT = [AR.alloc([nkeys], BF16, "KT%d" % i) for i in range(2)]
            QT = AR.alloc([N_HEADS, nq], BF16, "QT")
            MEMSET("pool", QT.ap[96:128, :, :], 0.0, w=[QT.dep])
            for i_ in range(2):
                MEMSET("pool", KT[i_].ap[96:128, :], 0.0, w=[KT[i_].dep])
            tmpA = [AR.alloc([512], F32, "tmpA%d" % i) for i in range(2)]
            tmpB = [AR.alloc([512], F32, "tmpB%d" % i) for i in range(2)]
            tmpC = [AR.alloc([512], BF16, "tmpC%d" % i) for i in range(2)]
            junk2 = AR.alloc([512], F32, "junk2")
            st2 = AR.alloc([32], F32, "st2")
            gkb = [AR.alloc([512], BF16, "gkb%d" % i) for i in range(2)]
            if v == 1:
                ctab = [AR.alloc([512], F32, "ctab%d" % i) for i in range(2)]
                stab = [AR.alloc([512], F32, "stab%d" % i) for i in range(2)]
            def rope_block(kb):
                ks = slice(kb * 512, (kb + 1) * 512)
                gk = gkb[kb % 2]
                TS("pool", gk.ap[0:32, :], kpeT.ap[0:32, ks], gkpe.ap[0:32, 0:1], 0.0, ALU.mult, ALU.add,
                   r=[kpeT.dep, gkpe.dep], w=[gk.dep])
                MM(ps[5].ap[0:96, :], shiftT.ap[0:32, :], gk.ap[0:32, :], r=[shiftT.dep, gk.dep], w=[ps[5].dep])
                if v == 0:
                    CP("act", KT[0].ap[64:96, ks], ps[5].ap[64:96, :], r=[ps[5].dep], w=[KT[0].dep])
                    CP("dve", KT[1].ap[64:96, ks], ps[5].ap[64:96, :], r=[ps[5].dep], w=[KT[1].dep])
                else:
                    ct, stb = ctab[kb % 2], stab[kb % 2]
                    DMA(ct.ap[64:96, :], I["cosk"][:, ks], w=[ct.dep])
                    DMA(stb.ap[64:96, :], I["sink"][:, ks], w=[stb.dep])
                    MM(ps[6].ap[0:96, :], shiftrotT.ap[0:32, :], gk.ap[0:32, :], r=[shiftrotT.dep, gk.dep], w=[ps[6].dep])
                    ta, tb_ = tmpA[kb % 2], tmpB[kb % 2]
                    TT("dve", ta.ap[64:96, :], ps[5].ap[64:96, :], ct.ap[64:96, :], ALU.mult, r=[ps[5].dep, ct.dep], w=[ta.dep])
                    TT("dve", tb_.ap[64:96, :], ps[6].ap[64:96, :], stb.ap[64:96, :], ALU.mult, r=[ps[6].dep, stb.dep], w=[tb_.dep])
                    TT("dve", KT[0].ap[64:96, ks], ta.ap[64:96, :], tb_.ap[64:96, :], ALU.add, r=[ta.dep, tb_.dep], w=[KT[0].dep])
                    TT("dve", KT[1].ap[64:96, ks], ta.ap[64:96, :], tb_.ap[64:96, :], ALU.add, r=[ta.dep, tb_.dep], w=[KT[1].dep])

            def kv_tile(kt):
                for half in range(2):
                    bk = (kt % 2) * 2 + half
                    for c in range(2):
                        MM(ps[bk].ap, kvcT.ap[:, c, kt * 128:(kt + 1) * 128], Wukv.ap[:, c, half * 512:(half + 1) * 512],
                           start=(c == 0), stop=(c == 1), r=[kvcT.dep, Wukv.dep], w=[ps[bk].dep])
                    pv_ = ps[bk].ap.rearrange("p (h x) -> p h x", x=128)
                    CP("act", Vext.ap[:, kt, half * 4:(half + 1) * 4, 0:64], pv_[:, :, 64:128], r=[ps[bk].dep], w=[Vext.dep])
                    ACTF(junk2.ap[:, 0:256].rearrange("p (h x) -> p h x", x=64), pv_[:, :, 0:64], AF.Square,
                         r=[ps[bk].dep], w=[junk2.dep])
                    P.op("dve", lambda e, half=half: e.reduce_sum(
                        out=st2.ap[:, half * 4:(half + 1) * 4], in_=junk2.ap[:, 0:256].rearrange("p (h x) -> p h x", x=64),
                        axis=AX.X), reads=[junk2.dep], writes=[st2.dep])
                TS("dve", st2.ap[:, 8:16], st2.ap[:, 0:8], sspe.ap[:, kt:kt + 1], 1.0 / QK, ALU.add, ALU.mult,
                   r=[st2.dep, sspe.dep], w=[st2.dep])
                ACTF(st2.ap[:, 16:24], st2.ap[:, 8:16], AF.Sqrt, r=[st2.dep, epsb.dep], w=[st2.dep], bias=epsb.ap[:, 0:1])
                RECIP(st2.ap[:, 24:32], st2.ap[:, 16:24], r=[st2.dep], w=[st2.dep])
                TS("dve", kscale.ap[:, kt, :], st2.ap[:, 24:32], float(QK) ** -0.5, None, ALU.mult, r=[st2.dep], w=[kscale.dep])


            qnT = AR.alloc([3, nq], BF16, "qnT")
            sqb = AR.alloc([3, 512], BF16, "sqb")
            if v == 1:
                cosq = AR.alloc([nq], F32, "cosq")
                sinq = AR.alloc([nq], F32, "sinq")
                DMA(cosq.ap[0:96, :], I["cosq"], w=[cosq.dep])
                DMA(sinq.ap[0:96, :], I["sinq"], w=[sinq.dep])
            qgroups = [(0, 512), (512, 512)] + ([(1024, 2)] if v == 1 else [])

            def q_gen():
                for gi, (q0, qn_) in enumerate(qgroups):
                    qs = slice(q0, q0 + qn_)
                    ACTF(sqb.ap[:, :, 0:qn_], q_cT.ap[:, :, qs], AF.Square, r=[q_cT.dep], w=[sqb.dep])
                    for c in range(3):
                        MM(ps[7].ap[:, 0:qn_], ones_b.ap, sqb.ap[:, c, 0:qn_], start=(c == 0), stop=(c == 2),
                           r=[ones_b.dep, sqb.dep], w=[ps[7].dep])
                    ta = tmpA[gi % 2]
                    ACTF(ta.ap[:, 0:qn_], ps[7].ap[:, 0:qn_], AF.Ln, r=[ps[7].dep, epsb.dep], w=[ta.dep], scale=1.0 / QL,
                         bias=epsb.ap[:, 0:1])
                    ACTF(ta.ap[:, 0:qn_], ta.ap[:, 0:qn_], AF.Exp, r=[ta.dep], w=[ta.dep], scale=-0.5)
                    for c in range(3):
                        STT(qnT.ap[:, c, qs], q_cT.ap[:, c, qs], vcol("g_qa", c), ta.ap[:, 0:qn_], ALU.mult, ALU.mult,
                            r=[q_cT.dep, vec.dep, ta.dep], w=[qnT.dep])
                    yield
                    for h in range(N_HEADS):
                        qb = 5 + (h % 2)
                        for c in range(3):
                            MM(ps[qb].ap[0:96, 0:qn_], Wuq.ap[:, c, h * QK:(h + 1) * QK], qnT.ap[:, c, qs], start=(c == 0),
                               stop=(c == 2), r=[Wuq.dep, qnT.dep], w=[ps[qb].dep])
                        tb_ = tmpB[h % 2]
                        tc_ = tmpC[h % 2]
                        CP("act", tb_.ap[0:96, 0:qn_], ps[qb].ap[0:96, 0:qn_], r=[ps[qb].dep], w=[tb_.dep])
                        ACTF(tc_.ap[0:96, 0:qn_], ps[qb].ap[0:96, 0:qn_], AF.Square, r=[ps[qb].dep], w=[tc_.dep])
                        MM(ps[7].ap[0:96, 0:qn_], ones_b.ap[0:96, 0:96], tc_.ap[0:96, 0:qn_], r=[ones_b.dep, tc_.dep], w=[ps[7].dep])
                        t2 = tmpA[(gi + 1) % 2]
                        ACTF(t2.ap[0:96, 0:qn_], ps[7].ap[0:96, 0:qn_], AF.Ln, r=[ps[7].dep, epsb.dep], w=[t2.dep],
                             scale=1.0 / QK, bias=epsb.ap[0:96, 0:1])
                        ACTF(t2.ap[0:96, 0:qn_], t2.ap[0:96, 0:qn_], AF.Exp, r=[t2.dep], w=[t2.dep], scale=-0.5)
                        if v == 0:
                            STT(QT.ap[0:96, h, qs], tb_.ap[0:96, 0:qn_], gq2.ap[0:96, 0:1], t2.ap[0:96, 0:qn_], ALU.mult, ALU.mult,
                                r=[tb_.dep, gq2.dep, t2.dep], w=[QT.dep])
                        else:
                            STT(tb_.ap[0:96, 0:qn_], tb_.ap[0:96, 0:qn_], gq2.ap[0:96, 0:1], t2.ap[0:96, 0:qn_], ALU.mult, ALU.mult,
                                r=[tb_.dep, gq2.dep, t2.dep], w=[tb_.dep])
                            CP("pool", tc_.ap[0:96, 0:qn_], tb_.ap[0:96, 0:qn_], r=[tb_.dep], w=[tc_.dep])
                            MM(ps[qb].ap[0:96, 0:qn_], rot96T.ap[0:96, :], tc_.ap[0:96, 0:qn_], r=[rot96T.dep, tc_.dep], w=[ps[qb].dep])
                            TT("dve", t2.ap[0:96, 0:qn_], ps[qb].ap[0:96, 0:qn_], sinq.ap[0:96, qs], ALU.mult,
                               r=[ps[qb].dep, sinq.dep], w=[t2.dep])
                            TT("dve", tb_.ap[0:96, 0:qn_], tb_.ap[0:96, 0:qn_], cosq.ap[0:96, qs], ALU.mult,
                               r=[tb_.dep, cosq.dep], w=[tb_.dep])
                            TT("dve", QT.ap[0:96, h, qs], tb_.ap[0:96, 0:qn_], t2.ap[0:96, 0:qn_], ALU.add,
                               r=[tb_.dep, t2.dep], w=[QT.dep])

                        yield

            qg_ = q_gen()
            for kb in range(nkeys // 512):
                rope_block(kb)
            for kt in range(nkt):
                kv_tile(kt)
                if kt % 3 != 2:
                    next(qg_, None)
            for _ in qg_:
                pass
            PT = [AR.alloc([512], BF16, "PT%d" % i) for i in range(3)]
            rc = AR.alloc([8], F32, "rc")
            if v == 0:
                att_groups = [(s_ * 256, 256, [2 * s_, 2 * s_ + 1]) for s_ in range(4)]
            else:
                att_groups = [(0, 512, list(range(nkt))), (512, 512, list(range(nkt))), (1024, 2, list(range(nkt)))]
            it = 0
            accn = 0
            for h in range(N_HEADS):
                KTh = KT[h % 2]
                for kb in range(nkeys // 512):
                    ks = slice(kb * 512, (kb + 1) * 512)
                    fb = 5 + (kb % 2)
                    for c in range(2):
                        MM(ps[fb].ap[:, :], Wukv.ap[:, c, h * 128:(h + 1) * 128], kvcT.ap[:, c, ks], start=(c == 0), stop=(c == 1),
                           r=[Wukv.dep, kvcT.dep], w=[ps[fb].dep])
                    CP("dve" if kb % 2 else "act", KTh.ap[0:64, ks], ps[fb].ap[0:64, :], r=[ps[fb].dep], w=[KTh.dep])
                for (q0, qn_, kts) in att_groups:
                    ab = 3 + (accn % 2)
                    accn += 1
                    nqt_g = (qn_ + 127) // 128
                    def s_mm(ki_, it_):
                        kt_ = kts[ki_]
                        MM(ps[it_ % 3].ap[:, 0:qn_], KTh.ap[:, kt_ * 128:(kt_ + 1) * 128], QT.ap[:, h, q0:q0 + qn_],
                           r=[KTh.dep, QT.dep], w=[ps[it_ % 3].dep])
                    s_mm(0, it)
                    for ki, kt in enumerate(kts):
                        sb_ = it % 3
                        pt = PT[it % 3]
                        it += 1
                        if ki + 1 < len(kts):
                            s_mm(ki + 1, it)
                        ACTF(pt.ap[:, 0:qn_], ps[sb_].ap[:, 0:qn_], AF.Exp, r=[ps[sb_].dep, kscale.dep], w=[pt.dep],
                             scale=kscale.ap[:, kt, h:h + 1])
                        for qt in range(nqt_g):
                            m_ = min(128, qn_ - qt * 128)
                            MM(ps[ab].ap[0:m_, qt * 65:(qt + 1) * 65], pt.ap[:, qt * 128:qt * 128 + m_], Vext.ap[:, kt, h, :],
                               start=(ki == 0 and qt == 0), stop=(ki == len(kts) - 1), r=[pt.dep, Vext.dep], w=[ps[ab].dep], skip=True)
                    for qt in range(nqt_g):
                        m_ = min(128, qn_ - qt * 128)
                        qti = q0 // 128 + qt
                        RECIP(rc.ap[0:m_, qt:qt + 1], ps[ab].ap[0:m_, qt * 65 + 64:qt * 65 + 65], r=[ps[ab].dep], w=[rc.dep])
                        TS("dve", O_all.ap[0:m_, qti, h * 64:(h + 1) * 64], ps[ab].ap[0:m_, qt * 65:qt * 65 + 64],
                           rc.ap[0:m_, qt:qt + 1], None, ALU.mult, r=[ps[ab].dep, rc.dep], w=[O_all.dep])
            AR.release(g2m)
            d_ = dbg_out("att%d" % v, [128, nqt * 512])
            if d_ is not None:
                DMA(d_, O_all.ap.rearrange("p t f -> p (t f)"), r=[O_all.dep])
            nout2 = 8 if v == 0 else 10
            y2hy = AR.alloc([nout2, 512], F32, "y2hy")
            hm = AR.mark()
            nb = ntok // 128
            nfb = 2 if v == 0 else 32
            nfilt = nfb * 128
            nks = 2 * nfb
            tsuf = "p" if v == 0 else "l"
            if v == 0:
                out1 = list(range(8))
                out2 = list(range(8))
            else:
                out1 = list(range(32))
                out2 = list(range(8)) + [31, 8]
            cg = 16 if v == 0 else 4
            ng = 128 // cg
            nh = 2
            Fb = AR.alloc([512], BF16, "Fb")
            CBt = AR.alloc([4, 128], BF16, "CB")
            R1T = AR.alloc([128], BF16, "R1T")
            E0T = AR.alloc([128], BF16, "E0T")
            DMA(Fb.ap, I["Fb"], w=[Fb.dep])
            DMA(CBt.ap, I["CB_" + tsuf], w=[CBt.dep])
            DMA(R1T.ap, I["R1T"], w=[R1T.dep])
            DMA(E0T.ap, I["E0T"], w=[E0T.dep])
            nrow_d = nb * cg
            FAd = AR.alloc([nh, 3, 128], BF16, "FAd")
            DMA(FAd.ap[0:nrow_d], I["FAd_" + tsuf], w=[FAd.dep])
            nkch = 1 if v == 0 else 2
            krows = 128
            FAk = [AR.alloc([nh, 3, 128], BF16, "FAk%d" % i) for i in range(nkch)]
            for i in range(nkch):
                DMA(FAk[i].ap[0:krows], I["FAk_" + tsuf][i], w=[FAk[i].dep])
            np1 = 2 * len(out1)
            np2 = 2 * len(out2)
            GA1 = AR.alloc([nh, 2, 2 * np1 * cg], BF16, "GA1")
            GA2 = AR.alloc([nh, 2, 2 * np2 * cg], BF16, "GA2")
            DMA(GA1.ap, I["GA1_" + tsuf], w=[GA1.dep])
            DMA(GA2.ap, I["GA2_" + tsuf], w=[GA2.dep])
            h2T = AR.alloc([nfilt], BF16, "h2T")
            W3b = AR.alloc([2048], BF16, "W3b")
            DMA(W3b.ap[0:64, :], I["filt_w3"], w=[W3b.dep], q="pool")
            fm_ = AR.mark()
            zT = AR.alloc([nfilt], F32, "zT")
            DMA(zT.ap[0:33, :], I["zT_" + tsuf], w=[zT.dep])
            w1t = AR.alloc([64], F32, "w1t")
            w2t = AR.alloc([64], F32, "w2t")
            DMA(w1t.ap[0:33, :], I["filt_w1"], w=[w1t.dep])
            DMA(w2t.ap[0:64, :], I["filt_w2"], w=[w2t.dep])
            fcol = AR.alloc([8], F32, "fcol")
            for i_, nm in enumerate(["filt_b1", "filt_freq1", "filt_b2", "filt_freq2"]):
                DMA(fcol.ap[0:64, i_:i_ + 1], I[nm].rearrange("(p o) -> p o", o=1), w=[fcol.dep], slow=True)
            for l_ in range(2):
                TS("dve", fcol.ap[0:64, 4 + 2 * l_:5 + 2 * l_], fcol.ap[0:64, 1 + 2 * l_:2 + 2 * l_], 1.0 / 3.0, None, ALU.mult,
                   r=[fcol.dep], w=[fcol.dep])
                TT("dve", fcol.ap[0:64, 5 + 2 * l_:6 + 2 * l_], fcol.ap[0:64, 4 + 2 * l_:5 + 2 * l_], fcol.ap[0:64, 2 * l_:2 * l_ + 1],
                   ALU.mult, r=[fcol.dep], w=[fcol.dep])
            h1T = AR.alloc([nfilt], F32, "h1T")
            sA = [AR.alloc([512], F32, "sA%d" % i) for i in range(2)]
            sB = [AR.alloc([512], F32, "sB%d" % i) for i in range(2)]
            ncb = max(1, nfilt // 512)
            cw = min(512, nfilt)
            for l_ in range(2):
                for cbk in range(ncb):
                    cs = slice(cbk * cw, (cbk + 1) * cw)
                    fb_ = 6 + (cbk % 2)
                    if l_ == 0:
                        MM(ps[fb_].ap[0:64, 0:cw], w1t.ap[0:33, 0:64], zT.ap[0:33, cs], r=[w1t.dep, zT.dep], w=[ps[fb_].dep])
                    else:
                        MM(ps[fb_].ap[0:64, 0:cw], w2t.ap[0:64, 0:64], h1T.ap[0:64, cs], r=[w2t.dep, h1T.dep], w=[ps[fb_].dep])
                    s_ = sA[cbk % 2]
                    q_ = sB[cbk % 2]
                    ACTF(s_.ap[0:64, 0:cw], ps[fb_].ap[0:64, 0:cw], AF.Sin, r=[ps[fb_].dep, fcol.dep], w=[s_.dep],
                         scale=fcol.ap[0:64, 4 + 2 * l_:5 + 2 * l_], bias=fcol.ap[0:64, 5 + 2 * l_:6 + 2 * l_])
                    TT("dve", q_.ap[0:64, 0:cw], s_.ap[0:64, 0:cw], s_.ap[0:64, 0:cw], ALU.mult, r=[s_.dep], w=[q_.dep])
                    TS("dve", q_.ap[0:64, 0:cw], q_.ap[0:64, 0:cw], -4.0, 3.0, ALU.mult, ALU.add, r=[q_.dep], w=[q_.dep])
                    if l_ == 0:
                        TT("dve", h1T.ap[0:64, cs], q_.ap[0:64, 0:cw], s_.ap[0:64, 0:cw], ALU.mult, r=[q_.dep, s_.dep], w=[h1T.dep])
                    else:
                        TT("dve", h2T.ap[0:64, cs], q_.ap[0:64, 0:cw], s_.ap[0:64, 0:cw], ALU.mult, r=[q_.dep, s_.dep], w=[h2T.dep])
            AR.release(fm_)
            if v == 1:
                seamc = AR.alloc([4], F32, "seamc")
                DMA(seamc.ap, I["seamc"], w=[seamc.dep])
                cw02 = AR.alloc([8], F32, "cw02")

            XB = [2, 3, 6, 7]

            def fwd2d(chunks, FAt, consume, consume_pe=None):
                s1p = [[AR.alloc([2, 2, 256], BF16, "s1p_%d_%d" % (i, c_)) for c_ in range(len(chunks))] for i in range(2)]

                def stage1(g):
                    for ci_, (U, sn) in enumerate(chunks):
                        M = sn * cg
                        sbk = (g % 2) if ci_ == 0 else 4 + (g % 2)
                        MM(ps[sbk].ap[0:M, :], U.ap[:, cg * g:cg * (g + 1), :].rearrange("p c s -> p (c s)"), Fb.ap,
                           r=[U.dep, Fb.dep], w=[ps[sbk].dep])
                        S = s1p[(g // 2) % 2][ci_]
                        CP("act" if ci_ == 0 else "dve", S.ap[0:M, :, g % 2, :], ps[sbk].ap[0:M, :].rearrange("p (t b) -> p t b", b=256),
                           r=[ps[sbk].dep], w=[S.dep])

                def stage2(p):
                    banks = []
                    for h in range(nh):
                        xb_ = XB[(2 * p + h) % 4]
                        k_ = 0
                        nt_ = 2 * len(chunks)
                        for ci_, (U, sn) in enumerate(chunks):
                            M = sn * cg
                            S = s1p[p % 2][ci_]
                            for (ti_, tm_) in ((0, 0), (2, 1)):
                                MM(ps[xb_].ap[:, :], FAt[ci_].ap[0:M, h, ti_, :], S.ap[0:M, tm_, :, :].rearrange("p g b -> p (g b)"),
                                   start=(k_ == 0), stop=(k_ == nt_ - 1), r=[FAt[ci_].dep, S.dep], w=[ps[xb_].dep])
                                k_ += 1
                        banks.append(xb_)
                    return banks

                npair = ng // 2
                stage1(0)
                stage1(1)
                for p in range(npair):
                    if p + 1 < npair:
                        stage1(2 * p + 2)
                        stage1(2 * p + 3)
                    banks = stage2(p)
                    for h in range(nh):
                        consume(2 * p + h, banks[h])
                    if consume_pe is not None and p >= 1:
                        consume_pe(p - 1)
                if consume_pe is not None:
                    consume_pe(npair - 1)

            y2hy_cols = len(out2)
            for cc in range(4):
                ccm = AR.mark()
                Vt = AR.alloc([128, nb], BF16, "Vt")
                X1t = AR.alloc([nb, 128], BF16, "X1t")
                X2t = AR.alloc([nb, 128], BF16, "X2t")
                Z1t = AR.alloc([128, nb], BF16, "Z1t")
                s1m = AR.mark()
                uins = [AR.alloc([ntok], BF16, "uin%d" % i) for i in range(2)]
                uo = AR.alloc([ntok], F32, "uo")
                for idx, dst in enumerate([Vt, X1t, X2t]):
                    rr = idx * 4 + cc
                    ui = uins[idx % 2]
                    if v == 1 and idx == 2:
                        MEMSET("pool", ui.ap[:, 1536:3584], 0.0, w=[ui.dep])
                        DMA(ui.ap[:, 0:1536], Uscr[v][rr * 128:(rr + 1) * 128, 0:1536], r=[Uscr_dep[v]], w=[ui.dep])
                        DMA(ui.ap[:, 3584:4096], Uscr[v][rr * 128:(rr + 1) * 128, 3584:4096], r=[Uscr_dep[v]], w=[ui.dep])
                    else:
                        DMA(ui.ap, Uscr[v][rr * 128:(rr + 1) * 128, :], r=[Uscr_dep[v]], w=[ui.dep])
                    w0c, w1c, w2c = vcol("hy_conv_w", rr), vcol("hy_conv_w", 12 + rr), vcol("hy_conv_w", 24 + rr)
                    TS("dve", uo.ap, ui.ap, w1c, vcol("hy_conv_b", rr), ALU.mult, ALU.add, r=[ui.dep, vec.dep], w=[uo.dep])
                    if v == 0:
                        for s_ in range(4):
                            a0 = s_ * 256
                            STT(uo.ap[:, a0 + 1:a0 + 256], ui.ap[:, a0:a0 + 255], w0c, uo.ap[:, a0 + 1:a0 + 256], ALU.mult, ALU.add,
                                r=[ui.dep, uo.dep, vec.dep], w=[uo.dep])
                            STT(uo.ap[:, a0:a0 + 255], ui.ap[:, a0 + 1:a0 + 256], w2c, uo.ap[:, a0:a0 + 255], ALU.mult, ALU.add,
                                r=[ui.dep, uo.dep, vec.dep], w=[uo.dep])
                    else:
                        STT(uo.ap[:, 1:ntok], ui.ap[:, 0:ntok - 1], w0c, uo.ap[:, 1:ntok], ALU.mult, ALU.add,
                            r=[ui.dep, uo.dep, vec.dep], w=[uo.dep])
                        STT(uo.ap[:, 0:ntok - 1], ui.ap[:, 1:ntok], w2c, uo.ap[:, 0:ntok - 1], ALU.mult, ALU.add,
                            r=[ui.dep, uo.dep, vec.dep], w=[uo.dep])
                        TS("dve", cw02.ap[:, 0:4], seamc.ap, w0c, None, ALU.mult, r=[seamc.dep, vec.dep], w=[cw02.dep])
                        TS("dve", cw02.ap[:, 4:8], seamc.ap, w2c, None, ALU.mult, r=[seamc.dep, vec.dep], w=[cw02.dep])
                        for si in range(4):
                            s_ = si * 1024
                            sm1 = (s_ - 1) % ntok
                            STT(uo.ap[:, s_:s_ + 1], ui.ap[:, sm1:sm1 + 1], cw02.ap[:, si:si + 1], uo.ap[:, s_:s_ + 1], ALU.mult, ALU.add,
                                r=[ui.dep, uo.dep, cw02.dep], w=[uo.dep])
                            STT(uo.ap[:, sm1:sm1 + 1], ui.ap[:, s_:s_ + 1], cw02.ap[:, 4 + si:5 + si], uo.ap[:, sm1:sm1 + 1], ALU.mult, ALU.add,
                                r=[ui.dep, uo.dep, cw02.dep], w=[uo.dep])
                    if idx == 0 and cc == 0:
                        d_ = dbg_out("u%d" % v, [128, ntok])
                        if d_ is not None:
                            DMA(d_, uo.ap, r=[uo.dep])
                    for b4 in range(nb // 4):
                        tb = b4 % 2
                        for bi in range(4):
                            blk = b4 * 4 + bi
                            TR(ps[tb].ap[:, bi * 128:(bi + 1) * 128], uo.ap[:, blk * 128:(blk + 1) * 128], ident_f.ap,
                               r=[uo.dep, ident_f.dep], w=[ps[tb].dep])
                        if idx == 0:
                            CP("act" if b4 % 2 else "dve", dst.ap[:, :, b4 * 4:b4 * 4 + 4],
                               ps[tb].ap.rearrange("p (b c) -> p c b", c=128), r=[ps[tb].dep], w=[dst.dep])
                        else:
                            CP("act" if b4 % 2 else "dve", dst.ap[:, b4 * 4:b4 * 4 + 4, :],
                               ps[tb].ap.rearrange("p (b c) -> p b c", c=128), r=[ps[tb].dep], w=[dst.dep])
                AR.release(s1m)
                for o in range(2):
                    om = AR.mark()
                    Hs = AR.alloc([ng, nh, 2, 128], BF16, "Hs")
                    km = AR.mark()
                    if v == 0:
                        UK4 = AR.alloc([128, 8], BF16, "UK4")
                        MEMSET("pool", UK4.ap[:, :, nks:8], 0.0, w=[UK4.dep])
                        UKa = Buf(UK4.ap[:, :, 0:nfb], "UKa")
                        UKb = Buf(UK4.ap[:, :, nfb:nks], "UKb")
                        UKa.dep = UK4.dep
                        UKb.dep = UK4.dep
                        kch_ = [(UK4, 8)]
                    else:
                        UKa = AR.alloc([128, nfb], BF16, "UKa")
                        UKb = AR.alloc([128, nfb], BF16, "UKb")
                        kch_ = [(UKa, nfb), (UKb, nfb)]
                    hbk = AR.alloc([nfb, 128], BF16, "hbk")
                    wint = AR.alloc([nfb, 128], BF16, "wint")
                    DMA(wint.ap, I["win_" + tsuf].rearrange("(a p) c -> p a c", p=128)[:, :, cc * 128:(cc + 1) * 128], w=[wint.dep])
                    skr = AR.alloc([128], F32, "skr")
                    DMA(skr.ap[0:1, :], I["filt_bias"][o:o + 1, cc * 128:(cc + 1) * 128], w=[skr.dep])
                    for dr in range(2):
                        col0 = (o * 2 + dr) * 512 + cc * 128
                        dstb = UKa if dr == 0 else hbk
                        ngrp = (nfb + 3) // 4
                        for b4 in range(ngrp):
                            nb4 = min(4, nfb - b4 * 4)
                            fb_ = 6 + (b4 % 2)
                            for bi in range(nb4):
                                blk = b4 * 4 + bi
                                MM(ps[fb_].ap[:, bi * 128:(bi + 1) * 128], h2T.ap[0:64, blk * 128:(blk + 1) * 128],
                                   W3b.ap[0:64, col0:col0 + 128], r=[h2T.dep, W3b.dep], w=[ps[fb_].dep])
                            if dr == 0:
                                TT("dve", UKa.ap[:, :, b4 * 4:b4 * 4 + nb4],
                                   ps[fb_].ap[:, 0:nb4 * 128].rearrange("p (b c) -> p c b", c=128),
                                   wint.ap[:, b4 * 4:b4 * 4 + nb4, :].rearrange("p b c -> p c b"),
                                   ALU.mult, r=[ps[fb_].dep, wint.dep], w=[dstb.dep])
                            else:
                                TT("dve", hbk.ap[:, b4 * 4:b4 * 4 + nb4, :],
                                   ps[fb_].ap[:, 0:nb4 * 128].rearrange("p (b c) -> p b c", c=128), wint.ap[:, b4 * 4:b4 * 4 + nb4, :],
                                   ALU.mult, r=[ps[fb_].dep, wint.dep], w=[dstb.dep])
                    TT("dve", UKa.ap[0:1, :, 0], UKa.ap[0:1, :, 0], skr.ap[0:1, :], ALU.add, r=[UKa.dep, skr.dep], w=[UKa.dep])
                    if o == 0 and cc == 0:
                        d_ = dbg_out("hf%d" % v, [128, nfb * 128])
                        if d_ is not None:
                            dtmp = AR.alloc([nfb, 128], F32, "dtmp")
                            CP("dve", dtmp.ap, UKa.ap.rearrange("p c b -> p b c"), r=[UKa.dep], w=[dtmp.dep])
                            DMA(d_, dtmp.ap.rearrange("p b c -> p (b c)"), r=[dtmp.dep])
                    for b4 in range((nfb + 3) // 4):
                        nb4 = min(4, nfb - b4 * 4)
                        fb_ = 6 + (b4 % 2)
                        for bi in range(nb4):
                            ip = b4 * 4 + bi + 1
                            last = (ip == nfb)
                            MM(ps[fb_].ap[:, bi * 128:(bi + 1) * 128], R1T.ap, hbk.ap[:, ip - 1, :], start=True, stop=last,
                               r=[R1T.dep, hbk.dep], w=[ps[fb_].dep])
                            if not last:
                                MM(ps[fb_].ap[:, bi * 128:(bi + 1) * 128], E0T.ap, hbk.ap[:, ip, :], start=False, stop=True,
                                   r=[E0T.dep, hbk.dep], w=[ps[fb_].dep])
                        CP("act", UKb.ap[:, :, b4 * 4:b4 * 4 + nb4],
                           ps[fb_].ap[:, 0:nb4 * 128].rearrange("p (b c) -> p c b", c=128), r=[ps[fb_].dep], w=[UKb.dep])

                    def consume_k(g, xb_):
                        CP("act" if g % 2 else "dve", Hs.ap[:, g].rearrange("p h r b -> p (h r b)"), ps[xb_].ap[:, 0:nh * 256],
                           r=[ps[xb_].dep], w=[Hs.dep])
                    fwd2d(kch_, FAk, consume_k)
                    AR.release(km)
                    src = Vt if o == 0 else Z1t
                    outs = out1 if o == 0 else out2
                    nout = len(outs)
                    npos = 2 * nout
                    GA = GA1 if o == 0 else GA2
                    ncol = 2 * npos * cg
                    Zt = AR.alloc([2, npos, 128], BF16, "Zt")
                    tAB = [AR.alloc([nh, 2, 128], BF16, "tAB%d" % i) for i in range(2)]
                    tCD = [AR.alloc([nh, 2, 128], BF16, "tCD%d" % i) for i in range(2)]
                    Yb = [AR.alloc([nh, 2, 128], BF16, "Yb%d" % i) for i in range(6)]

                    def consume_d(g, xb_):
                        X4 = ps[xb_].ap[:, 0:nh * 256].rearrange("p (h r b) -> p h r b", r=2, b=128)
                        H4 = Hs.ap[:, g]
                        ab, cd, yb = tAB[g % 2], tCD[g % 2], Yb[g % 6]
                        TT("dve", ab.ap, X4, H4, ALU.mult, r=[ps[xb_].dep, Hs.dep], w=[ab.dep])
                        TT("dve", cd.ap[:, :, 0, :], X4[:, :, 0, :], H4[:, :, 1, :], ALU.mult, r=[ps[xb_].dep, Hs.dep], w=[cd.dep])
                        TT("dve", cd.ap[:, :, 1, :], X4[:, :, 1, :], H4[:, :, 0, :], ALU.mult, r=[ps[xb_].dep, Hs.dep], w=[cd.dep])
                        TT("pool", yb.ap[:, :, 0, :], ab.ap[:, :, 0, :], ab.ap[:, :, 1, :], ALU.subtract, r=[ab.dep], w=[yb.dep])
                        TT("pool", yb.ap[:, :, 1, :], cd.ap[:, :, 0, :], cd.ap[:, :, 1, :], ALU.add, r=[cd.dep], w=[yb.dep])

                    def consume_d_pe(p):
                        for gi in range(2):
                            g = 2 * p + gi
                            zb = 4 + gi
                            k_ = 0
                            for h in range(nh):
                                yb = Yb[(2 * p + h) % 6]
                                for ri in range(2):
                                    MM(ps[zb].ap[:, 0:ncol], yb.ap[:, gi, ri, :], GA.ap[:, h, ri, :], start=(k_ == 0), stop=(k_ == 2 * nh - 1),
                                       r=[yb.dep, GA.dep], w=[ps[zb].dep])
                                    k_ += 1
                            CP("act", Zt.ap[:, :, :, cg * g:cg * (g + 1)], ps[zb].ap[:, 0:ncol].rearrange("p (r n c) -> p r n c", r=2, c=cg),
                               r=[ps[zb].dep], w=[Zt.dep])
                    fwd2d([(src, nb)], [FAd], consume_d, consume_d_pe)
                    m0 = 0
                    bi_ = 0
                    while m0 < nout:
                        mn = min(4, nout - m0)
                        yb_ = 6 + (bi_ % 2)
                        bi_ += 1
                        ops_ = [(0, 0, m0), (1, 1, m0), (2, 0, nout + m0), (3, 1, nout + m0)]
                        for k_, (ci_, ri, n0) in enumerate(ops_):
                            MM(ps[yb_].ap[:, 0:mn * 128], CBt.ap[:, ci_, :], Zt.ap[:, ri, n0:n0 + mn, :], start=(k_ == 0), stop=(k_ == 3),
                               r=[CBt.dep, Zt.dep], w=[ps[yb_].dep])
                        yv = ps[yb_].ap[:, 0:mn * 128].rearrange("p (m c) -> p m c", c=128)
                        if o == 0:
                            TT("dve", Z1t.ap[:, :, m0:m0 + mn], yv.rearrange("p m c -> p c m"),
                               X1t.ap[:, m0:m0 + mn, :].rearrange("p m c -> p c m"), ALU.mult,
                               r=[ps[yb_].dep, X1t.dep], w=[Z1t.dep])
                        else:
                            mm_ = 0
                            while mm_ < mn:
                                sl = outs[m0 + mm_]
                                run = 1
                                while mm_ + run < mn and outs[m0 + mm_ + run] == sl + run:
                                    run += 1
                                TT("dve", y2hy.ap[:, m0 + mm_:m0 + mm_ + run, cc * 128:(cc + 1) * 128], yv[:, mm_:mm_ + run, :],
                                   X2t.ap[:, sl:sl + run, :], ALU.mult, r=[ps[yb_].dep, X2t.dep], w=[y2hy.dep])
                                mm_ += run
                        m0 += mn
                    AR.release(om)
                    if o == 0 and cc == 0:
                        d_ = dbg_out("z1_%d" % v, [128, nb * 128])
                        if d_ is not None:
                            dtmp2 = AR.alloc([nb, 128], F32, "dtmp2")
                            CP("dve", dtmp2.ap, Z1t.ap.rearrange("p c b -> p b c"), r=[Z1t.dep], w=[dtmp2.dep])
                            DMA(d_, dtmp2.ap.rearrange("p b c -> p (b c)"), r=[dtmp2.dep])
                AR.release(ccm)
            AR.release(hm)
            d_ = dbg_out("hyo%d" % v, [128, len(out2) * 512])
            if d_ is not None:
                DMA(d_, y2hy.ap.rearrange("p t f -> p (t f)"), r=[y2hy.dep])

            mgm = AR.mark()
            Wout = AR.alloc([KC, D], BF16, "Wout")
            g1bc = AR.alloc([D], F32, "g1bc")
            DMA(g1bc.ap, modD_flat[(v * 48 + 16) * 128:(v * 48 + 24) * 128].partition_broadcast(128), r=[modD_dep], w=[g1bc.dep])
            wout_d = I["w_out"].rearrange("(k p) n -> p k n", p=128)
            for k in range(KC):
                DMA(Wout.ap[:, k, :], wout_d[:, k, :], w=[Wout.dep], q="pool")
            gtmp = [AR.alloc([512], F32, "gtmp%d" % i) for i in range(2)]
            mixb = [AR.alloc([D], BF16, "mix%d" % i) for i in range(2)]
            mixT = [AR.alloc([KC, 128], BF16, "mixT%d" % i) for i in range(2)]
            xr = [AR.alloc([D], F32, "xr%d" % i) for i in range(2)]
            x1t = [AR.alloc([D], F32, "x1t%d" % i) for i in range(2)]
            hyh = AR.alloc([512], F32, "hyh")
            junk3 = AR.alloc([512], F32, "junk3")
            st3 = AR.alloc([8], F32, "st3")
            def merge_a(qi):
                halo = (v == 1 and qi == 8)
                m_ = 2 if halo else 128
                mx = mixb[qi % 2]
                ACTF(junk3.ap[0:m_, :], O_all.ap[0:m_, qi, :], AF.Square, r=[O_all.dep], w=[junk3.dep, st3.dep],
                     accum_out=st3.ap[0:m_, 0:1])
                ACTF(st3.ap[0:m_, 1:2], st3.ap[0:m_, 0:1], AF.Sqrt, r=[st3.dep, epsb.dep], w=[st3.dep], scale=1.0 / 512,
                     bias=epsb.ap[0:m_, 0:1])
                RECIP(st3.ap[0:m_, 2:3], st3.ap[0:m_, 1:2], r=[st3.dep], w=[st3.dep])
                STT(mx.ap[0:m_, 0:512], O_all.ap[0:m_, qi, :], st3.ap[0:m_, 2:3], gatt_bc.ap[0:m_, :], ALU.mult, ALU.mult,
                    r=[O_all.dep, st3.dep, gatt_bc.dep], w=[mx.dep])
                if halo:
                    MM(ps[7].ap[0:2, :], sel.ap[:, 0:2], y2hy.ap[:, 8, :], start=True, stop=False, r=[sel.dep, y2hy.dep], w=[ps[7].dep])
                    MM(ps[7].ap[0:2, :], sel.ap[:, 2:4], y2hy.ap[:, 9, :], start=False, stop=True, r=[sel.dep, y2hy.dep], w=[ps[7].dep])
                    CP("act", hyh.ap[0:2, :], ps[7].ap[0:2, :], r=[ps[7].dep], w=[hyh.dep])
                    hsrc, hdep = hyh.ap[0:2, :], hyh.dep
                else:
                    hsrc, hdep = y2hy.ap[0:m_, qi, :], y2hy.dep
                ACTF(junk3.ap[0:m_, :], hsrc, AF.Square, r=[hdep], w=[junk3.dep, st3.dep], accum_out=st3.ap[0:m_, 3:4])
                ACTF(st3.ap[0:m_, 4:5], st3.ap[0:m_, 3:4], AF.Sqrt, r=[st3.dep, epsb.dep], w=[st3.dep], scale=1.0 / 512,
                     bias=epsb.ap[0:m_, 0:1])
                RECIP(st3.ap[0:m_, 5:6], st3.ap[0:m_, 4:5], r=[st3.dep], w=[st3.dep])
                STT(mx.ap[0:m_, 512:1024], hsrc, st3.ap[0:m_, 5:6], ghy_bc.ap[0:m_, :], ALU.mult, ALU.mult,
                    r=[hdep, st3.dep, ghy_bc.dep], w=[mx.dep])
                xrt = xr[qi % 2]
                if halo:
                    DMA(xrt.ap[0:1, :], xin[4095:4096, :], w=[xrt.dep])
                    DMA(xrt.ap[1:2, :], xin[1024:1025, :], w=[xrt.dep])
                else:
                    DMA(xrt.ap, xin[qi * 128:(qi + 1) * 128, :], w=[xrt.dep])

            def merge_b(qi):
                halo = (v == 1 and qi == 8)
                m_ = 2 if halo else 128
                mx = mixb[qi % 2]
                tb = qi % 2
                mT = mixT[qi % 2]
                for k in range(KC):
                    TR(psb(tb)[:, k * 128:k * 128 + m_], mx.ap[0:m_, k * 128:(k + 1) * 128], ident_b.ap[0:m_, 0:m_],
                       r=[mx.dep, ident_b.dep], w=[ps[tb].dep])
                CP("act", mT.ap[:, :, 0:m_], psb(tb).rearrange("p (k t) -> p k t", t=128)[:, :, 0:m_], r=[ps[tb].dep], w=[mT.dep])
                xrt = xr[qi % 2]
                xo = x1t[qi % 2]
                for half in range(2):
                    ob = 2 + 2 * (qi % 2) + half
                    for k in range(KC):
                        MM(ps[ob].ap[0:m_, :], mT.ap[:, k, 0:m_], Wout.ap[:, k, half * 512:(half + 1) * 512], start=(k == 0),
                           stop=(k == KC - 1), r=[mT.dep, Wout.dep], w=[ps[ob].dep])
                    gt = gtmp[half]
                    TT("dve", gt.ap[0:m_, :], ps[ob].ap[0:m_, :], g1bc.ap[0:m_, half * 512:(half + 1) * 512], ALU.mult,
                       r=[ps[ob].dep, g1bc.dep], w=[gt.dep])
                    TT("pool", xo.ap[0:m_, half * 512:(half + 1) * 512], gt.ap[0:m_, :], xrt.ap[0:m_, half * 512:(half + 1) * 512],
                       ALU.add, r=[gt.dep, xrt.dep], w=[xo.dep])
                r0 = 1024 if halo else qi * 128
                DMA(x1scr[v][r0:r0 + m_, :], xo.ap[0:m_, :], r=[xo.dep], w=[x1scr_dep[v]], q="pool")

            merge_a(0)
            for qi in range(nqt):
                if qi + 1 < nqt:
                    merge_a(qi + 1)
                merge_b(qi)
            AR.release(gm)

        def load_ffn_weights():
            Wup = AR.alloc([KC, 2 * DFF], BF16, "Wup")
            Wdn = AR.alloc([22, D], BF16, "Wdn")
            wup_d = I["w_up"].rearrange("(k p) n -> p k n", p=128)
            for k in range(KC):
                for hh in range(2):
                    DMA(Wup.ap[:, k, hh * DFF:(hh + 1) * DFF], wup_d[:, k, hh * DFF:(hh + 1) * DFF], w=[Wup.dep], q="pool")
            wdn_d = I["w_down"].rearrange("(i p) n -> p i n", p=128)
            for i in range(0, 22, 2):
                DMA(Wdn.ap[:, i:i + 2, :], wdn_d[:, i:i + 2, :], w=[Wdn.dep], q="pool")
            return Wup, Wdn

        def run_ffn(v, Wup, Wdn):
            nqt = 8 if v == 0 else 9
            gm = AR.mark()
            ncolT = 1024 if v == 0 else 1026
            h2Tt = AR.alloc([KC, ncolT], BF16, "h2Tt")
            g2bc = AR.alloc([D], F32, "g2bc")
            DMA(g2bc.ap, modD_flat[(v * 48 + 40) * 128:(v * 48 + 48) * 128].partition_broadcast(128), r=[modD_dep], w=[g2bc.dep])
            n2m = AR.mark()
            xts2 = [AR.alloc([D], F32, "x2t%d" % i) for i in range(3)]
            xns2 = [AR.alloc([D], BF16, "x2n%d" % i) for i in range(3)]
            junk4 = AR.alloc([D], F32, "junk4")
            st4 = AR.alloc([4], F32, "st4")

            def n2_pre(qi):
                halo = (v == 1 and qi == 8)
                m_ = 2 if halo else 128
                r0 = 1024 if halo else qi * 128
                xt = xts2[qi % 3]
                xn = xns2[qi % 3]
                DMA(xt.ap[0:m_, :], x1scr[v][r0:r0 + m_, :], r=[x1scr_dep[v]], w=[xt.dep])
                ACTF(junk4.ap[0:m_, :], xt.ap[0:m_, :], AF.Square, r=[xt.dep], w=[junk4.dep, st4.dep], accum_out=st4.ap[0:m_, 0:1])
                ACTF(st4.ap[0:m_, 1:2], st4.ap[0:m_, 0:1], AF.Sqrt, r=[st4.dep, epsb.dep], w=[st4.dep], scale=1.0 / D,
                     bias=epsb.ap[0:m_, 0:1])
                RECIP(st4.ap[0:m_, 2:3], st4.ap[0:m_, 1:2], r=[st4.dep], w=[st4.dep])
                TS("dve", xn.ap[0:m_, :], xt.ap[0:m_, :], st4.ap[0:m_, 2:3], None, ALU.mult, r=[xt.dep, st4.dep], w=[xn.dep])

            def n2_mid(qi):
                halo = (v == 1 and qi == 8)
                m_ = 2 if halo else 128
                xn = xns2[qi % 3]
                tb = qi % 2
                for k in range(KC):
                    TR(psb(tb)[:, k * 128:k * 128 + m_], xn.ap[0:m_, k * 128:(k + 1) * 128], ident_b.ap[0:m_, 0:m_],
                       r=[xn.dep, ident_b.dep], w=[ps[tb].dep])
                pv3 = psb(tb).rearrange("p (k t) -> p k t", t=128)
                for k in range(KC):
                    if halo:
                        pairs = [(h2Tt.ap[:, k, 0:1], pv3[:, k, 0:1]), (h2Tt.ap[:, k, 1025:1026], pv3[:, k, 1:2])]
                    elif v == 0:
                        pairs = [(h2Tt.ap[:, k, qi * 128:(qi + 1) * 128], pv3[:, k, :])]
                    else:
                        pairs = [(h2Tt.ap[:, k, 1 + qi * 128:1 + (qi + 1) * 128], pv3[:, k, :])]
                    for (o_, i_) in pairs:
                        if qi % 2 == 0:
                            TS("dve", o_, i_, A2.ap[:, k, v:v + 1], mod.ap[:, v, 24 + k:25 + k], ALU.mult, ALU.add,
                               r=[ps[tb].dep, A2.dep, mod.dep], w=[h2Tt.dep])
                        else:
                            ACTF(o_, i_, AF.Identity, r=[ps[tb].dep, A2.dep, mod.dep], w=[h2Tt.dep], scale=A2.ap[:, k, v:v + 1],
                                 bias=mod.ap[:, v, 24 + k:25 + k])

            n2_pre(0)
            n2_pre(1)
            for qi in range(nqt):
                if qi + 2 < nqt:
                    n2_pre(qi + 2)
                n2_mid(qi)
            AR.release(n2m)
            actT = AR.alloc([22, 256], BF16, "actT")
            wmk = AR.alloc([2], F32, "wmk")
            cab = [AR.alloc([256], F32, "cab%d" % i) for i in range(4)]
            sab = [AR.alloc([256], F32, "sab%d" % i) for i in range(2)]
            x1w = [AR.alloc([D], F32, "x1w%d" % i) for i in range(2)]
            yw = [AR.alloc([D], F32, "yw%d" % i) for i in range(2)]
            gtmp2 = [AR.alloc([512], F32, "gtmp2_%d" % i) for i in range(2)]
            yout = O["y_prompt"] if v == 0 else O["y_sample"]
            cnt_ = 0
            for s_ in range(4):
                c0 = 256 * s_
                ncol = 256 if v == 0 else 258
                for i in range(22):
                    cas = []
                    for part in range(2):
                        ch = i + 22 * part
                        ub = cnt_ % 4
                        cnt_ += 1
                        for k in range(KC):
                            MM(ps[ub].ap[:, 0:ncol], Wup.ap[:, k, ch * 128:(ch + 1) * 128], h2Tt.ap[:, k, c0:c0 + ncol], start=(k == 0),
                               stop=(k == KC - 1), r=[Wup.dep, h2Tt.dep], w=[ps[ub].dep])
                        ca = cab[ub]
                        pu = ps[ub]
                        w0c, w1c, w2c = vcol("ffn_conv_w", ch), vcol("ffn_conv_w", NFF + ch), vcol("ffn_conv_w", 2 * NFF + ch)
                        bc_ = vcol("ffn_conv_b", ch)
                        if v == 0:
                            ACTF(ca.ap, pu.ap[:, 0:256], AF.Identity, r=[pu.dep, vec.dep], w=[ca.dep], scale=w1c, bias=bc_)
                            STT(ca.ap[:, 1:256], pu.ap[:, 0:255], w0c, ca.ap[:, 1:256], ALU.mult, ALU.add, r=[pu.dep, ca.dep, vec.dep], w=[ca.dep])
                            STT(ca.ap[:, 0:255], pu.ap[:, 1:256], w2c, ca.ap[:, 0:255], ALU.mult, ALU.add, r=[pu.dep, ca.dep, vec.dep], w=[ca.dep])
                        else:
                            ACTF(ca.ap, pu.ap[:, 1:257], AF.Identity, r=[pu.dep, vec.dep], w=[ca.dep], scale=w1c, bias=bc_)
                            lo_ = 1 if s_ == 0 else 0
                            hi_ = 255 if s_ == 3 else 256
                            STT(ca.ap[:, lo_:256], pu.ap[:, lo_:256], w0c, ca.ap[:, lo_:256], ALU.mult, ALU.add,
                                r=[pu.dep, ca.dep, vec.dep], w=[ca.dep])
                            STT(ca.ap[:, 0:hi_], pu.ap[:, 2:2 + hi_], w2c, ca.ap[:, 0:hi_], ALU.mult, ALU.add,
                                r=[pu.dep, ca.dep, vec.dep], w=[ca.dep])
                            if s_ == 0:
                                TS("dve", wmk.ap[:, 0:1], halomask.ap[:, 0:1], w0c, None, ALU.mult, r=[halomask.dep, vec.dep], w=[wmk.dep])
                                STT(ca.ap[:, 0:1], pu.ap[:, 0:1], wmk.ap[:, 0:1], ca.ap[:, 0:1], ALU.mult, ALU.add,
                                    r=[pu.dep, ca.dep, wmk.dep], w=[ca.dep])
                            if s_ == 3:
                                TS("dve", wmk.ap[:, 1:2], halomask.ap[:, 1:2], w2c, None, ALU.mult, r=[halomask.dep, vec.dep], w=[wmk.dep])
                                STT(ca.ap[:, 255:256], pu.ap[:, 257:258], wmk.ap[:, 1:2], ca.ap[:, 255:256], ALU.mult, ALU.add,
                                    r=[pu.dep, ca.dep, wmk.dep], w=[ca.dep])
                        cas.append(ca)
                    sa = sab[i % 2]
                    ACTF(sa.ap, cas[0].ap, AF.Silu, r=[cas[0].dep], w=[sa.dep])
                    TT("dve", actT.ap[:, i, :], sa.ap, cas[1].ap, ALU.mult, r=[sa.dep, cas[1].dep], w=[actT.dep])
                for tl in range(2):
                    ti = s_ * 2 + tl
                    xw = x1w[ti % 2]
                    yo = yw[ti % 2]
                    DMA(xw.ap, x1scr[v][ti * 128:(ti + 1) * 128, :], r=[x1scr_dep[v]], w=[xw.dep])
                    for half in range(2):
                        db = 4 + tl * 2 + half
                        for i in range(22):
                            MM(ps[db].ap, actT.ap[:, i, tl * 128:(tl + 1) * 128], Wdn.ap[:, i, half * 512:(half + 1) * 512], start=(i == 0),
                               stop=(i == 21), r=[actT.dep, Wdn.dep], w=[ps[db].dep])
                        gt2 = gtmp2[half]
                        TT("dve", gt2.ap, ps[db].ap, g2bc.ap[:, half * 512:(half + 1) * 512], ALU.mult,
                           r=[ps[db].dep, g2bc.dep], w=[gt2.dep])
                        TT("pool", yo.ap[:, half * 512:(half + 1) * 512], gt2.ap, xw.ap[:, half * 512:(half + 1) * 512], ALU.add,
                           r=[gt2.dep, xw.dep], w=[yo.dep])
                    DMA(yout[ti * 128:(ti + 1) * 128, :], yo.ap, r=[yo.dep], q="pool")
            AR.release(gm)

        for v_ in GROUPS:
            run_group(v_)
        Wup_, Wdn_ = load_ffn_weights()
        for v_ in GROUPS:
            run_ffn(v_, Wup_, Wdn_)
        P.emit()
    return nc, list(DBG.keys())


_CACHE = {}


def make_in_maps(inputs):
    f32 = lambda a: np.ascontiguousarray(np.asarray(a, dtype=np.float32))
    xpr = f32(inputs["x_prompt"])
    xsm = f32(inputs["x_sample"])
    maps = []
    for core in range(8):
        b, j = core // 4, core % 4
        m = {}
        m["xp"] = xpr[4 * core:4 * core + 4].reshape(TP, D)
        m["xs"] = np.ascontiguousarray(np.roll(xsm[b], -1024 * j, axis=0))
        m["ckv_c"] = f32(inputs["cache_ckv"])[b, 0]
        m["kpe_c"] = f32(inputs["cache_kpe"])[b, 0]
        m["cvec"] = np.ascontiguousarray(np.stack([f32(inputs["c_ctx"]), f32(inputs["c"])[b]], axis=1))
        for nm in ["w_ada", "b_ada", "g_mix", "w_in", "g_qa", "w_uq", "g_kva", "w_ukv", "g_q", "g_k", "hy_conv_w",
                   "hy_conv_b", "filt_w1", "filt_b1", "filt_freq1", "filt_w2", "filt_b2", "filt_freq2", "filt_w3",
                   "filt_bias", "g_out_att", "g_out_hy", "w_out", "g_ffn", "w_up", "ffn_conv_w", "ffn_conv_b", "w_down"]:
            a = f32(inputs[nm])[0]
            if nm in ("hy_conv_w", "ffn_conv_w"):
                a = a.reshape(-1)
            m[nm] = np.ascontiguousarray(a)
        m.update(host_tables(core))
        maps.append(m)
    return maps


def kernel(**inputs):
    if "nc" not in _CACHE:
        _CACHE["nc"] = build_program()[0]
    nc = _CACHE["nc"]
    maps = make_in_maps(inputs)
    res = run_bass_kernel_spmd(nc, maps, core_ids=list(range(8)))
    R = res.results
    y_prompt = np.concatenate([R[c]["y_prompt"].reshape(4, 256, D) for c in range(8)], axis=0)
    y_sample = np.stack([np.concatenate([R[b * 4 + j]["y_sample"] for j in range(4)], axis=0) for b in range(2)], axis=0)
    new_ckv = np.concatenate([R[c]["new_ckv"].reshape(4, 1, 256, KVL) for c in range(8)], axis=0)
    new_kpe = np.concatenate([R[c]["new_kpe"].reshape(4, 1, 256, ROPE) for c in range(8)], axis=0)
    return (y_prompt.astype(np.float32), y_sample.astype(np.float32), new_ckv.astype(np.float32), new_kpe.astype(np.float32))
```
